# Optimizing a Trainium2 kernel written in Bass

```python
import math, functools
import jax, jax.numpy as jnp
from jax import lax
import numpy as np

D_MODEL = 2048
BATCH = 4
SEQ = 2048
DEPTH = 1
DEC_BATCH = 128
DEC_SEQ = 1
PAST_LEN = 8192
PAGE_SIZE = 128

D_POOL = D_MODEL // 2
POOL_WINDOWS = (2, 4, 8, 16)
N_POOL_GROUPS = len(POOL_WINDOWS)
POOL_GROUP = D_POOL // N_POOL_GROUPS
POOL_STATE = max(POOL_WINDOWS) - 1
QK_NOPE = 128
QK_ROPE = 64
V_HEAD = 128
N_HEADS = (D_MODEL // 2) // V_HEAD
Q_LORA = 512
KV_LORA = 512
D_ATTN = N_HEADS * V_HEAD
D_MIX = D_POOL + D_ATTN
D_IN = D_POOL + Q_LORA + KV_LORA + QK_ROPE
D_FF = 5632
ROPE_THETA = 10000.0
Q_BLOCK = 128
EPS = 1e-6
SM_SCALE = (QK_NOPE + QK_ROPE) ** -0.5
POOL_SPARE_NUM = 5
POOL_SPARE_DEN = 4

kernel_name = 'hybrid_pool_mla_macaron_step'


def rms_norm(x, g):
    xf = x.astype(jnp.float32)
    y = xf * lax.rsqrt(jnp.mean(xf * xf, axis=-1, keepdims=True) + EPS)
    return (y * g.astype(jnp.float32)).astype(x.dtype)


def swiglu(x, w_gate, w_up, w_down):
    return (jax.nn.silu(x @ w_gate) * (x @ w_up)) @ w_down


def rope(x, pos):
    half = QK_ROPE // 2
    inv = ROPE_THETA ** (-jnp.arange(half, dtype=jnp.float32) / half)
    ang = pos.astype(jnp.float32)[:, None] * inv[None, :]
    cos = jnp.cos(ang)[None, :, None, :]
    sin = jnp.sin(ang)[None, :, None, :]
    xf = x.astype(jnp.float32)
    x1, x2 = xf[..., :half], xf[..., half:]
    return jnp.concatenate([x1 * cos - x2 * sin, x2 * cos + x1 * sin], axis=-1).astype(x.dtype)


def multiscale_pool(u_hist, u_new, pos_new, w_pool, pool_scale):
    B, T, _ = u_new.shape
    u_ext = jnp.concatenate([u_hist, u_new], axis=1).astype(jnp.float32)
    cs = jnp.cumsum(u_ext, axis=1)
    cs = jnp.concatenate([jnp.zeros_like(cs[:, :1]), cs], axis=1)
    end = cs[:, POOL_STATE + 1:]
    means = []
    for g, w in enumerate(POOL_WINDOWS):
        sl = slice(g * POOL_GROUP, (g + 1) * POOL_GROUP)
        start = cs[:, POOL_STATE + 1 - w:POOL_STATE + 1 - w + T, sl]
        cnt = jnp.minimum(pos_new + 1, w).astype(jnp.float32)[None, :, None]
        means.append((end[..., sl] - start) / cnt)
    mean = jnp.concatenate(means, axis=-1)
    d = (mean - u_ext[:, POOL_STATE:]).reshape(B, T, N_POOL_GROUPS, POOL_GROUP)
    out = jnp.einsum('btgc,gcd->btgd', d.astype(w_pool.dtype), w_pool).reshape(B, T, D_POOL)
    return (out * pool_scale).astype(u_new.dtype)


def attend_latent(q_lat, q_pe, c_kv, k_pe, mask):
    s = (jnp.einsum('bqhc,bsc->bhqs', q_lat, c_kv).astype(jnp.float32)
         + jnp.einsum('bqhr,bsr->bhqs', q_pe, k_pe).astype(jnp.float32)) * SM_SCALE
    s = jnp.where(mask[None, None], s, -jnp.inf)
    p = jax.nn.softmax(s, axis=-1).astype(c_kv.dtype)
    return jnp.einsum('bhqs,bsc->bqhc', p, c_kv)


def prompt_attend(q_lat, q_pe, c_kv, k_pe):
    B, S = q_lat.shape[:2]
    nb = S // Q_BLOCK
    key_pos = jnp.arange(S)

    def block(args):
        ql, qp, i = args
        qpos = i * Q_BLOCK + jnp.arange(Q_BLOCK)
        mask = key_pos[None, :] <= qpos[:, None]
        return attend_latent(ql, qp, c_kv, k_pe, mask)

    qlb = q_lat.reshape(B, nb, Q_BLOCK, N_HEADS, KV_LORA).swapaxes(0, 1)
    qpb = q_pe.reshape(B, nb, Q_BLOCK, N_HEADS, QK_ROPE).swapaxes(0, 1)
    o = lax.map(block, (qlb, qpb, jnp.arange(nb)))
    return o.swapaxes(0, 1).reshape(B, S, N_HEADS, KV_LORA)


def sample_attend(q_lat, q_pe, c_kv, k_pe, ckv_past, kpe_past):
    T = q_lat.shape[1]
    P = ckv_past.shape[1]
    c_all = jnp.concatenate([ckv_past.astype(c_kv.dtype), c_kv], axis=1)
    k_all = jnp.concatenate([kpe_past.astype(k_pe.dtype), k_pe], axis=1)
    mask = jnp.arange(P + T)[None, :] <= (P + jnp.arange(T))[:, None]
    return attend_latent(q_lat, q_pe, c_all, k_all, mask)


def decoder_layer(x, pos, pool_hist, attend, lp):
    (g_ffn1_pre, w1_gate, w1_up, w1_down, g_ffn1_post,
     g_mix_pre, w_in, w_pool, pool_scale, g_q, w_uq, g_kv, w_uk, w_uv, w_out, g_mix_post,
     g_ffn2_pre, w2_gate, w2_up, w2_down, g_ffn2_post) = lp
    B, T, _ = x.shape
    h = x
    h = h + 0.5 * rms_norm(swiglu(rms_norm(h, g_ffn1_pre), w1_gate, w1_up, w1_down), g_ffn1_post)
    z = rms_norm(h, g_mix_pre) @ w_in
    u = z[..., :D_POOL]
    c_q = z[..., D_POOL:D_POOL + Q_LORA]
    c_kv_raw = z[..., D_POOL + Q_LORA:D_POOL + Q_LORA + KV_LORA]
    k_pe_raw = z[..., D_POOL + Q_LORA + KV_LORA:]
    pool_out = multiscale_pool(pool_hist, u, pos, w_pool, pool_scale)
    q = (rms_norm(c_q, g_q) @ w_uq).reshape(B, T, N_HEADS, QK_NOPE + QK_ROPE)
    q_nope, q_pe = q[..., :QK_NOPE], rope(q[..., QK_NOPE:], pos)
    c_kv = rms_norm(c_kv_raw, g_kv)
    k_pe = rope(k_pe_raw[:, :, None, :], pos)[:, :, 0]
    q_lat = jnp.einsum('bqhd,chd->bqhc', q_nope, w_uk)
    o_lat = attend(q_lat, q_pe, c_kv, k_pe)
    attn_out = jnp.einsum('bqhc,chv->bqhv', o_lat, w_uv).reshape(B, T, D_ATTN)
    mix = jnp.concatenate([pool_out, attn_out], axis=-1) @ w_out
    h = h + rms_norm(mix, g_mix_post)
    h = h + 0.5 * rms_norm(swiglu(rms_norm(h, g_ffn2_pre), w2_gate, w2_up, w2_down), g_ffn2_post)
    new_pool = jnp.concatenate([pool_hist.astype(u.dtype), u], axis=1)[:, -POOL_STATE:]
    return h, c_kv, k_pe, new_pool


def setup_inputs(seed: int = 0) -> dict:
    key = jax.random.key(seed)
    ks = iter(jax.random.split(key, 40))
    f32 = jnp.float32

    def nrm(shape, fan_in):
        return jax.random.normal(next(ks), shape, f32) * (fan_in ** -0.5)

    def gain(n):
        return 1.0 + 0.02 * jax.random.normal(next(ks), (DEPTH, n), f32)

    n_pages = PAST_LEN // PAGE_SIZE
    n_used = DEC_BATCH * n_pages
    n_phys = n_used * POOL_SPARE_NUM // POOL_SPARE_DEN
    perm = jax.random.permutation(next(ks), n_phys)
    page_table = perm[:n_used].reshape(DEC_BATCH, n_pages).astype(jnp.int32)
    return {
        'x_prompt': jax.random.normal(next(ks), (BATCH, SEQ, D_MODEL), f32),
        'x_sample': jax.random.normal(next(ks), (DEC_BATCH, DEC_SEQ, D_MODEL), f32),
        'cache_kv_latent': jax.random.normal(next(ks), (DEPTH, n_phys, PAGE_SIZE, KV_LORA), f32),
        'cache_k_rope': jax.random.normal(next(ks), (DEPTH, n_phys, PAGE_SIZE, QK_ROPE), f32),
        'state_pool': jax.random.normal(next(ks), (DEPTH, DEC_BATCH, POOL_STATE, D_POOL), f32),
        'page_table': page_table,
        'g_ffn1_pre': gain(D_MODEL),
        'w1_gate': nrm((DEPTH, D_MODEL, D_FF), D_MODEL),
        'w1_up': nrm((DEPTH, D_MODEL, D_FF), D_MODEL),
        'w1_down': nrm((DEPTH, D_FF, D_MODEL), D_FF),
        'g_ffn1_post': gain(D_MODEL),
        'g_mix_pre': gain(D_MODEL),
        'w_in': nrm((DEPTH, D_MODEL, D_IN), D_MODEL),
        'w_pool': nrm((DEPTH, N_POOL_GROUPS, POOL_GROUP, POOL_GROUP), POOL_GROUP),
        'pool_scale': gain(D_POOL),
        'g_q': gain(Q_LORA),
        'w_uq': nrm((DEPTH, Q_LORA, N_HEADS * (QK_NOPE + QK_ROPE)), Q_LORA),
        'g_kv': gain(KV_LORA),
        'w_uk': nrm((DEPTH, KV_LORA, N_HEADS, QK_NOPE), KV_LORA),
        'w_uv': nrm((DEPTH, KV_LORA, N_HEADS, V_HEAD), KV_LORA),
        'w_out': nrm((DEPTH, D_MIX, D_MODEL), D_MIX),
        'g_mix_post': gain(D_MODEL),
        'g_ffn2_pre': gain(D_MODEL),
        'w2_gate': nrm((DEPTH, D_MODEL, D_FF), D_MODEL),
        'w2_up': nrm((DEPTH, D_MODEL, D_FF), D_MODEL),
        'w2_down': nrm((DEPTH, D_FF, D_MODEL), D_FF),
        'g_ffn2_post': gain(D_MODEL),
    }


def reference(x_prompt, x_sample, cache_kv_latent, cache_k_rope, state_pool, page_table,
              g_ffn1_pre, w1_gate, w1_up, w1_down, g_ffn1_post,
              g_mix_pre, w_in, w_pool, pool_scale, g_q, w_uq, g_kv, w_uk, w_uv, w_out, g_mix_post,
              g_ffn2_pre, w2_gate, w2_up, w2_down, g_ffn2_post):
    n_pages = page_table.shape[1]
    past_len = n_pages * PAGE_SIZE
    bp, sp = x_prompt.shape[0], x_prompt.shape[1]
    bs, ts = x_sample.shape[0], x_sample.shape[1]
    pos_p = jnp.arange(sp, dtype=jnp.int32)
    pos_s = past_len + jnp.arange(ts, dtype=jnp.int32)
    hp, hs = x_prompt, x_sample
    p_kv, p_pe, p_pool, s_kv, s_pe, s_pool = [], [], [], [], [], []
    for l in range(DEPTH):
        lp = (g_ffn1_pre[l], w1_gate[l], w1_up[l], w1_down[l], g_ffn1_post[l],
              g_mix_pre[l], w_in[l], w_pool[l], pool_scale[l], g_q[l], w_uq[l], g_kv[l],
              w_uk[l], w_uv[l], w_out[l], g_mix_post[l],
              g_ffn2_pre[l], w2_gate[l], w2_up[l], w2_down[l], g_ffn2_post[l])
        hist0 = jnp.zeros((bp, POOL_STATE, D_POOL), hp.dtype)
        hp, ckv, kpe, pool = decoder_layer(hp, pos_p, hist0, prompt_attend, lp)
        p_kv.append(ckv)
        p_pe.append(kpe)
        p_pool.append(pool)
        ckv_past = cache_kv_latent[l][page_table].reshape(bs, past_len, KV_LORA)
        kpe_past = cache_k_rope[l][page_table].reshape(bs, past_len, QK_ROPE)
        attend = functools.partial(sample_attend, ckv_past=ckv_past, kpe_past=kpe_past)
        hs, ckv_s, kpe_s, pool_s = decoder_layer(hs, pos_s, state_pool[l], attend, lp)
        s_kv.append(ckv_s)
        s_pe.append(kpe_s)
        s_pool.append(pool_s)
    return (hp, hs, jnp.stack(p_kv), jnp.stack(p_pe), jnp.stack(p_pool),
            jnp.stack(s_kv), jnp.stack(s_pe), jnp.stack(s_pool))
```

```python
import numpy as np
import concourse.bass as bass
import concourse.mybir as mybir
from concourse.bass_utils import run_bass_kernel_spmd
from contextlib import ExitStack
import os

F32, BF16, I32 = mybir.dt.float32, mybir.dt.bfloat16, mybir.dt.int32
AF = mybir.ActivationFunctionType
ALU = mybir.AluOpType
AX = mybir.AxisListType

D = 2048; DC = 16; FF = 5632; FC = 44; DP = 1024; QL = 512; KV = 512; R = 64
NH = 8; SEQ = 2048; NS = 16; NPG = 64; PAGE = 128
NPHYS = 10240
NLOC = 2048 + NS
EPS = 1e-6
SM_SCALE = float((128 + 64) ** -0.5)
ENGS = ['tensor', 'scalar', 'vector', 'gpsimd', 'sync']


class Buf:
    def __init__(s, name):
        s.name = name; s.w = None; s.r = {}; s.rd = []


class Op:
    __slots__ = ('eng', 'fn', 'deps', 'signal', 'ev', 'dsem', 'n')

    def __init__(s, eng, fn, dsem):
        s.eng = eng; s.fn = fn; s.deps = []; s.signal = False; s.ev = None; s.dsem = dsem


class Ker:
    def __init__(s, nc, es):
        s.nc = nc; s.es = es; s.ops = []; s.nall = 0
        s.esem = {e: es.enter_context(nc.semaphore('e_' + e)) for e in ENGS}
        s.ecnt = {e: 0 for e in ENGS}
        s.dsem = {}; s.dcnt = {}
        s.waited = {e: {} for e in ENGS}
        s.fence = []; s.fenced = set(ENGS)
        s.last = {}; s.phase0 = 0

    def _sem(s, key):
        if key not in s.dsem:
            s.dsem[key] = s.es.enter_context(s.nc.semaphore('d_' + key)); s.dcnt[key] = 0
        return s.dsem[key]

    def op(s, eng, fn, reads=(), writes=(), dsem=None):
        o = Op(eng, fn, dsem); o.n = s.nall; s.nall += 1
        deps = {}
        def add(d):
            if d is None or d is o: return
            if eng == 'tensor' and d.eng == 'tensor' and d.dsem is None: return
            deps[id(d)] = d
        for b in reads:
            add(b.w)
        for b in writes:
            add(b.w)
            for d in b.r.values(): add(d)
            for d in b.rd: add(d)
        if eng not in s.fenced:
            for d in s.fence: add(d)
            s.fenced.add(eng)
        o.deps = list(deps.values())
        for b in writes:
            b.w = o; b.r = {}; b.rd = []
        for b in reads:
            if b in writes: continue
            if dsem is not None: b.rd.append(o)
            else: b.r[eng] = o
        s.ops.append(o); s.last[eng] = o
        if dsem is not None: s.last['dma_' + dsem] = o
        return o

    def barrier(s):
        s.fence = list(s.last.values()); s.fenced = set()
        for o in s.fence: o.signal = True

    def emit(s):
        for o in s.ops:
            for d in o.deps: d.signal = True
        for o in s.ops:
            if o.dsem is not None:
                s._sem(o.dsem); s.dcnt[o.dsem] += 16; o.ev = (s.dsem[o.dsem], s.dcnt[o.dsem])
            elif o.signal:
                s.ecnt[o.eng] += 1; o.ev = (s.esem[o.eng], s.ecnt[o.eng])
        with s.nc.Block() as blk:
            for eng in ENGS:
                ops_e = [o for o in s.ops if o.eng == eng]
                if not ops_e: continue
                def body(e, ops_e=ops_e, eng=eng):
                    wd = s.waited[eng]
                    for o in ops_e:
                        for d in o.deps:
                            if d.ev is None:
                                assert d.n < s.phase0, (d.eng, d.n)
                                continue
                            sem, val = d.ev
                            k = id(sem)
                            if wd.get(k, 0) < val:
                                e.wait_ge(sem, val); wd[k] = val
                        if o.fn is None: continue
                        ins = o.fn(e)
                        if o.dsem is not None: ins.then_inc(o.ev[0], 16)
                        elif o.signal: ins.then_inc(o.ev[0], 1)
                getattr(blk, eng)(body)
        s.ops = []; s.phase0 = s.nall

    def dma(s, q, out, in_, sem, reads=(), writes=()):
        return s.op(q, lambda e: e.dma_start(out=out, in_=in_), reads, writes, dsem=sem)

    def mm(s, out, lhsT, rhs, start, stop, reads, writes):
        return s.op('tensor', lambda e: e.matmul(out, lhsT, rhs, start=start, stop=stop), reads, writes)

    def tr(s, out, in_, ident, reads, writes):
        return s.op('tensor', lambda e: e.transpose(out, in_, ident), reads, writes)

    def act(s, out, in_, func, reads, writes, scale=None, bias=None, accum=None, eng='scalar'):
        kw = {}
        if scale is not None: kw['scale'] = scale
        if bias is not None: kw['bias'] = bias
        if accum is not None: kw['accum_out'] = accum
        return s.op('scalar', lambda e: e.activation(out, in_, func, **kw), reads, writes)

    def cp(s, eng, out, in_, reads, writes):
        if eng == 'scalar':
            return s.op(eng, lambda e: e.copy(out, in_), reads, writes)
        return s.op(eng, lambda e: e.tensor_copy(out, in_), reads, writes)

    def tt(s, eng, out, a, b, op, reads, writes):
        return s.op(eng, lambda e: e.tensor_tensor(out, a, b, op), reads, writes)

    def ts(s, eng, out, a, s1, op0, reads, writes, s2=None, op1=None):
        if op1 is None:
            return s.op(eng, lambda e: e.tensor_scalar(out, a, s1, None, op0), reads, writes)
        return s.op(eng, lambda e: e.tensor_scalar(out, a, s1, s2, op0, op1), reads, writes)

    def stt(s, out, a, sc, b, op0, op1, reads, writes):
        return s.op('vector', lambda e: e.scalar_tensor_tensor(out, a, sc, b, op0, op1), reads, writes)

    def recip(s, out, in_, reads, writes):
        return s.op('vector', lambda e: e.reciprocal(out, in_), reads, writes)


def eval_tiles(t):
    return [tuple(int(v) for v in x.split(':')) for x in t.split(',')]


def build_program(stop=99):
    nc = bass.Bass("TRN2", target_bir_lowering=False)
    es = ExitStack()
    def din(name, shape, dt=F32):
        return nc.dram_tensor(name, list(shape), dt, kind="ExternalInput").ap()
    def dout(name, shape, dt=F32):
        return nc.dram_tensor(name, list(shape), dt, kind="ExternalOutput").ap()
    def dscr(name, shape, dt):
        return nc.dram_tensor(name, list(shape), dt, kind="Internal").ap()

    x_loc = din('x_loc', [NLOC, D])
    cache_kv = din('cache_kv', [NPHYS * PAGE, KV])
    cache_kr = din('cache_kr', [NPHYS * PAGE, R])
    state_pool = din('state_pool', [NS, 15, DP])
    ptab = din('ptab', [128, NS * NPG // 4], I32)
    w_f32 = {}
    for nm in ('w1_gate', 'w1_up', 'w2_gate', 'w2_up'):
        w_f32[nm] = din(nm, [D, FF])
    for nm in ('w1_down', 'w2_down'):
        w_f32[nm] = din(nm, [FF, D])
    w_in = din('w_in', [D, 2112]); w_out = din('w_out', [D, D])
    w_pool = din('w_pool', [4 * 256, 256])
    wq_nope = din('wq_nope', [QL, 1024]); wq_rope = din('wq_rope', [QL, 512]); wq_rot = din('wq_rot', [QL, 512])
    wukT = din('wukT', [128, NH * KV]); w_uv = din('w_uv', [KV, NH * 128])
    gvec = din('gvec', [128, 6 * DC + 8 + 4])
    gkv_b = din('gkv_b', [128, KV])
    cosk = din('cosk', [128, 17, R]); sink = din('sink', [128, 17, R])
    cosq = din('cosq', [128, 1024 + NS]); sinq = din('sinq', [128, 1024 + NS])
    invc = din('invc', [128, 3, 4, 512])
    consts = din('consts', [128, 4])
    tri_in = din('tri', [128, 128])
    ident_in = din('ident', [128, 128])

    y_out = dout('y_out', [1024 + NS, D])
    kv_out = dout('kv_out', [NLOC, KV]); kpe_out = dout('kpe_out', [NLOC, R])
    pool_last = dout('pool_last', [16, DP]); spool_new = dout('spool_new', [16, DP])
    spool_hist = dout('spool_hist', [NS, 14, DP])

    wsc = {}
    for nm in ('w1_gate', 'w1_up', 'w2_gate', 'w2_up'):
        wsc[nm] = dscr('s_' + nm, [FC, 128, D], BF16)
    for nm in ('w1_down', 'w2_down'):
        wsc[nm] = dscr('s_' + nm, [DC, 128, FF], BF16)
    s_winfm = dscr('s_winfm', [12, 128, D], BF16)
    s_winkv = dscr('s_winkv', [128, DC * 576], BF16)
    s_wo = dscr('s_wo', [DC, 128, D], BF16)
    s_small = {k: dscr('s_' + k, [128, n], BF16) for k, n in
               (('wqn', 4096), ('wqr', 2048), ('wqx', 2048), ('wuk', 4096), ('wuv', 4096), ('wpool', 2048))}
    xT = dscr('xT', [D, NLOC], F32); h1T = dscr('h1T', [D, NLOC], F32)
    h2T = dscr('h2T', [D, 1024 + NS], F32); h3T = dscr('h3T', [D, 1024 + NS], F32)
    cqn_s = dscr('cqn_s', [QL, 1024 + NS], F32)
    mixT = dscr('mixT', [D, 1024 + NS], BF16)
    KT_s = dscr('KT_s', [640, NLOC], BF16)
    V_s = dscr('V_s', [NLOC, KV], BF16)

    K = Ker(nc, es)
    _cnt = [0]
    def sb(name, shape, dt):
        _cnt[0] += 1
        return es2.enter_context(nc.sbuf_tensor('%s_%d' % (name, _cnt[0]), list(shape), dt))
    PS = [es.enter_context(nc.psum_tensor('ps%d' % i, [128, 512], F32)) for i in range(8)]
    bPS = [Buf('ps%d' % i) for i in range(8)]
    identf = es.enter_context(nc.sbuf_tensor('identf', [128, 128], F32)); b_identf = Buf('identf')
    ones_bf = es.enter_context(nc.sbuf_tensor('ones_bf', [128, 128], BF16)); b_ones = Buf('ones')
    ones_f = es.enter_context(nc.sbuf_tensor('ones_f', [128, 128], F32))
    gv = es.enter_context(nc.sbuf_tensor('gv', [128, 6 * DC + 12], F32)); b_gv = Buf('gv')
    cst = es.enter_context(nc.sbuf_tensor('cst', [128, 4], F32)); b_cst = Buf('cst')
    K.dma('sync', identf[:, :], ident_in[:, :], 'c_id', writes=[b_identf])
    K.dma('sync', gv[:, :], gvec[:, :], 'c_gv', writes=[b_gv])
    K.dma('sync', cst[:, :], consts[:, :], 'c_cst', writes=[b_cst])
    K.op('vector', lambda e: e.memset(ones_bf[:, :], 1.0), writes=[b_ones])
    K.op('vector', lambda e: e.memset(ones_f[:, :], 1.0), writes=[b_ones])
    epsb = cst[:, 1:2]
    G_F1PRE, G_F1POST, G_MIXPRE, G_MIXPOST, G_F2PRE, G_F2POST = [i * DC for i in range(6)]
    G_PSC = 6 * DC; G_Q = 6 * DC + 8

    with ExitStack() as es2:
        NSL = 6
        S = [sb('castS%d' % i, [128, 4096], F32) for i in range(NSL)]
        T = [sb('castT%d' % i, [128, 4096], BF16) for i in range(NSL)]
        bS = [Buf('cS%d' % i) for i in range(NSL)]; bT = [Buf('cT%d' % i) for i in range(NSL)]
        jobs = []
        def job_perm(src, col0, kc_n, row0, dst3, ncols=256, lst=None):
            jn = ncols // 128
            sv = src[row0:row0 + kc_n * 128, col0:col0 + ncols].rearrange("(k p) c -> p k c", p=128)
            (jobs if lst is None else lst).append((sv, kc_n * ncols,
                         lambda Si, n=kc_n, jn=jn: Si[:, :n * jn * 128].rearrange("p (k j f) -> p j k f", k=n, j=jn),
                         lambda Ti, n=kc_n, jn=jn: Ti[:, :n * jn * 128].rearrange("p (j k f) -> p j k f", j=jn, k=n),
                         lambda Si, n=kc_n, jn=jn: Si[:, :n * jn * 128].rearrange("p (k c) -> p k c", k=n),
                         lambda Ti, n=kc_n, jn=jn: Ti[:, :n * jn * 128].rearrange("p (j x) -> p j x", j=jn),
                         dst3))
        def job_plain(sv, a, b, dst):
            jobs.append((sv, a * b, lambda Si: Si[:, :a * b], lambda Ti: Ti[:, :a * b],
                         lambda Si: Si[:, :a * b].rearrange("p (a b) -> p a b", a=a),
                         lambda Ti: Ti[:, :a * b].rearrange("p (a b) -> p a b", a=a), dst))
        def jobs_gate(src, dst, npair, c0=0):
            for jj in range(npair):
                job_perm(src, c0 + jj * 256, DC, 0, dst[2 * jj:2 * jj + 2, :, :].rearrange("j p x -> p j x"))
        def jobs_down(src, dst):
            for mm_ in range(8):
                for q in range(4):
                    job_perm(src, mm_ * 256, 11, q * 1408,
                             dst[2 * mm_:2 * mm_ + 2, :, q * 1408:(q + 1) * 1408].rearrange("m p x -> p m x"))
        jobs_gate(w_f32['w1_gate'], wsc['w1_gate'], 22); jobs_gate(w_f32['w1_up'], wsc['w1_up'], 22)
        jobs_down(w_f32['w1_down'], wsc['w1_down'])
        jobs_gate(w_in, s_winfm, 6)
        for k0, kn in ((0, 6), (6, 6), (12, 4)):
            job_plain(w_in[k0 * 128:(k0 + kn) * 128, 1536:2112].rearrange("(k p) c -> p k c", p=128), kn, 576,
                      s_winkv[:, k0 * 576:(k0 + kn) * 576].rearrange("p (a b) -> p a b", a=kn))
        job_plain(wq_nope.rearrange("(k p) c -> p k c", p=128), 4, 1024, s_small['wqn'].rearrange("p (a b) -> p a b", a=4))
        job_plain(wq_rope.rearrange("(k p) c -> p k c", p=128), 4, 512, s_small['wqr'].rearrange("p (a b) -> p a b", a=4))
        job_plain(wq_rot.rearrange("(k p) c -> p k c", p=128), 4, 512, s_small['wqx'].rearrange("p (a b) -> p a b", a=4))
        job_plain(wukT.rearrange("p (a b) -> p a b", a=8), 8, 512, s_small['wuk'].rearrange("p (a b) -> p a b", a=8))
        job_plain(w_uv.rearrange("(k p) c -> p k c", p=128), 4, 1024, s_small['wuv'].rearrange("p (a b) -> p a b", a=4))
        job_plain(w_pool.rearrange("(k p) c -> p k c", p=128), 8, 256, s_small['wpool'].rearrange("p (a b) -> p a b", a=8))
        jobs_bg = []
        def bg_gate(src, dst, npair):
            for jj in range(npair):
                for hf in range(2):
                    job_perm(src, jj * 256, 8, hf * 1024, dst[2 * jj:2 * jj + 2, :, hf * 1024:(hf + 1) * 1024].rearrange("j p x -> p j x"), lst=jobs_bg)
        def bg_down(src, dst):
            for m_ in range(DC):
                for q in range(4):
                    job_perm(src, m_ * 128, 11, q * 1408, dst[m_:m_ + 1, :, q * 1408:(q + 1) * 1408].rearrange("j p x -> p j x"), ncols=128, lst=jobs_bg)
        bg_gate(w_out, s_wo, 8)
        bg_gate(w_f32['w2_gate'], wsc['w2_gate'], 22); bg_gate(w_f32['w2_up'], wsc['w2_up'], 22)
        bg_down(w_f32['w2_down'], wsc['w2_down'])
        PF = 4
        ceng = ['vector', 'scalar', 'vector', 'scalar', 'gpsimd']
        for n in range(len(jobs) + PF):
            if n < len(jobs):
                sv, ne, civ, cov, siv, sov, dst = jobs[n]; i = n % NSL
                K.dma('sync', siv(S[i]), sv, 'cS%d' % i, writes=[bS[i]])
            m = n - PF
            if m >= 0:
                sv, ne, civ, cov, siv, sov, dst = jobs[m]; i = m % NSL
                K.cp(ceng[m % 5], cov(T[i]), civ(S[i]), [bS[i]], [bT[i]])
                K.dma('sync', dst, sov(T[i]), 'cT%d' % i, reads=[bT[i]])
        K.barrier(); K.emit()
    if stop <= 0: return nc, es, K, locals()

    with ExitStack() as es2:
        XB = [sb('xb%d' % i, [128, D], F32) for i in range(2)]; bXB = [Buf('xb%d' % i) for i in range(2)]
        XS = [sb('xs%d' % i, [128, DC, 128], F32) for i in range(2)]; bXS = [Buf('xs%d' % i) for i in range(2)]
        for blk in range(17):
            nt = 128 if blk < 16 else NS
            i = blk % 2
            K.dma('sync', XB[i][:nt, :], x_loc[blk * 128:blk * 128 + nt, :], 'xb%d' % i, writes=[bXB[i]])
            for g in range(4):
                pb = (blk * 4 + g) % 8
                for c4 in range(4):
                    c = g * 4 + c4
                    K.tr(PS[pb][:, c4 * 128:c4 * 128 + nt], XB[i][:nt, c * 128:(c + 1) * 128], identf[:nt, :nt],
                         [bXB[i], b_identf], [bPS[pb]])
                K.cp('scalar' if g % 2 == 0 else 'vector', XS[i][:, g * 4:(g + 1) * 4, :nt],
                     PS[pb][:, :].rearrange("p (c t) -> p c t", c=4)[:, :, :nt], [bPS[pb]], [bXS[i]])
            K.dma('sync', xT[:, blk * 128:blk * 128 + nt].rearrange("(c p) n -> p c n", p=128), XS[i][:, :, :nt],
                  'xs%d' % i, reads=[bXS[i]])
        K.barrier(); K.emit()
    if stop <= 1: return nc, es, K, locals()

    def prenorm(BIG, bBIG, XN, bXN, N, gcol, SQ, bSQ, R1, bR1, ps_ss, b_ss, nchunk=DC, dim=D):
        for c in range(nchunk):
            i = c % 2
            K.act(SQ[i][:, :N], BIG[:, c, :N], AF.Square, [bBIG], [bSQ[i]])
            K.mm(ps_ss[:, :N], ones_bf[:, :], SQ[i][:, :N], c == 0, c == nchunk - 1, [bSQ[i], b_ones], [b_ss])
        K.act(R1[0][:, :N], ps_ss[:, :N], AF.Sqrt, [b_ss, b_cst], [bR1[0]], scale=1.0 / dim, bias=epsb)
        K.recip(R1[1][:, :N], R1[0][:, :N], [bR1[0]], [bR1[1]])
        for c in range(nchunk):
            K.stt(XN[:, c, :N], BIG[:, c, :N], gv[:, gcol + c:gcol + c + 1], R1[1][:, :N], ALU.mult, ALU.mult,
                  [bBIG, bR1[1], b_gv], [bXN])

    def linear_to_big(inT, b_in, KC, wscr, M, WS, bWS, wkey, BIG, bBIG, N, SQ, bSQ, ps_y, b_y, ps_ss, b_ss):
        pend = None
        for m in range(M):
            i = m % 2
            K.dma('sync', WS[i][:, :KC * 128], wscr[m, :, :], wkey + str(i), writes=[bWS[i]])
            for kc in range(KC):
                K.mm(ps_y[i][:, :N], WS[i][:, kc * 128:(kc + 1) * 128], inT[:, kc, :N], kc == 0, kc == KC - 1,
                     [bWS[i], b_in], [b_y[i]])
            if pend is not None and not os.environ.get('DBG_NOPEND'):
                K.mm(*pend[0], **pend[1])
            K.cp('vector', BIG[:, m, :N], ps_y[i][:, :N], [b_y[i]], [bBIG])
            K.act(SQ[i][:, :N], BIG[:, m, :N], AF.Square, [bBIG], [bSQ[i]])
            pend = ((ps_ss[:, :N], ones_bf[:, :], SQ[i][:, :N], m == 0, m == M - 1), dict(reads=[bSQ[i], b_ones], writes=[b_ss]))
        if not os.environ.get('DBG_NOPEND'): K.mm(*pend[0], **pend[1])

    def post_residual(BIG, bBIG, N, gcol, alpha, R1, bR1, ps_ss, b_ss, src, dst, col0, HC, bHC, OC, bOC, TMP, bTMP, dcol0=None):
        if dcol0 is None: dcol0 = col0
        K.act(R1[0][:, :N], ps_ss[:, :N], AF.Sqrt, [b_ss, b_cst], [bR1[0]], scale=1.0 / D, bias=epsb)
        K.recip(R1[1][:, :N], R1[0][:, :N], [bR1[0]], [bR1[1]])
        for c in range(DC):
            i = c % 2
            K.dma('sync', HC[i][:, :N], src[c * 128:(c + 1) * 128, col0:col0 + N], 'hc%d' % i, writes=[bHC[i]])
            K.stt(TMP[:, :N], BIG[:, c, :N], gv[:, gcol + c:gcol + c + 1], R1[1][:, :N], ALU.mult, ALU.mult,
                  [bBIG, bR1[1], b_gv], [bTMP])
            K.stt(OC[i][:, :N], TMP[:, :N], float(alpha), HC[i][:, :N], ALU.mult, ALU.add, [bTMP, bHC[i]], [bOC[i]])
            K.dma('sync', dst[c * 128:(c + 1) * 128, dcol0:dcol0 + N], OC[i][:, :N], 'oc%d' % i, reads=[bOC[i]])

    def ffn_phase(tiles, src, dst, gpre, gpost, wg, wu, wd, bg=None):
        with ExitStack() as es2_:
            nonlocal es2
            es2 = es2_
            WT = 512 + NS
            BIG = sb('BIG', [128, DC, WT], F32); bBIG = Buf('BIG')
            XN = sb('XN', [128, DC, WT], BF16); bXN = Buf('XN')
            AT = sb('AT', [128, FC, WT], BF16); bAT = Buf('AT')
            WGU = [sb('wgu%d' % i, [128, 2, D], BF16) for i in range(4)]; bWGU = [Buf('wgu%d' % i) for i in range(4)]
            WD = [sb('wd%d' % i, [128, FF], BF16) for i in range(2)]; bWD = [Buf('wd%d' % i) for i in range(2)]
            SQ = [sb('sq%d' % i, [128, WT], BF16) for i in range(2)]; bSQ = [Buf('sq%d' % i) for i in range(2)]
            R1 = [sb('r1%d' % i, [128, WT], F32) for i in range(2)]; bR1 = [Buf('r1%d' % i) for i in range(2)]
            SG = [sb('sg%d' % i, [128, WT], F32) for i in range(2)]; bSG = [Buf('sg%d' % i) for i in range(2)]
            HC = [sb('hc%d' % i, [128, WT], F32) for i in range(4)]; bHC = [Buf('hc%d' % i) for i in range(4)]
            OC = [sb('oc%d' % i, [128, WT], F32) for i in range(2)]; bOC = [Buf('oc%d' % i) for i in range(2)]
            TMP2 = [sb('tmp%d' % i, [128, WT], F32) for i in range(2)]; bTMP2 = [Buf('tmp%d' % i) for i in range(2)]
            SQB = [sb('sqb%d' % i, [128, WT], BF16) for i in range(2)]; bSQB = [Buf('sqb%d' % i) for i in range(2)]
            R1B = [sb('r1b%d' % i, [128, WT], F32) for i in range(2)]; bR1B = [Buf('r1b%d' % i) for i in range(2)]
            if bg:
                BS = [sb('bgS%d' % i, [128, 2048], F32) for i in range(1)]; bBS = [Buf('bgS%d' % i) for i in range(1)]
                BT = [sb('bgT%d' % i, [128, 2048], BF16) for i in range(1)]; bBT = [Buf('bgT%d' % i) for i in range(1)]
            bgn = [0]
            def bg_step():
                n = bgn[0]; bgn[0] += 1
                if not bg or n > len(bg): return
                if n >= 1:
                    sv, ne, civ, cov, siv, sov, dd = bg[n - 1]
                    K.cp('scalar', cov(BT[0]), civ(BS[0]), [bBS[0]], [bBT[0]])
                    K.dma('gpsimd', dd, sov(BT[0]), 'bgT0', reads=[bBT[0]])
                if n < len(bg):
                    sv, ne, civ, cov, siv, sov, dd = bg[n]
                    K.dma('gpsimd', siv(BS[0]), sv, 'bgS0', writes=[bBS[0]])
            def wgu_load(j):
                ws = j % 4
                K.dma('sync', WGU[ws][:, 0, :], wg[j, :, :], 'wgu%d' % ws, writes=[bWGU[ws]])
                K.dma('sync', WGU[ws][:, 1, :], wu[j, :, :], 'wgu%d' % ws, writes=[bWGU[ws]])
            def ld_chunk(tl, c):
                (c0_, N_, ex_) = tl; h = c % 4
                K.dma('sync', HC[h][:, :N_], src[c * 128:(c + 1) * 128, c0_:c0_ + N_], 'hc%d' % h, writes=[bHC[h]])
                if ex_:
                    K.dma('sync', HC[h][:, N_:N_ + ex_[1]], src[c * 128:(c + 1) * 128, ex_[0]:ex_[0] + ex_[1]], 'hc%d' % h, writes=[bHC[h]])
            for ti, (col0, N, ex) in enumerate(tiles):
                EN = ex[1] if ex else 0; W = N + EN
                nxt = tiles[ti + 1] if ti + 1 < len(tiles) else None
                if ti == 0:
                    K.dma('sync', BIG[:, :, :N], src[:, col0:col0 + N].rearrange("(c p) n -> p c n", p=128), 'big', writes=[bBIG])
                    if ex:
                        K.dma('sync', BIG[:, :, N:W], src[:, ex[0]:ex[0] + EN].rearrange("(c p) n -> p c n", p=128), 'big', writes=[bBIG])
                    for c in range(DC):
                        i = c % 2
                        K.act(SQ[i][:, :W], BIG[:, c, :W], AF.Square, [bBIG], [bSQ[i]])
                        K.mm(PS[6][:, :N], ones_bf[:, :], SQ[i][:, :N], c == 0, c == DC - 1, [bSQ[i], b_ones], [bPS[6]])
                        if ex:
                            K.mm(PS[7][:, :EN], ones_bf[:, :], SQ[i][:, N:W], c == 0, c == DC - 1, [bSQ[i], b_ones], [bPS[7]])
                    K.act(R1[0][:, :N], PS[6][:, :N], AF.Sqrt, [bPS[6], b_cst], [bR1[0]], scale=1.0 / D, bias=epsb)
                    if ex:
                        K.act(R1[0][:, N:W], PS[7][:, :EN], AF.Sqrt, [bPS[7], b_cst], [bR1[0]], scale=1.0 / D, bias=epsb)
                    K.recip(R1[1][:, :W], R1[0][:, :W], [bR1[0]], [bR1[1]])
                    for c in range(DC):
                        K.stt(XN[:, c, :W], BIG[:, c, :W], gv[:, gpre + c:gpre + c + 1], R1[1][:, :W], ALU.mult, ALU.mult,
                              [bBIG, bR1[1], b_gv], [bXN])
                    for j in range(4): wgu_load(j)
                for j in range(FC):
                    i = j % 2
                    ws = j % 4
                    for which in range(2):
                        pb = 2 * i + which
                        for kc in range(DC):
                            K.mm(PS[pb][:, :N], WGU[ws][:, which, kc * 128:(kc + 1) * 128], XN[:, kc, :N], kc == 0, kc == DC - 1,
                                 [bWGU[ws], bXN], [bPS[pb]])
                        if ex:
                            for kc in range(DC):
                                K.mm(PS[6 + i][:, which * 32:which * 32 + EN], WGU[ws][:, which, kc * 128:(kc + 1) * 128], XN[:, kc, N:W],
                                     kc == 0, kc == DC - 1, [bWGU[ws], bXN], [bPS[6 + i]])
                    K.act(SG[i][:, :N], PS[2 * i][:, :N], AF.Silu, [bPS[2 * i]], [bSG[i]])
                    K.tt('vector', AT[:, j, :N], SG[i][:, :N], PS[2 * i + 1][:, :N], ALU.mult, [bSG[i], bPS[2 * i + 1]], [bAT])
                    bg_step()
                    if j + 4 < FC: wgu_load(j + 4)
                    if ex:
                        K.act(SG[i][:, N:W], PS[6 + i][:, 0:EN], AF.Silu, [bPS[6 + i]], [bSG[i]])
                        K.tt('vector', AT[:, j, N:W], SG[i][:, N:W], PS[6 + i][:, 32:32 + EN], ALU.mult, [bSG[i], bPS[6 + i]], [bAT])
                pend = []
                for m in range(DC):
                    i = m % 2
                    K.dma('sync', WD[i][:, :], wd[m, :, :], 'wd%d' % i, writes=[bWD[i]])
                    for kc in range(FC):
                        K.mm(PS[4 + i][:, :N], WD[i][:, kc * 128:(kc + 1) * 128], AT[:, kc, :N], kc == 0, kc == FC - 1, [bWD[i], bAT], [bPS[4 + i]])
                    if ex:
                        for kc in range(FC):
                            K.mm(PS[i][:, :EN], WD[i][:, kc * 128:(kc + 1) * 128], AT[:, kc, N:W], kc == 0, kc == FC - 1, [bWD[i], bAT], [bPS[i]])
                    for p_ in pend: K.mm(*p_[0], **p_[1])
                    K.cp('vector', BIG[:, m, :N], PS[4 + i][:, :N], [bPS[4 + i]], [bBIG])
                    if ex:
                        K.cp('vector', BIG[:, m, N:W], PS[i][:, :EN], [bPS[i]], [bBIG])
                    K.act(SQ[i][:, :W], BIG[:, m, :W], AF.Square, [bBIG], [bSQ[i]])
                    pend = [((PS[7][:, :N], ones_bf[:, :], SQ[i][:, :N], m == 0, m == DC - 1), dict(reads=[bSQ[i], b_ones], writes=[bPS[7]]))]
                    if ex:
                        pend.append(((PS[2][:, :EN], ones_bf[:, :], SQ[i][:, N:W], m == 0, m == DC - 1), dict(reads=[bSQ[i], b_ones], writes=[bPS[2]])))
                    if nxt:
                        (nc0, nN, nex) = nxt; nEN = nex[1] if nex else 0; nW = nN + nEN
                        ld_chunk(nxt, m)
                        K.act(SQB[i][:, :nW], HC[m % 4][:, :nW], AF.Square, [bHC[m % 4]], [bSQB[i]])
                        pend.append(((PS[6][:, :nN], ones_bf[:, :], SQB[i][:, :nN], m == 0, m == DC - 1), dict(reads=[bSQB[i], b_ones], writes=[bPS[6]])))
                        if nex:
                            pend.append(((PS[3][:, :nEN], ones_bf[:, :], SQB[i][:, nN:nW], m == 0, m == DC - 1), dict(reads=[bSQB[i], b_ones], writes=[bPS[3]])))
                for p_ in pend: K.mm(*p_[0], **p_[1])
                if nxt:
                    K.act(R1B[0][:, :nN], PS[6][:, :nN], AF.Sqrt, [bPS[6], b_cst], [bR1B[0]], scale=1.0 / D, bias=epsb)
                    if nex:
                        K.act(R1B[0][:, nN:nW], PS[3][:, :nEN], AF.Sqrt, [bPS[3], b_cst], [bR1B[0]], scale=1.0 / D, bias=epsb)
                    K.recip(R1B[1][:, :nW], R1B[0][:, :nW], [bR1B[0]], [bR1B[1]])
                    ld_chunk(nxt, 0); ld_chunk(nxt, 1)
                    for c in range(DC):
                        if c + 2 < DC: ld_chunk(nxt, c + 2)
                        K.stt(XN[:, c, :nW], HC[c % 4][:, :nW], gv[:, gpre + c:gpre + c + 1], R1B[1][:, :nW], ALU.mult, ALU.mult,
                              [bHC[c % 4], bR1B[1], b_gv], [bXN])
                    for j in range(4): wgu_load(j)
                K.act(R1[0][:, :N], PS[7][:, :N], AF.Sqrt, [bPS[7], b_cst], [bR1[0]], scale=1.0 / D, bias=epsb)
                if ex:
                    K.act(R1[0][:, N:W], PS[2][:, :EN], AF.Sqrt, [bPS[2], b_cst], [bR1[0]], scale=1.0 / D, bias=epsb)
                K.recip(R1[1][:, :W], R1[0][:, :W], [bR1[0]], [bR1[1]])
                def ld_hc(c):
                    h = c % 4
                    K.dma('sync', HC[h][:, :N], src[c * 128:(c + 1) * 128, col0:col0 + N], 'hc%d' % h, writes=[bHC[h]])
                    if ex:
                        K.dma('sync', HC[h][:, N:W], src[c * 128:(c + 1) * 128, ex[0]:ex[0] + EN], 'hc%d' % h, writes=[bHC[h]])
                ld_hc(0); ld_hc(1)
                for c in range(DC):
                    i = c % 2; h = c % 4; TMP = TMP2[i]; bTMP = bTMP2[i]
                    if c + 2 < DC: ld_hc(c + 2)
                    K.stt(TMP[:, :W], BIG[:, c, :W], gv[:, gpost + c:gpost + c + 1], R1[1][:, :W], ALU.mult, ALU.mult,
                          [bBIG, bR1[1], b_gv], [bTMP])
                    K.stt(OC[i][:, :W], TMP[:, :W], 0.5, HC[h][:, :W], ALU.mult, ALU.add, [bTMP, bHC[h]], [bOC[i]])
                    K.dma('sync', dst[c * 128:(c + 1) * 128, col0:col0 + N], OC[i][:, :N], 'oc%d' % i, reads=[bOC[i]])
                    if ex:
                        K.dma('sync', dst[c * 128:(c + 1) * 128, ex[0]:ex[0] + EN], OC[i][:, N:W], 'oc%d' % i, reads=[bOC[i]])
            while bg and bgn[0] <= len(bg): bg_step()
            K.barrier(); K.emit()

    es2 = None
    TILES_ALL = [(1024, 512, None), (1536, 512, None), (0, 512, None), (512, 512, (2048, NS))]
    ffn_phase(TILES_ALL, xT, h1T, G_F1PRE, G_F1POST, wsc['w1_gate'], wsc['w1_up'], wsc['w1_down'], bg=jobs_bg)
    if stop <= 2: return nc, es, K, locals()

    def v4(ap):
        return ap.rearrange("p (b t) -> p b t", b=4)

    with ExitStack() as es2:
        BIG = sb('BIG', [128, DC, 512], F32); bBIG = Buf('BIG')
        HN = sb('HN', [128, DC, 512], BF16); bHN = Buf('HN')
        WKV = sb('WKV', [128, DC, 576], BF16); bWKV = Buf('WKV')
        WFM = [sb('wfm%d' % i, [128, D], BF16) for i in range(2)]; bWFM = [Buf('wfm%d' % i) for i in range(2)]
        WPL = sb('WPL', [128, 2048], BF16); bWPL = Buf('WPL')
        UE = [sb('ue%d' % i, [128, 4, 143], F32) for i in range(2)]; bUE = [Buf('ue%d' % i) for i in range(2)]
        PA = sb('PA', [128, 4, 143], F32); bPA = Buf('PA'); PB = sb('PB', [128, 4, 143], F32); bPB = Buf('PB')
        DT = sb('DT', [128, 8, 512], BF16); bDT = Buf('DT')
        CQ = sb('CQ', [128, 4, 512], F32); bCQ = Buf('CQ')
        CRAW = sb('CRAW', [128, 512], F32); bCRAW = Buf('CRAW'); JUNK = sb('JUNK', [128, 512], F32); bJUNK = Buf('JUNK')
        CKV = [sb('ckv%d' % i, [128, 512], F32) for i in range(2)]; bCKV = [Buf('ckv%d' % i) for i in range(2)]
        VST = [sb('vst%d' % i, [128, 512], BF16) for i in range(2)]; bVST = [Buf('vst%d' % i) for i in range(2)]
        KPE = [sb('kpe%d' % i, [128, 64], F32) for i in range(2)]; bKPE = [Buf('kpe%d' % i) for i in range(2)]
        KR = sb('KR', [128, 256], F32); bKR = Buf('KR')
        SSK = sb('SSK', [128, 4], F32); bSS = Buf('SSK')
        KTST = sb('KTST', [128, 5, 512], BF16); bKTST = Buf('KTST')
        COSK = sb('COSK', [128, 17, 64], F32); SINK = sb('SINK', [128, 17, 64], F32); GKVB = sb('GKVB', [128, 512], F32)
        bTAB = Buf('TAB')
        INVC = sb('INVC', [128, 4, 512], F32); bINVC = Buf('INVC')
        UTAIL = sb('UTAIL', [128, 8, 8, 15], F32); bUTAIL = Buf('UTAIL')
        ULAST = sb('ULAST', [128, 8, 16], F32); bULAST = Buf('ULAST')
        PLS = sb('PLS', [16, 1024], F32); bPLS = Buf('PLS')
        POS = [sb('pos%d' % i, [128, 512], BF16) for i in range(2)]; bPOS = [Buf('pos%d' % i) for i in range(2)]
        SQ = [sb('sq%d' % i, [128, 512], BF16) for i in range(2)]; bSQ = [Buf('sq%d' % i) for i in range(2)]
        R1 = [sb('r1%d' % i, [128, 512], F32) for i in range(2)]; bR1 = [Buf('r1%d' % i) for i in range(2)]
        HS = [sb('hs%d' % i, [120, 1024], F32) for i in range(2)]; bHS = [Buf('hs%d' % i) for i in range(2)]
        UEXS = sb('UEXS', [128, 8, 16, 16], F32); bUEXS = Buf('UEXS')
        SWS = sb('SWS', [128, 16], F32); bSWS = Buf('SWS')
        K.dma('sync', WKV[:, :, :], s_winkv.rearrange("p (a b) -> p a b", a=DC), 'wkv', writes=[bWKV])
        K.dma('sync', WPL[:, :], s_small['wpool'][:, :], 'wpl', writes=[bWPL])
        K.dma('sync', COSK[:, :, :], cosk[:, :, :], 'tab', writes=[bTAB])
        K.dma('sync', SINK[:, :, :], sink[:, :, :], 'tab', writes=[bTAB])
        K.dma('sync', GKVB[:, :], gkv_b[:, :], 'tab', writes=[bTAB])
        for h in range(2):
            K.dma('sync', HS[h][:, :], state_pool[8 * h:8 * h + 8, :, :].rearrange("b r c -> (b r) c"), 'hs%d' % h, writes=[bHS[h]])
            for m in range(8):
                pb = 3 + (m % 2)
                K.tr(PS[pb][:, 0:120], HS[h][:120, m * 128:(m + 1) * 128], identf[:120, :120], [bHS[h], b_identf], [bPS[pb]])
                K.cp('scalar', UEXS[:, m, 8 * h:8 * h + 8, 0:15], PS[pb][:, 0:120].rearrange("p (b r) -> p b r", b=8),
                     [bPS[pb]], [bUEXS])
        K.dma('sync', spool_hist[:, :, :], state_pool[:, 1:15, :], 'sph')
        P2T = [(1024, 512, 'partner', 0), (1536, 512, 'partner', 1), (0, 512, 'own', 0), (512, 512, 'own', 1), (2048, NS, 'sample', 0)]
        for (col0, N, kind, tb) in P2T:
            nblk = max(1, N // 128); nt = min(N, 128)
            ocol0 = 1024 if kind == 'sample' else tb * 512
            K.dma('sync', BIG[:, :, :N], h1T[:, col0:col0 + N].rearrange("(c p) n -> p c n", p=128), 'big', writes=[bBIG])
            prenorm(BIG, bBIG, HN, bHN, N, G_MIXPRE, SQ, bSQ, R1, bR1, PS[6], bPS[6])
            def kv_mm(b):
                pk, pr = (0, 1) if b % 2 == 0 else (5, 7)
                for kc in range(DC):
                    K.mm(PS[pk][:nt, :512], HN[:, kc, b * 128:b * 128 + nt], WKV[:, kc, 0:512], kc == 0, kc == DC - 1, [bHN, bWKV], [bPS[pk]])
                for kc in range(DC):
                    K.mm(PS[pr][:nt, 0:64], HN[:, kc, b * 128:b * 128 + nt], WKV[:, kc, 512:576], kc == 0, kc == DC - 1, [bHN, bWKV], [bPS[pr]])
            kv_mm(0)
            for b in range(nblk):
                gb = col0 // 128 + b; i = b % 2
                pk, pr = (0, 1) if b % 2 == 0 else (5, 7)
                if b + 1 < nblk: kv_mm(b + 1)
                K.cp('scalar', CRAW[:nt, :], PS[pk][:nt, :512], [bPS[pk]], [bCRAW])
                K.tt('vector', JUNK[:nt, :], CRAW[:nt, :], CRAW[:nt, :], ALU.mult, [bCRAW], [bJUNK])
                K.op('vector', lambda e, nt=nt: e.tensor_reduce(out=SSK[:nt, 0:1], in_=JUNK[:nt, :], axis=AX.X, op=ALU.add), [bJUNK], [bSS])
                K.act(SSK[:nt, 1:2], SSK[:nt, 0:1], AF.Sqrt, [bSS, b_cst], [bSS], scale=1.0 / KV, bias=cst[:nt, 1:2])
                K.recip(SSK[:nt, 2:3], SSK[:nt, 1:2], [bSS], [bSS])
                K.stt(CKV[i][:nt, :], CRAW[:nt, :], SSK[:nt, 2:3], GKVB[:nt, :], ALU.mult, ALU.mult, [bCRAW, bSS, bTAB], [bCKV[i]])
                K.dma('sync', kv_out[gb * 128:gb * 128 + nt, :], CKV[i][:nt, :], 'ckv%d' % i, reads=[bCKV[i]])
                K.cp('gpsimd', VST[i][:nt, :], CKV[i][:nt, :], [bCKV[i]], [bVST[i]])
                K.dma('sync', V_s[gb * 128:gb * 128 + nt, :], VST[i][:nt, :], 'vst%d' % i, reads=[bVST[i]])
                for cc in range(4):
                    K.tr(PS[2][:, cc * 128:cc * 128 + nt], CKV[i][:nt, cc * 128:(cc + 1) * 128], identf[:nt, :nt],
                         [bCKV[i], b_identf], [bPS[2]])
                K.cp('scalar', KTST[:, 0:4, b * 128:b * 128 + nt], v4(PS[2][:, :])[:, :, :nt], [bPS[2]], [bKTST])
                K.cp('scalar', KR[:nt, 0:64], PS[pr][:nt, 0:64], [bPS[pr]], [bKR])
                K.cp('scalar', KR[:nt, 64:96], PS[pr][:nt, 32:64], [bPS[pr]], [bKR])
                K.cp('scalar', KR[:nt, 96:128], PS[pr][:nt, 0:32], [bPS[pr]], [bKR])
                K.tt('vector', KR[:nt, 128:192], KR[:nt, 0:64], COSK[:nt, gb, :], ALU.mult, [bKR, bTAB], [bKR])
                K.tt('vector', KR[:nt, 192:256], KR[:nt, 64:128], SINK[:nt, gb, :], ALU.mult, [bKR, bTAB], [bKR])
                K.tt('vector', KPE[i][:nt, :], KR[:nt, 128:192], KR[:nt, 192:256], ALU.add, [bKR], [bKPE[i]])
                K.dma('sync', kpe_out[gb * 128:gb * 128 + nt, :], KPE[i][:nt, :], 'kpe%d' % i, reads=[bKPE[i]])
                K.tr(PS[pr][0:64, 128:128 + nt], KPE[i][:nt, 0:64], identf[:nt, :nt], [bKPE[i], b_identf], [bPS[pr]])
                K.cp('scalar', KTST[0:64, 4, b * 128:b * 128 + nt], PS[pr][0:64, 128:128 + nt], [bPS[pr]], [bKTST])
            K.dma('sync', KT_s[0:512, col0:col0 + N].rearrange("(c p) n -> p c n", p=128), KTST[:, 0:4, :N], 'ktst', reads=[bKTST])
            K.dma('sync', KT_s[512:576, col0:col0 + N], KTST[0:64, 4, :N], 'ktst', reads=[bKTST])
            if kind != 'partner':
                tsel = 2 if kind == 'sample' else tb
                K.dma('sync', INVC[:, :, :], invc[:, tsel, :, :], 'invc', writes=[bINVC])
            pms = []
            for m in range(8):
                i = m % 2; pu = PS[3 + i]; bpu = bPS[3 + i]; g = m // 2; w = (2, 4, 8, 16)[g]
                K.dma('sync', WFM[i][:, :], s_winfm[m, :, :], 'wfm%d' % i, writes=[bWFM[i]])
                for kc in range(DC):
                    K.mm(pu[:, :N], WFM[i][:, kc * 128:(kc + 1) * 128], HN[:, kc, :N], kc == 0, kc == DC - 1, [bWFM[i], bHN], [bpu])
                if kind == 'partner':
                    K.cp('scalar', UTAIL[:, m, tb * 4:(tb + 1) * 4, :], v4(pu[:, :])[:, :, 113:128], [bpu], [bUTAIL])
                    continue
                if kind == 'own':
                    U = UE[i]; bU = bUE[i]
                    K.cp('scalar', U[:, :, 15:143], v4(pu[:, :]), [bpu], [bU])
                    K.cp('gpsimd', U[:, :, 0:15], UTAIL[:, m, tb * 4:(tb + 1) * 4, :], [bUTAIL], [bU])
                    K.tt('gpsimd', PA[:, :, 1:143], U[:, :, 1:143], U[:, :, 0:142], ALU.add, [bU], [bPA]); cur, bcur = PA, bPA
                    if w >= 4:
                        K.tt('gpsimd', PB[:, :, 3:143], PA[:, :, 3:143], PA[:, :, 1:141], ALU.add, [bPA], [bPB]); cur, bcur = PB, bPB
                    if w >= 8:
                        K.tt('gpsimd', PA[:, :, 7:143], PB[:, :, 7:143], PB[:, :, 3:139], ALU.add, [bPB], [bPA]); cur, bcur = PA, bPA
                    if w >= 16:
                        K.tt('gpsimd', PB[:, :, 15:143], PA[:, :, 15:143], PA[:, :, 7:135], ALU.add, [bPA], [bPB]); cur, bcur = PB, bPB
                    K.tt('gpsimd', cur[:, :, 15:143], cur[:, :, 15:143], v4(INVC[:, g, :]), ALU.mult, [bcur, bINVC], [bcur])
                    K.tt('gpsimd', v4(DT[:, m, :]), cur[:, :, 15:143], U[:, :, 15:143], ALU.subtract, [bcur, bU], [bDT])
                    if tb == 1:
                        K.cp('gpsimd', ULAST[:, m, :], U[:, 3, 127:143], [bU], [bULAST])
                else:
                    K.cp('scalar', UEXS[:, m, :, 15], pu[:, :NS], [bpu], [bUEXS])
                    K.op('vector', lambda e, m=m, w=w: e.tensor_reduce(out=SWS[:, :], in_=UEXS[:, m, :, 16 - w:16], axis=AX.X, op=ALU.add),
                         [bUEXS], [bSWS])
                    K.stt(DT[:, m, :NS], SWS[:, :], 1.0 / w, UEXS[:, m, :, 15], ALU.mult, ALU.subtract, [bSWS, bUEXS], [bDT])
                    K.cp('gpsimd', ULAST[:, m, :], UEXS[:, m, :, 15], [bUEXS], [bULAST])
                if m % 2 == 1:
                  def pm(g=g):
                    for dd in range(2):
                          for cc in range(2):
                              k = g * 2 + cc
                              K.mm(PS[5][:, :N], WPL[:, k * 256 + dd * 128:k * 256 + (dd + 1) * 128], DT[:, 2 * g + cc, :N], cc == 0, cc == 1,
                                   [bWPL, bDT], [bPS[5]])
                          mo = 2 * g + dd
                          K.ts('vector', POS[dd][:, :N], PS[5][:, :N], gv[:, G_PSC + mo:G_PSC + mo + 1], ALU.mult, [bPS[5], b_gv], [bPOS[dd]])
                          K.dma('sync', mixT[mo * 128:(mo + 1) * 128, ocol0:ocol0 + N], POS[dd][:, :N], 'pos%d' % dd, reads=[bPOS[dd]])
                  pms.append(pm)
                  if len(pms) > 1: pms.pop(0)()
            if kind == 'partner':
                continue
            if kind == 'sample' or tb == 1:
                for m in range(8):
                    pb = 7 if m < 4 else 5
                    K.tr(PS[pb][:16, (m % 4) * 128:(m % 4 + 1) * 128], ULAST[:, m, :], identf[:, :], [bULAST, b_identf], [bPS[pb]])
                K.cp('scalar', PLS[:16, 0:512], PS[7][:16, :512], [bPS[7]], [bPLS])
                K.cp('scalar', PLS[:16, 512:1024], PS[5][:16, :512], [bPS[5]], [bPLS])
                K.dma('sync', (spool_new if kind == 'sample' else pool_last)[:, :], PLS[:16, :], 'pls', reads=[bPLS])
            for mq in range(4):
                m = 8 + mq; i = m % 2; pu = PS[3 + i]; bpu = bPS[3 + i]
                K.dma('sync', WFM[i][:, :], s_winfm[m, :, :], 'wfm%d' % i, writes=[bWFM[i]])
                for kc in range(DC):
                    K.mm(pu[:, :N], WFM[i][:, kc * 128:(kc + 1) * 128], HN[:, kc, :N], kc == 0, kc == DC - 1, [bWFM[i], bHN], [bpu])
                K.cp('scalar', CQ[:, mq, :N], pu[:, :N], [bpu], [bCQ])
            while pms: pms.pop(0)()
            prenorm(CQ, bCQ, CQ, bCQ, N, G_Q, SQ, bSQ, R1, bR1, PS[6], bPS[6], nchunk=4, dim=QL)
            K.dma('sync', cqn_s[:, ocol0:ocol0 + N].rearrange("(c p) n -> p c n", p=128), CQ[:, :, :N], 'cq', reads=[bCQ])
        K.barrier(); K.emit()
    if stop <= 3: return nc, es, K, locals()

    with ExitStack() as es2:
        KT = sb('KT', [128, 5, 2048], BF16); V = sb('V', [128, 16, 512], BF16); bKT = Buf('KT'); bV = Buf('V')
        KTN = sb('KTN', [128, 5, NS], BF16); VN = sb('VN', [NS, 512], BF16)
        WQN = sb('WQN', [128, 4096], BF16); WQR = sb('WQR', [128, 2048], BF16); WQX = sb('WQX', [128, 2048], BF16)
        WUK = sb('WUK', [128, 4096], BF16); WUV = sb('WUV', [128, 4096], BF16); bW = Buf('Wq')
        CQL = sb('CQL', [128, 4, 128], F32); bCQL = Buf('CQL'); CQB = sb('CQB', [128, 4, 128], BF16); bCQB = Buf('CQB')
        QN = sb('QN', [128, 8, 128], BF16); bQN = Buf('QN')
        QCAT = sb('QCAT', [128, 5, 1024], BF16); bQCAT = Buf('QCAT')
        COSQ = sb('COSQ', [128, 1024 + NS], F32); SINQ = sb('SINQ', [128, 1024 + NS], F32); TRI = sb('TRI', [128, 128], F32); bTQ = Buf('TQ')
        T1 = sb('T1', [128, 128], F32); T2 = sb('T2', [128, 128], F32); bT1 = Buf('T1'); bT2 = Buf('T2')
        PT = [sb('pt%d' % i, [128, 512], BF16) for i in range(2)]; bPT = [Buf('pt%d' % i) for i in range(2)]
        ACC = sb('ACC', [128, 1024], F32); bACC = Buf('ACC'); RINV = sb('RINV', [128, 512], F32); bRINV = Buf('RINV')
        OT = sb('OT', [128, 4, 1024], BF16); bOT = Buf('OT'); AST = sb('AST', [128, 8, 512], BF16); bAST = Buf('AST')
        PTAB = sb('PTAB', [128, NS * NPG // 4], I32); bPTAB = Buf('PTAB')
        IDX = sb('IDX', [128, NS * NPG // 4], I32); bIDX = Buf('IDX')
        K.dma('sync', KT[:, 0:4, :], KT_s[0:512, 0:2048].rearrange("(c p) n -> p c n", p=128), 'kt', writes=[bKT])
        K.dma('sync', KT[0:64, 4, :], KT_s[512:576, 0:2048], 'kt', writes=[bKT])
        K.dma('sync', V[:, :, :], V_s[0:2048, :].rearrange("(k p) c -> p k c", p=128), 'v', writes=[bV])
        K.dma('sync', KTN[:, 0:4, :], KT_s[0:512, 2048:2048 + NS].rearrange("(c p) n -> p c n", p=128), 'kt', writes=[bKT])
        K.dma('sync', KTN[0:64, 4, :], KT_s[512:576, 2048:2048 + NS], 'kt', writes=[bKT])
        K.dma('sync', VN[:, :], V_s[2048:2048 + NS, :], 'v', writes=[bV])
        for t_, k_ in ((WQN, 'wqn'), (WQR, 'wqr'), (WQX, 'wqx'), (WUK, 'wuk'), (WUV, 'wuv')):
            K.dma('sync', t_[:, :], s_small[k_][:, :], 'wq', writes=[bW])
        K.dma('sync', COSQ[:, :], cosq[:, :], 'tq', writes=[bTQ]); K.dma('sync', SINQ[:, :], sinq[:, :], 'tq', writes=[bTQ])
        K.dma('sync', TRI[:, :], tri_in[:, :], 'tq', writes=[bTQ])
        K.dma('sync', PTAB[:, :], ptab[:, :], 'ptab', writes=[bPTAB])
        K.ts('vector', IDX[:, :], PTAB[:, :], 32.0, ALU.mult, [bPTAB, b_cst], [bIDX], s2=cst[:, 2:3], op1=ALU.add)

        def qpath(c0, nq, qcat_view):
            K.dma('sync', CQL[:, :, :nq], cqn_s[:, c0:c0 + nq].rearrange("(c p) n -> p c n", p=128), 'cql', writes=[bCQL])
            K.cp('vector', CQB[:, :, :nq], CQL[:, :, :nq], [bCQL], [bCQB])
            for h in range(NH):
                for kc in range(4):
                    K.mm(PS[6][:, :nq], WQN[:, kc * 1024 + h * 128:kc * 1024 + (h + 1) * 128], CQB[:, kc, :nq], kc == 0, kc == 3, [bW, bCQB], [bPS[6]])
                K.cp('scalar', QN[:, h, :nq], PS[6][:, :nq], [bPS[6]], [bQN])
            for h in range(NH):
                for kc in range(4):
                    K.mm(PS[6][0:64, :nq], WQR[:, kc * 512 + h * 64:kc * 512 + (h + 1) * 64], CQB[:, kc, :nq], kc == 0, kc == 3, [bW, bCQB], [bPS[6]])
                for kc in range(4):
                    K.mm(PS[7][0:64, :nq], WQX[:, kc * 512 + h * 64:kc * 512 + (h + 1) * 64], CQB[:, kc, :nq], kc == 0, kc == 3, [bW, bCQB], [bPS[7]])
                K.tt('vector', T1[0:64, :nq], PS[6][0:64, :nq], COSQ[0:64, c0:c0 + nq], ALU.mult, [bPS[6], bTQ], [bT1])
                K.tt('vector', T2[0:64, :nq], PS[7][0:64, :nq], SINQ[0:64, c0:c0 + nq], ALU.mult, [bPS[7], bTQ], [bT2])
                K.tt('vector', qcat_view(4, h)[0:64, :], T1[0:64, :nq], T2[0:64, :nq], ALU.add, [bT1, bT2], [bQCAT])
            for h in range(NH):
                for cc in range(4):
                    K.mm(PS[6][:, cc * 128:cc * 128 + nq], WUK[:, h * 512 + cc * 128:h * 512 + (cc + 1) * 128], QN[:, h, :nq], True, True,
                         [bW, bQN], [bPS[6]])
                for cc in range(4):
                    K.cp('scalar' if cc % 2 == 0 else 'vector', qcat_view(cc, h), PS[6][:, cc * 128:cc * 128 + nq], [bPS[6]], [bQCAT])

        for i in range(8):
            tb = i // 4
            qpath(i * 128, 128, lambda ch, h: QCAT[:, ch, h * 128:(h + 1) * 128])
            kbs = list(range(0, i + 1)) + list(range(8, 8 + i + 1))
            for half in range(2):
                hs = slice(half * 512, (half + 1) * 512)
                def st(n, kb):
                    ps = PS[n % 2]; bps = bPS[n % 2]; pt = PT[n % 2]; bpt = bPT[n % 2]
                    for ch in range(5):
                        rows = 128 if ch < 4 else 64
                        K.mm(ps[:, :512], KT[:rows, ch, kb * 128:(kb + 1) * 128], QCAT[:rows, ch, hs], ch == 0, ch == 4, [bKT, bQCAT], [bps])
                    K.act(pt[:, :], ps[:, :512], AF.Exp, [bps], [bpt], scale=SM_SCALE)
                    if kb == i:
                        for hh in range(4):
                            K.tt('vector', pt[:, hh * 128:(hh + 1) * 128], pt[:, hh * 128:(hh + 1) * 128], TRI[:, :], ALU.mult, [bpt, bTQ], [bpt])
                    if kb == 8:
                        K.ts('vector', pt[:, :], pt[:, :], cst[:, 0:1], ALU.mult, [bpt, b_cst], [bpt])
                    if n == 0:
                        K.cp('gpsimd', ACC[:, hs], pt[:, :], [bpt], [bACC])
                    else:
                        K.tt('gpsimd', ACC[:, hs], ACC[:, hs], pt[:, :], ALU.add, [bpt, bACC], [bACC])
                def pv(n, kb):
                    pt = PT[n % 2]; bpt = bPT[n % 2]
                    for cc in range(4):
                        K.mm(PS[2 + cc][:, :512], V[:, kb, cc * 128:(cc + 1) * 128], pt[:, :], n == 0, n == len(kbs) - 1, [bV, bpt], [bPS[2 + cc]])
                st(0, kbs[0])
                for n, kb in enumerate(kbs):
                    if n + 1 < len(kbs): st(n + 1, kbs[n + 1])
                    pv(n, kb)
                K.mm(PS[6][:, :512], ones_f[:, :], ACC[:, hs], True, True, [b_ones, bACC], [bPS[6]])
                K.recip(RINV[:, :], PS[6][:, :512], [bPS[6]], [bRINV])
                for cc in range(4):
                    K.tt('vector', OT[:, cc, hs], PS[2 + cc][:, :512], RINV[:, :], ALU.mult, [bPS[2 + cc], bRINV], [bOT])
            for h in range(NH):
                for cc in range(4):
                    K.mm(PS[7][:, :128], WUV[:, cc * 1024 + h * 128:cc * 1024 + (h + 1) * 128], OT[:, cc, h * 128:(h + 1) * 128], cc == 0, cc == 3,
                         [bW, bOT], [bPS[7]])
                K.cp('scalar', AST[:, h, (i % 4) * 128:(i % 4 + 1) * 128], PS[7][:, :128], [bPS[7]], [bAST])
            if i % 4 == 3:
                K.dma('sync', mixT[1024:2048, tb * 512:(tb + 1) * 512].rearrange("(h p) n -> p h n", p=128), AST[:, :, :], 'ast', reads=[bAST])

        qpath(1024, NS, lambda ch, h: QCAT[:, ch, h * NS:(h + 1) * NS])
        GP = 4; NG = NPG // GP; NGT = NS * NG
        KPV = [sb('kpv%d' % i, [128, GP, 512], F32) for i in range(3)]; bKPV = [Buf('kpv%d' % i) for i in range(3)]
        KPR = [sb('kpr%d' % i, [128, GP, 64], F32) for i in range(3)]; bKPR = [Buf('kpr%d' % i) for i in range(3)]
        KTP = [sb('ktp%d' % i, [128, 4, 128], BF16) for i in range(8)]; bKTP = [Buf('ktp%d' % i) for i in range(8)]
        KTR = [sb('ktr%d' % i, [64, GP * 128], BF16) for i in range(2)]; bKTR = [Buf('ktr%d' % i) for i in range(2)]
        VP = [sb('vp%d' % i, [128, 512], BF16) for i in range(12)]; bVP = [Buf('vp%d' % i) for i in range(12)]
        PTS = [sb('pts%d' % i, [128, GP * 8], BF16) for i in range(2)]; bPTS = [Buf('pts%d' % i) for i in range(2)]
        PTN = sb('PTN', [NS, 8], BF16); bPTN = Buf('PTN')
        ACCS2 = [sb('ACCS%d' % i, [128, GP * 8], F32) for i in range(2)]; bACCS2 = [Buf('ACCS%d' % i) for i in range(2)]; RINVS = sb('RINVS', [8, 1], F32); bRINVS = Buf('RINVS')
        OS = sb('OS', [8, 512], F32); bOS = Buf('OS')
        OTS = sb('OTS', [128, 4, NH * NS], BF16); bOTS = Buf('OTS')
        ckv4 = cache_kv.rearrange("(r j) c -> r (j c)", j=GP); ckr4 = cache_kr.rearrange("(r j) c -> r (j c)", j=GP)
        def qrhs(ch, rows, b):
            return QCAT[:rows, ch, 0:NH * NS].rearrange("p (h b) -> p h b", b=NS)[:, :, b]
        def stageA(G):
            sl = G % 3
            K.op('gpsimd', lambda e, sl=sl, G=G: e.indirect_dma_start(
                out=KPV[sl][:, :, :].rearrange("p j c -> p (j c)"), out_offset=None, in_=ckv4[:, :],
                in_offset=bass.IndirectOffsetOnAxis(ap=IDX[:, G:G + 1], axis=0)), [bIDX], [bKPV[sl]], dsem='kpv%d' % sl)
            K.op('gpsimd', lambda e, sl=sl, G=G: e.indirect_dma_start(
                out=KPR[sl][:, :, :].rearrange("p j c -> p (j c)"), out_offset=None, in_=ckr4[:, :],
                in_offset=bass.IndirectOffsetOnAxis(ap=IDX[:, G:G + 1], axis=0)), [bIDX], [bKPR[sl]], dsem='kpr%d' % sl)
        def stageB(G):
            sl = G % 3
            for j in range(GP):
                u = (G * GP + j) % 8; pb = 3 + (u % 4)
                for ch in range(4):
                    K.tr(PS[pb][:, ch * 128:(ch + 1) * 128], KPV[sl][:, j, ch * 128:(ch + 1) * 128], identf[:, :], [bKPV[sl], b_identf], [bPS[pb]])
                K.cp('scalar', KTP[u][:, :, :], v4(PS[pb][:, :]), [bPS[pb]], [bKTP[u]])
                K.tr(PS[7][0:64, j * 128:(j + 1) * 128], KPR[sl][:, j, :], identf[:, :], [bKPR[sl], b_identf], [bPS[7]])
                uv = (G * GP + j) % 12
                K.cp('vector', VP[uv][:, :], KPV[sl][:, j, :], [bKPV[sl]], [bVP[uv]])
            K.cp('vector', KTR[G % 2][0:64, :], PS[7][0:64, :512], [bPS[7]], [bKTR[G % 2]])
        def stageC1(G):
            b = G // NG; g = G % NG; ps = PS[G % 2]; bps = bPS[G % 2]; pts = PTS[G % 2]; bpts = bPTS[G % 2]
            ACCS = ACCS2[b % 2]; bACCS = bACCS2[b % 2]
            for j in range(GP):
                u = (G * GP + j) % 8
                for ch in range(5):
                    if ch < 4:
                        K.mm(ps[:, j * 8:(j + 1) * 8], KTP[u][:, ch, :], qrhs(ch, 128, b), ch == 0, False, [bKTP[u], bQCAT], [bps])
                    else:
                        K.mm(ps[:, j * 8:(j + 1) * 8], KTR[G % 2][0:64, j * 128:(j + 1) * 128], qrhs(4, 64, b), False, True, [bKTR[G % 2], bQCAT], [bps])
            K.act(pts[:, :], ps[:, 0:GP * 8], AF.Exp, [bps], [bpts], scale=SM_SCALE)
            if g == 0:
                K.cp('gpsimd', ACCS[:, :], pts[:, :], [bpts], [bACCS])
            else:
                K.tt('gpsimd', ACCS[:, :], ACCS[:, :], pts[:, :], ALU.add, [bpts, bACCS], [bACCS])
            if g == NG - 1:
                for ch in range(5):
                    rows = 128 if ch < 4 else 64
                    K.mm(ps[:NS, 40:48], KTN[:rows, ch, :], qrhs(ch, rows, b), ch == 0, ch == 4, [bKT, bQCAT], [bps])
                K.act(PTN[:, :], ps[:NS, 40:48], AF.Exp, [bps], [bPTN], scale=SM_SCALE)
                K.ts('vector', PTN[:, :], PTN[:, :], identf[:NS, b:b + 1], ALU.mult, [bPTN, b_identf], [bPTN])
                K.tt('gpsimd', ACCS[:NS, 0:8], ACCS[:NS, 0:8], PTN[:, :], ALU.add, [bPTN, bACCS], [bACCS])
        def stageC2(G):
            b = G // NG; g = G % NG; ps = PS[G % 2]; bps = bPS[G % 2]; pts = PTS[G % 2]; bpts = bPTS[G % 2]
            ACCS = ACCS2[b % 2]; bACCS = bACCS2[b % 2]
            for j in range(GP):
                uv = (G * GP + j) % 12
                K.mm(PS[2][0:8, :512], pts[:, j * 8:(j + 1) * 8], VP[uv][:, :], g == 0 and j == 0, False, [bpts, bVP[uv]], [bPS[2]])
            if g == NG - 1:
                K.mm(PS[2][0:8, :512], PTN[:, :], VN[:NS, :], False, True, [bV, bPTN], [bPS[2]])
                for j in range(GP):
                    K.mm(ps[0:8, 56:57], ACCS[:, j * 8:(j + 1) * 8], ones_f[:, 0:1], j == 0, j == GP - 1, [b_ones, bACCS], [bps])
                K.recip(RINVS[:, :], ps[0:8, 56:57], [bps], [bRINVS])
                K.ts('vector', OS[:, :], PS[2][0:8, :512], RINVS[:, 0:1], ALU.mult, [bPS[2], bRINVS], [bOS])
                for cc in range(4):
                    K.tr(ps[:, 64 + cc * 8:64 + (cc + 1) * 8], OS[0:8, cc * 128:(cc + 1) * 128], identf[0:8, 0:8], [bOS, b_identf], [bps])
                K.cp('vector', OTS[:, :, :].rearrange("p c (h b) -> p c h b", b=NS)[:, :, :, b],
                     ps[:, 64:96].rearrange("p (c h) -> p c h", c=4), [bps], [bOTS])
        for G in range(NGT + 3):
            if G < NGT: stageA(G)
            if 1 <= G <= NGT: stageB(G - 1)
            if 2 <= G <= NGT + 1: stageC1(G - 2)
            if G >= 3: stageC2(G - 3)
        for h in range(NH):
            for cc in range(4):
                K.mm(PS[7][:, :NS], WUV[:, cc * 1024 + h * 128:cc * 1024 + (h + 1) * 128], OTS[:, cc, h * NS:(h + 1) * NS], cc == 0, cc == 3,
                     [bW, bOTS], [bPS[7]])
            K.cp('scalar', AST[:, h, :NS], PS[7][:, :NS], [bPS[7]], [bAST])
        K.dma('sync', mixT[1024:2048, 1024:1024 + NS].rearrange("(h p) n -> p h n", p=128), AST[:, :, :NS], 'ast', reads=[bAST])
        K.barrier(); K.emit()
    if stop <= 4: return nc, es, K, locals()

    with ExitStack() as es2:
        BIG = sb('BIG', [128, DC, 512], F32); bBIG = Buf('BIG')
        MIX = sb('MIX', [128, DC, 512], BF16); bMIX = Buf('MIX')
        WS = [sb('wo%d' % i, [128, D], BF16) for i in range(2)]; bWS = [Buf('wo%d' % i) for i in range(2)]
        SQ = [sb('sq%d' % i, [128, 512], BF16) for i in range(2)]; bSQ = [Buf('sq%d' % i) for i in range(2)]
        R1 = [sb('r1%d' % i, [128, 512], F32) for i in range(2)]; bR1 = [Buf('r1%d' % i) for i in range(2)]
        HC = [sb('hc%d' % i, [128, 512], F32) for i in range(2)]; bHC = [Buf('hc%d' % i) for i in range(2)]
        OC = [sb('oc%d' % i, [128, 512], F32) for i in range(2)]; bOC = [Buf('oc%d' % i) for i in range(2)]
        TMP = sb('tmp', [128, 512], F32); bTMP = Buf('tmp')
        for (scol0, N, dcol0) in ((0, 512, 0), (512, 512, 512), (2048, NS, 1024)):
            K.dma('sync', MIX[:, :, :N], mixT[:, dcol0:dcol0 + N].rearrange("(c p) n -> p c n", p=128), 'mix', writes=[bMIX])
            linear_to_big(MIX, bMIX, DC, s_wo, DC, WS, bWS, 'wo', BIG, bBIG, N, SQ, bSQ, [PS[4], PS[5]], [bPS[4], bPS[5]], PS[7], bPS[7])
            post_residual(BIG, bBIG, N, G_MIXPOST, 1.0, R1, bR1, PS[7], bPS[7], h1T, h2T, scol0, HC, bHC, OC, bOC, TMP, bTMP, dcol0=dcol0)
        K.barrier(); K.emit()
    if stop <= 5: return nc, es, K, locals()

    ffn_phase([(0, 512, None), (512, 512, (1024, NS))], h2T, h3T, G_F2PRE, G_F2POST, wsc['w2_gate'], wsc['w2_up'], wsc['w2_down'])

    with ExitStack() as es2:
        HB = [sb('hb%d' % i, [128, DC, 128], F32) for i in range(2)]; bHB = [Buf('hb%d' % i) for i in range(2)]
        YB = [sb('yb%d' % i, [128, D], F32) for i in range(2)]; bYB = [Buf('yb%d' % i) for i in range(2)]
        for blk in range(9):
            nt = 128 if blk < 8 else NS; i = blk % 2
            K.dma('sync', HB[i][:, :, :nt], h3T[:, blk * 128:blk * 128 + nt].rearrange("(c p) n -> p c n", p=128), 'hb%d' % i, writes=[bHB[i]])
            for g in range(4):
                pb = (blk * 4 + g) % 8
                for c4 in range(4):
                    c = g * 4 + c4
                    K.tr(PS[pb][:nt, c4 * 128:(c4 + 1) * 128], HB[i][:, c, :nt], identf[:, :], [bHB[i], b_identf], [bPS[pb]])
                K.cp('scalar' if g % 2 == 0 else 'vector', YB[i][:nt, g * 512:(g + 1) * 512], PS[pb][:nt, :512], [bPS[pb]], [bYB[i]])
            K.dma('sync', y_out[blk * 128:blk * 128 + nt, :], YB[i][:nt, :], 'yb%d' % i, reads=[bYB[i]])
        K.barrier(); K.op('sync', None); K.emit()
    return nc, es, K, locals()


def _rope_tables(pos):
    half = R // 2
    inv = (10000.0 ** (-np.arange(half, dtype=np.float32) / half)).astype(np.float32)
    ang = pos.astype(np.float32)[:, None] * inv[None, :]
    return np.cos(ang).astype(np.float32), np.sin(ang).astype(np.float32)


_CACHE = {}


def kernel(x_prompt, x_sample, cache_kv_latent, cache_k_rope, state_pool, page_table,
           g_ffn1_pre, w1_gate, w1_up, w1_down, g_ffn1_post,
           g_mix_pre, w_in, w_pool, pool_scale, g_q, w_uq, g_kv, w_uk, w_uv, w_out, g_mix_post,
           g_ffn2_pre, w2_gate, w2_up, w2_down, g_ffn2_post):
    global NPHYS
    f = lambda a: np.ascontiguousarray(np.asarray(a, dtype=np.float32))
    NPHYS = int(cache_kv_latent.shape[1])
    if 'nc' not in _CACHE:
        nc, es, K, L = build_program()
        es.close()
        _CACHE['nc'] = nc
    nc = _CACHE['nc']
    x_prompt = f(x_prompt); x_sample = f(x_sample)
    w_uq_ = f(w_uq)[0].reshape(QL, NH, 192)
    rot = np.concatenate([np.arange(32, 64), np.arange(0, 32)])
    def cols(v):
        v = f(v).reshape(-1)
        return v.reshape(-1, 128).T
    gvec = np.concatenate([cols(g_ffn1_pre), cols(g_ffn1_post), cols(g_mix_pre), cols(g_mix_post), cols(g_ffn2_pre), cols(g_ffn2_post),
                           cols(pool_scale), cols(g_q)], axis=1)
    shared = {
        'cache_kv': f(cache_kv_latent)[0].reshape(NPHYS * PAGE, KV), 'cache_kr': f(cache_k_rope)[0].reshape(NPHYS * PAGE, R),
        'w1_gate': f(w1_gate)[0], 'w1_up': f(w1_up)[0], 'w1_down': f(w1_down)[0],
        'w2_gate': f(w2_gate)[0], 'w2_up': f(w2_up)[0], 'w2_down': f(w2_down)[0],
        'w_in': f(w_in)[0], 'w_out': f(w_out)[0], 'w_pool': f(w_pool)[0].reshape(1024, 256),
        'wq_nope': np.ascontiguousarray(w_uq_[:, :, :128].reshape(QL, 1024)),
        'wq_rope': np.ascontiguousarray(w_uq_[:, :, 128:].reshape(QL, 512)),
        'wq_rot': np.ascontiguousarray(w_uq_[:, :, 128:][:, :, rot].reshape(QL, 512)),
        'wukT': np.ascontiguousarray(f(w_uk)[0].transpose(2, 1, 0).reshape(128, NH * KV)),
        'w_uv': f(w_uv)[0].reshape(KV, NH * 128),
        'gvec': np.ascontiguousarray(gvec), 'gkv_b': np.ascontiguousarray(np.broadcast_to(f(g_kv).reshape(1, KV), (128, KV))),
        'tri': np.triu(np.ones((128, 128), np.float32)), 'ident': np.eye(128, dtype=np.float32),
    }
    pt = np.asarray(page_table).astype(np.int32)
    sp = f(state_pool)[0]
    wins = (2, 4, 8, 16)
    in_maps = []
    for c in range(8):
        s_, r_ = c // 2, c % 2
        own = [2 * j + r_ for j in range(8)]
        partner = [2 * j + 1 - r_ for j in range(8)] if r_ == 1 else [-1] + [2 * j - 1 for j in range(1, 8)]
        xl = np.zeros((NLOC, D), np.float32); pos = np.zeros(NLOC, np.int64)
        for li, gbk in enumerate(own + partner):
            if gbk >= 0:
                xl[li * 128:(li + 1) * 128] = x_prompt[s_, gbk * 128:(gbk + 1) * 128]
                pos[li * 128:(li + 1) * 128] = np.arange(gbk * 128, (gbk + 1) * 128)
        xl[2048:] = x_sample[c * NS:(c + 1) * NS, 0]
        pos[2048:] = pt.shape[1] * PAGE
        cs, sn = _rope_tables(pos)
        ck = np.zeros((17 * 128, R), np.float32); sk = np.zeros((17 * 128, R), np.float32)
        ck[:NLOC] = np.concatenate([cs, cs], 1); sk[:NLOC] = np.concatenate([-sn, sn], 1)
        qsel = np.concatenate([np.arange(0, 1024), np.arange(2048, NLOC)])
        cq = np.zeros((128, 1024 + NS), np.float32); sq_ = np.zeros((128, 1024 + NS), np.float32)
        cq[:64] = np.concatenate([cs[qsel], cs[qsel]], 1).T; sq_[:64] = np.concatenate([-sn[qsel], sn[qsel]], 1).T
        ic = np.zeros((3, 4, 512), np.float32)
        for t in range(2):
            p_ = pos[t * 512:(t + 1) * 512]
            for g, w in enumerate(wins):
                ic[t, g] = 1.0 / np.minimum(p_ + 1, w)
        for g, w in enumerate(wins):
            ic[2, g] = 1.0 / w
        cst = np.zeros((128, 4), np.float32); cst[:, 0] = 1.0 if r_ == 1 else 0.0; cst[:, 1] = EPS; cst[:, 2] = np.arange(128) % 32
        m = dict(shared)
        m.update({'x_loc': xl, 'state_pool': np.ascontiguousarray(sp[c * NS:(c + 1) * NS]),
                  'ptab': np.ascontiguousarray(pt[c * NS:(c + 1) * NS].reshape(NS * NPG // 4, 4).T[np.arange(128) // 32]),
                  'cosk': np.ascontiguousarray(ck.reshape(17, 128, R).transpose(1, 0, 2)),
                  'sink': np.ascontiguousarray(sk.reshape(17, 128, R).transpose(1, 0, 2)),
                  'cosq': cq, 'sinq': sq_, 'invc': np.ascontiguousarray(np.broadcast_to(ic[None], (128, 3, 4, 512))), 'consts': cst})
        in_maps.append(m)
    res = run_bass_kernel_spmd(nc, in_maps, core_ids=list(range(8))).results
    B = 4
    y_p = np.zeros((B, SEQ, D), np.float32); y_s = np.zeros((B * 32, 1, D), np.float32)
    p_kv = np.zeros((1, B, SEQ, KV), np.float32); p_pe = np.zeros((1, B, SEQ, R), np.float32)
    p_pool = np.zeros((1, B, 15, DP), np.float32)
    s_kv = np.zeros((1, 128, 1, KV), np.float32); s_pe = np.zeros((1, 128, 1, R), np.float32); s_pool = np.zeros((1, 128, 15, DP), np.float32)
    for c in range(8):
        s_, r_ = c // 2, c % 2; o = res[c]
        for j in range(8):
            gbk = 2 * j + r_
            y_p[s_, gbk * 128:(gbk + 1) * 128] = o['y_out'][j * 128:(j + 1) * 128]
            p_kv[0, s_, gbk * 128:(gbk + 1) * 128] = o['kv_out'][j * 128:(j + 1) * 128]
            p_pe[0, s_, gbk * 128:(gbk + 1) * 128] = o['kpe_out'][j * 128:(j + 1) * 128]
        if r_ == 1:
            p_pool[0, s_] = o['pool_last'][1:16]
        sl = slice(c * NS, (c + 1) * NS)
        y_s[sl, 0] = o['y_out'][1024:1024 + NS]
        s_kv[0, sl, 0] = o['kv_out'][2048:2048 + NS]; s_pe[0, sl, 0] = o['kpe_out'][2048:2048 + NS]
        s_pool[0, sl, :14] = o['spool_hist']; s_pool[0, sl, 14] = o['spool_new']
    return (y_p, y_s, p_kv, p_pe, p_pool, s_kv, s_pe, s_pool)
```

```python
import numpy as np
import concourse.bass as bass
import concourse.mybir as mybir
from concourse.bass_utils import run_bass_kernel_spmd
from contextlib import ExitStack
import os

F32, BF16, I32 = mybir.dt.float32, mybir.dt.bfloat16, mybir.dt.int32
AF = mybir.ActivationFunctionType
ALU = mybir.AluOpType
AX = mybir.AxisListType

D = 2048; DC = 16; FF = 5632; FC = 44; DP = 1024; QL = 512; KV = 512; R = 64
NH = 8; SEQ = 2048; NS = 16; NPG = 64; PAGE = 128
NPHYS = 10240
NLOC = 2048 + NS
EPS = 1e-6
SM_SCALE = float((128 + 64) ** -0.5)
ENGS = ['tensor', 'scalar', 'vector', 'gpsimd', 'sync']


class Buf:
    def __init__(s, name):
        s.name = name; s.w = None; s.r = {}; s.rd = []


class Op:
    __slots__ = ('eng', 'fn', 'deps', 'signal', 'ev', 'dsem', 'n')

    def __init__(s, eng, fn, dsem):
        s.eng = eng; s.fn = fn; s.deps = []; s.signal = False; s.ev = None; s.dsem = dsem


class Ker:
    def __init__(s, nc, es):
        s.nc = nc; s.es = es; s.ops = []; s.nall = 0
        s.esem = {e: es.enter_context(nc.semaphore('e_' + e)) for e in ENGS}
        s.ecnt = {e: 0 for e in ENGS}
        s.dsem = {}; s.dcnt = {}
        s.waited = {e: {} for e in ENGS}
        s.fence = []; s.fenced = set(ENGS)
        s.last = {}; s.phase0 = 0

    def _sem(s, key):
        if key not in s.dsem:
            s.dsem[key] = s.es.enter_context(s.nc.semaphore('d_' + key)); s.dcnt[key] = 0
        return s.dsem[key]

    def op(s, eng, fn, reads=(), writes=(), dsem=None):
        o = Op(eng, fn, dsem); o.n = s.nall; s.nall += 1
        deps = {}
        def add(d):
            if d is None or d is o: return
            if eng == 'tensor' and d.eng == 'tensor' and d.dsem is None: return
            deps[id(d)] = d
        for b in reads:
            add(b.w)
        for b in writes:
            add(b.w)
            for d in b.r.values(): add(d)
            for d in b.rd: add(d)
        if eng not in s.fenced:
            for d in s.fence: add(d)
            s.fenced.add(eng)
        o.deps = list(deps.values())
        for b in writes:
            b.w = o; b.r = {}; b.rd = []
        for b in reads:
            if b in writes: continue
            if dsem is not None: b.rd.append(o)
            else: b.r[eng] = o
        s.ops.append(o); s.last[eng] = o
        if dsem is not None: s.last['dma_' + dsem] = o
        return o

    def barrier(s):
        s.fence = list(s.last.values()); s.fenced = set()
        for o in s.fence: o.signal = True

    def emit(s):
        for o in s.ops:
            for d in o.deps: d.signal = True
        for o in s.ops:
            if o.dsem is not None:
                s._sem(o.dsem); s.dcnt[o.dsem] += 16; o.ev = (s.dsem[o.dsem], s.dcnt[o.dsem])
            elif o.signal:
                s.ecnt[o.eng] += 1; o.ev = (s.esem[o.eng], s.ecnt[o.eng])
        with s.nc.Block() as blk:
            for eng in ENGS:
                ops_e = [o for o in s.ops if o.eng == eng]
                if not ops_e: continue
                def body(e, ops_e=ops_e, eng=eng):
                    wd = s.waited[eng]
                    for o in ops_e:
                        for d in o.deps:
                            if d.ev is None:
                                assert d.n < s.phase0, (d.eng, d.n)
                                continue
                            sem, val = d.ev
                            k = id(sem)
                            if wd.get(k, 0) < val:
                                e.wait_ge(sem, val); wd[k] = val
                        if o.fn is None: continue
                        ins = o.fn(e)
                        if o.dsem is not None: ins.then_inc(o.ev[0], 16)
                        elif o.signal: ins.then_inc(o.ev[0], 1)
                getattr(blk, eng)(body)
        s.ops = []; s.phase0 = s.nall

    def dma(s, q, out, in_, sem, reads=(), writes=()):
        return s.op(q, lambda e: e.dma_start(out=out, in_=in_), reads, writes, dsem=sem)

    def mm(s, out, lhsT, rhs, start, stop, reads, writes):
        return s.op('tensor', lambda e: e.matmul(out, lhsT, rhs, start=start, stop=stop), reads, writes)

    def tr(s, out, in_, ident, reads, writes):
        return s.op('tensor', lambda e: e.transpose(out, in_, ident), reads, writes)

    def act(s, out, in_, func, reads, writes, scale=None, bias=None, accum=None, eng='scalar'):
        kw = {}
        if scale is not None: kw['scale'] = scale
        if bias is not None: kw['bias'] = bias
        if accum is not None: kw['accum_out'] = accum
        return s.op('scalar', lambda e: e.activation(out, in_, func, **kw), reads, writes)

    def cp(s, eng, out, in_, reads, writes):
        if eng == 'scalar':
            return s.op(eng, lambda e: e.copy(out, in_), reads, writes)
        return s.op(eng, lambda e: e.tensor_copy(out, in_), reads, writes)

    def tt(s, eng, out, a, b, op, reads, writes):
        return s.op(eng, lambda e: e.tensor_tensor(out, a, b, op), reads, writes)

    def ts(s, eng, out, a, s1, op0, reads, writes, s2=None, op1=None):
        if op1 is None:
            return s.op(eng, lambda e: e.tensor_scalar(out, a, s1, None, op0), reads, writes)
        return s.op(eng, lambda e: e.tensor_scalar(out, a, s1, s2, op0, op1), reads, writes)

    def stt(s, out, a, sc, b, op0, op1, reads, writes):
        return s.op('vector', lambda e: e.scalar_tensor_tensor(out, a, sc, b, op0, op1), reads, writes)

    def recip(s, out, in_, reads, writes):
        return s.op('vector', lambda e: e.reciprocal(out, in_), reads, writes)


def eval_tiles(t):
    return [tuple(int(v) for v in x.split(':')) for x in t.split(',')]


def build_program(stop=99):
    nc = bass.Bass("TRN2", target_bir_lowering=False)
    es = ExitStack()
    def din(name, shape, dt=F32):
        return nc.dram_tensor(name, list(shape), dt, kind="ExternalInput").ap()
    def dout(name, shape, dt=F32):
        return nc.dram_tensor(name, list(shape), dt, kind="ExternalOutput").ap()
    def dscr(name, shape, dt):
        return nc.dram_tensor(name, list(shape), dt, kind="Internal").ap()

    x_loc = din('x_loc', [NLOC, D])
    cache_kv = din('cache_kv', [NPHYS * PAGE, KV])
    cache_kr = din('cache_kr', [NPHYS * PAGE, R])
    state_pool = din('state_pool', [NS, 15, DP])
    ptab = din('ptab', [128, NS * NPG // 4], I32)
    w_f32 = {}
    for nm in ('w1_gate', 'w1_up', 'w2_gate', 'w2_up'):
        w_f32[nm] = din(nm, [D, FF])
    for nm in ('w1_down', 'w2_down'):
        w_f32[nm] = din(nm, [FF, D])
    w_in = din('w_in', [D, 2112]); w_out = din('w_out', [D, D])
    w_pool = din('w_pool', [4 * 256, 256])
    wq_nope = din('wq_nope', [QL, 1024]); wq_rope = din('wq_rope', [QL, 512]); wq_rot = din('wq_rot', [QL, 512])
    wukT = din('wukT', [128, NH * KV]); w_uv = din('w_uv', [KV, NH * 128])
    gvec = din('gvec', [128, 6 * DC + 8 + 4])
    gkv_b = din('gkv_b', [128, KV])
    cosk = din('cosk', [128, 17, R]); sink = din('sink', [128, 17, R])
    cosq = din('cosq', [128, 1024 + NS]); sinq = din('sinq', [128, 1024 + NS])
    invc = din('invc', [128, 3, 4, 512])
    consts = din('consts', [128, 4])
    tri_in = din('tri', [128, 128])
    ident_in = din('ident', [128, 128])

    y_out = dout('y_out', [1024 + NS, D])
    kv_out = dout('kv_out', [NLOC, KV]); kpe_out = dout('kpe_out', [NLOC, R])
    pool_last = dout('pool_last', [16, DP]); spool_new = dout('spool_new', [16, DP])
    spool_hist = dout('spool_hist', [NS, 14, DP])

    wsc = {}
    for nm in ('w1_gate', 'w1_up', 'w2_gate', 'w2_up'):
        wsc[nm] = dscr('s_' + nm, [FC, 128, D], BF16)
    for nm in ('w1_down', 'w2_down'):
        wsc[nm] = dscr('s_' + nm, [DC, 128, FF], BF16)
    s_winfm = dscr('s_winfm', [12, 128, D], BF16)
    s_winkv = dscr('s_winkv', [128, DC * 576], BF16)
    s_wo = dscr('s_wo', [DC, 128, D], BF16)
    s_small = {k: dscr('s_' + k, [128, n], BF16) for k, n in
               (('wqn', 4096), ('wqr', 2048), ('wqx', 2048), ('wuk', 4096), ('wuv', 4096), ('wpool', 2048))}
    xT = dscr('xT', [D, NLOC], F32); h1T = dscr('h1T', [D, NLOC], F32)
    h2T = dscr('h2T', [D, 1024 + NS], F32); h3T = dscr('h3T', [D, 1024 + NS], F32)
    cqn_s = dscr('cqn_s', [QL, 1024 + NS], F32)
    mixT = dscr('mixT', [D, 1024 + NS], BF16)
    KT_s = dscr('KT_s', [640, NLOC], BF16)
    V_s = dscr('V_s', [NLOC, KV], BF16)

    K = Ker(nc, es)
    _cnt = [0]
    def sb(name, shape, dt):
        _cnt[0] += 1
        return es2.enter_context(nc.sbuf_tensor('%s_%d' % (name, _cnt[0]), list(shape), dt))
    PS = [es.enter_context(nc.psum_tensor('ps%d' % i, [128, 512], F32)) for i in range(8)]
    bPS = [Buf('ps%d' % i) for i in range(8)]
    identf = es.enter_context(nc.sbuf_tensor('identf', [128, 128], F32)); b_identf = Buf('identf')
    ones_bf = es.enter_context(nc.sbuf_tensor('ones_bf', [128, 128], BF16)); b_ones = Buf('ones')
    ones_f = es.enter_context(nc.sbuf_tensor('ones_f', [128, 128], F32))
    gv = es.enter_context(nc.sbuf_tensor('gv', [128, 6 * DC + 12], F32)); b_gv = Buf('gv')
    cst = es.enter_context(nc.sbuf_tensor('cst', [128, 4], F32)); b_cst = Buf('cst')
    K.dma('sync', identf[:, :], ident_in[:, :], 'c_id', writes=[b_identf])
    K.dma('sync', gv[:, :], gvec[:, :], 'c_gv', writes=[b_gv])
    K.dma('sync', cst[:, :], consts[:, :], 'c_cst', writes=[b_cst])
    K.op('vector', lambda e: e.memset(ones_bf[:, :], 1.0), writes=[b_ones])
    K.op('vector', lambda e: e.memset(ones_f[:, :], 1.0), writes=[b_ones])
    epsb = cst[:, 1:2]
    G_F1PRE, G_F1POST, G_MIXPRE, G_MIXPOST, G_F2PRE, G_F2POST = [i * DC for i in range(6)]
    G_PSC = 6 * DC; G_Q = 6 * DC + 8

    with ExitStack() as es2:
        NSL = 6
        S = [sb('castS%d' % i, [128, 4096], F32) for i in range(NSL)]
        T = [sb('castT%d' % i, [128, 4096], BF16) for i in range(NSL)]
        bS = [Buf('cS%d' % i) for i in range(NSL)]; bT = [Buf('cT%d' % i) for i in range(NSL)]
        jobs = []
        def job_perm(src, col0, kc_n, row0, dst3, ncols=256, lst=None):
            jn = ncols // 128
            sv = src[row0:row0 + kc_n * 128, col0:col0 + ncols].rearrange("(k p) c -> p k c", p=128)
            (jobs if lst is None else lst).append((sv, kc_n * ncols,
                         lambda Si, n=kc_n, jn=jn: Si[:, :n * jn * 128].rearrange("p (k j f) -> p j k f", k=n, j=jn),
                         lambda Ti, n=kc_n, jn=jn: Ti[:, :n * jn * 128].rearrange("p (j k f) -> p j k f", j=jn, k=n),
                         lambda Si, n=kc_n, jn=jn: Si[:, :n * jn * 128].rearrange("p (k c) -> p k c", k=n),
                         lambda Ti, n=kc_n, jn=jn: Ti[:, :n * jn * 128].rearrange("p (j x) -> p j x", j=jn),
                         dst3))
        def job_plain(sv, a, b, dst):
            jobs.append((sv, a * b, lambda Si: Si[:, :a * b], lambda Ti: Ti[:, :a * b],
                         lambda Si: Si[:, :a * b].rearrange("p (a b) -> p a b", a=a),
                         lambda Ti: Ti[:, :a * b].rearrange("p (a b) -> p a b", a=a), dst))
        def jobs_gate(src, dst, npair, c0=0):
            for jj in range(npair):
                job_perm(src, c0 + jj * 256, DC, 0, dst[2 * jj:2 * jj + 2, :, :].rearrange("j p x -> p j x"))
        def jobs_down(src, dst):
            for mm_ in range(8):
                for q in range(4):
                    job_perm(src, mm_ * 256, 11, q * 1408,
                             dst[2 * mm_:2 * mm_ + 2, :, q * 1408:(q + 1) * 1408].rearrange("m p x -> p m x"))
        jobs_gate(w_f32['w1_gate'], wsc['w1_gate'], 22); jobs_gate(w_f32['w1_up'], wsc['w1_up'], 22)
        jobs_down(w_f32['w1_down'], wsc['w1_down'])
        jobs_gate(w_in, s_winfm, 6)
        for k0, kn in ((0, 6), (6, 6), (12, 4)):
            job_plain(w_in[k0 * 128:(k0 + kn) * 128, 1536:2112].rearrange("(k p) c -> p k c", p=128), kn, 576,
                      s_winkv[:, k0 * 576:(k0 + kn) * 576].rearrange("p (a b) -> p a b", a=kn))
        job_plain(wq_nope.rearrange("(k p) c -> p k c", p=128), 4, 1024, s_small['wqn'].rearrange("p (a b) -> p a b", a=4))
        job_plain(wq_rope.rearrange("(k p) c -> p k c", p=128), 4, 512, s_small['wqr'].rearrange("p (a b) -> p a b", a=4))
        job_plain(wq_rot.rearrange("(k p) c -> p k c", p=128), 4, 512, s_small['wqx'].rearrange("p (a b) -> p a b", a=4))
        job_plain(wukT.rearrange("p (a b) -> p a b", a=8), 8, 512, s_small['wuk'].rearrange("p (a b) -> p a b", a=8))
        job_plain(w_uv.rearrange("(k p) c -> p k c", p=128), 4, 1024, s_small['wuv'].rearrange("p (a b) -> p a b", a=4))
        job_plain(w_pool.rearrange("(k p) c -> p k c", p=128), 8, 256, s_small['wpool'].rearrange("p (a b) -> p a b", a=8))
        jobs_bg = []
        def bg_gate(src, dst, npair):
            for jj in range(npair):
                for hf in range(2):
                    job_perm(src, jj * 256, 8, hf * 1024, dst[2 * jj:2 * jj + 2, :, hf * 1024:(hf + 1) * 1024].rearrange("j p x -> p j x"), lst=jobs_bg)
        def bg_down(src, dst):
            for m_ in range(DC):
                for q in range(4):
                    job_perm(src, m_ * 128, 11, q * 1408, dst[m_:m_ + 1, :, q * 1408:(q + 1) * 1408].rearrange("j p x -> p j x"), ncols=128, lst=jobs_bg)
        bg_gate(w_out, s_wo, 8)
        bg_gate(w_f32['w2_gate'], wsc['w2_gate'], 22); bg_gate(w_f32['w2_up'], wsc['w2_up'], 22)
        bg_down(w_f32['w2_down'], wsc['w2_down'])
        PF = 4
        ceng = ['vector', 'scalar', 'vector', 'scalar', 'gpsimd']
        for n in range(len(jobs) + PF):
            if n < len(jobs):
                sv, ne, civ, cov, siv, sov, dst = jobs[n]; i = n % NSL
                K.dma('sync', siv(S[i]), sv, 'cS%d' % i, writes=[bS[i]])
            m = n - PF
            if m >= 0:
                sv, ne, civ, cov, siv, sov, dst = jobs[m]; i = m % NSL
                K.cp(ceng[m % 5], cov(T[i]), civ(S[i]), [bS[i]], [bT[i]])
                K.dma('sync', dst, sov(T[i]), 'cT%d' % i, reads=[bT[i]])
        K.barrier(); K.emit()
    if stop <= 0: return nc, es, K, locals()

    with ExitStack() as es2:
        XB = [sb('xb%d' % i, [128, D], F32) for i in range(2)]; bXB = [Buf('xb%d' % i) for i in range(2)]
        XS = [sb('xs%d' % i, [128, DC, 128], F32) for i in range(2)]; bXS = [Buf('xs%d' % i) for i in range(2)]
        for blk in range(17):
            nt = 128 if blk < 16 else NS
            i = blk % 2
            K.dma('sync', XB[i][:nt, :], x_loc[blk * 128:blk * 128 + nt, :], 'xb%d' % i, writes=[bXB[i]])
            for g in range(4):
                pb = (blk * 4 + g) % 8
                for c4 in range(4):
                    c = g * 4 + c4
                    K.tr(PS[pb][:, c4 * 128:c4 * 128 + nt], XB[i][:nt, c * 128:(c + 1) * 128], identf[:nt, :nt],
                         [bXB[i], b_identf], [bPS[pb]])
                K.cp('scalar' if g % 2 == 0 else 'vector', XS[i][:, g * 4:(g + 1) * 4, :nt],
                     PS[pb][:, :].rearrange("p (c t) -> p c t", c=4)[:, :, :nt], [bPS[pb]], [bXS[i]])
            K.dma('sync', xT[:, blk * 128:blk * 128 + nt].rearrange("(c p) n -> p c n", p=128), XS[i][:, :, :nt],
                  'xs%d' % i, reads=[bXS[i]])
        K.barrier(); K.emit()
    if stop <= 1: return nc, es, K, locals()

    def prenorm(BIG, bBIG, XN, bXN, N, gcol, SQ, bSQ, R1, bR1, ps_ss, b_ss, nchunk=DC, dim=D):
        for c in range(nchunk):
            i = c % 2
            K.act(SQ[i][:, :N], BIG[:, c, :N], AF.Square, [bBIG], [bSQ[i]])
            K.mm(ps_ss[:, :N], ones_bf[:, :], SQ[i][:, :N], c == 0, c == nchunk - 1, [bSQ[i], b_ones], [b_ss])
        K.act(R1[0][:, :N], ps_ss[:, :N], AF.Sqrt, [b_ss, b_cst], [bR1[0]], scale=1.0 / dim, bias=epsb)
        K.recip(R1[1][:, :N], R1[0][:, :N], [bR1[0]], [bR1[1]])
        for c in range(nchunk):
            K.stt(XN[:, c, :N], BIG[:, c, :N], gv[:, gcol + c:gcol + c + 1], R1[1][:, :N], ALU.mult, ALU.mult,
                  [bBIG, bR1[1], b_gv], [bXN])

    def linear_to_big(inT, b_in, KC, wscr, M, WS, bWS, wkey, BIG, bBIG, N, SQ, bSQ, ps_y, b_y, ps_ss, b_ss):
        pend = None
        for m in range(M):
            i = m % 2
            K.dma('sync', WS[i][:, :KC * 128], wscr[m, :, :], wkey + str(i), writes=[bWS[i]])
            for kc in range(KC):
                K.mm(ps_y[i][:, :N], WS[i][:, kc * 128:(kc + 1) * 128], inT[:, kc, :N], kc == 0, kc == KC - 1,
                     [bWS[i], b_in], [b_y[i]])
            if pend is not None and not os.environ.get('DBG_NOPEND'):
                K.mm(*pend[0], **pend[1])
            K.cp('vector', BIG[:, m, :N], ps_y[i][:, :N], [b_y[i]], [bBIG])
            K.act(SQ[i][:, :N], BIG[:, m, :N], AF.Square, [bBIG], [bSQ[i]])
            pend = ((ps_ss[:, :N], ones_bf[:, :], SQ[i][:, :N], m == 0, m == M - 1), dict(reads=[bSQ[i], b_ones], writes=[b_ss]))
        if not os.environ.get('DBG_NOPEND'): K.mm(*pend[0], **pend[1])

    def post_residual(BIG, bBIG, N, gcol, alpha, R1, bR1, ps_ss, b_ss, src, dst, col0, HC, bHC, OC, bOC, TMP, bTMP, dcol0=None):
        if dcol0 is None: dcol0 = col0
        K.act(R1[0][:, :N], ps_ss[:, :N], AF.Sqrt, [b_ss, b_cst], [bR1[0]], scale=1.0 / D, bias=epsb)
        K.recip(R1[1][:, :N], R1[0][:, :N], [bR1[0]], [bR1[1]])
        for c in range(DC):
            i = c % 2
            K.dma('sync', HC[i][:, :N], src[c * 128:(c + 1) * 128, col0:col0 + N], 'hc%d' % i, writes=[bHC[i]])
            K.stt(TMP[:, :N], BIG[:, c, :N], gv[:, gcol + c:gcol + c + 1], R1[1][:, :N], ALU.mult, ALU.mult,
                  [bBIG, bR1[1], b_gv], [bTMP])
            K.stt(OC[i][:, :N], TMP[:, :N], float(alpha), HC[i][:, :N], ALU.mult, ALU.add, [bTMP, bHC[i]], [bOC[i]])
            K.dma('sync', dst[c * 128:(c + 1) * 128, dcol0:dcol0 + N], OC[i][:, :N], 'oc%d' % i, reads=[bOC[i]])

    def ffn_phase(tiles, src, dst, gpre, gpost, wg, wu, wd, bg=None):
        with ExitStack() as es2_:
            nonlocal es2
            es2 = es2_
            WT = 512 + NS
            BIG = sb('BIG', [128, DC, WT], F32); bBIG = Buf('BIG')
            XN = sb('XN', [128, DC, WT], BF16); bXN = Buf('XN')
            AT = sb('AT', [128, FC, WT], BF16); bAT = Buf('AT')
            WGU = [sb('wgu%d' % i, [128, 2, D], BF16) for i in range(4)]; bWGU = [Buf('wgu%d' % i) for i in range(4)]
            WD = [sb('wd%d' % i, [128, FF], BF16) for i in range(2)]; bWD = [Buf('wd%d' % i) for i in range(2)]
            SQ = [sb('sq%d' % i, [128, WT], BF16) for i in range(2)]; bSQ = [Buf('sq%d' % i) for i in range(2)]
            R1 = [sb('r1%d' % i, [128, WT], F32) for i in range(2)]; bR1 = [Buf('r1%d' % i) for i in range(2)]
            SG = [sb('sg%d' % i, [128, WT], F32) for i in range(2)]; bSG = [Buf('sg%d' % i) for i in range(2)]
            HC = [sb('hc%d' % i, [128, WT], F32) for i in range(4)]; bHC = [Buf('hc%d' % i) for i in range(4)]
            OC = [sb('oc%d' % i, [128, WT], F32) for i in range(2)]; bOC = [Buf('oc%d' % i) for i in range(2)]
            TMP2 = [sb('tmp%d' % i, [128, WT], F32) for i in range(1)] * 2; bTMP2 = [Buf('tmp0')] * 2
            SQB = [sb('sqb%d' % i, [128, WT], BF16) for i in range(2)]; bSQB = [Buf('sqb%d' % i) for i in range(2)]
            R1B = [sb('r1b%d' % i, [128, WT], F32) for i in range(2)]; bR1B = [Buf('r1b%d' % i) for i in range(2)]
            if bg:
                BS = [sb('bgS%d' % i, [128, 2048], F32) for i in range(2)]; bBS = [Buf('bgS%d' % i) for i in range(2)]
                BT = [sb('bgT%d' % i, [128, 2048], BF16) for i in range(1)]; bBT = [Buf('bgT%d' % i) for i in range(1)]
            bgn = [0]
            def bg_in(n):
                if n < len(bg):
                    sv, ne, civ, cov, siv, sov, dd = bg[n]
                    K.dma('gpsimd', siv(BS[n % 2]), sv, 'bgS%d' % (n % 2), writes=[bBS[n % 2]])
            def bg_step():
                n = bgn[0]; bgn[0] += 1
                if not bg or n > len(bg): return
                if n == 0:
                    bg_in(0); bg_in(1); return
                sv, ne, civ, cov, siv, sov, dd = bg[n - 1]; i = (n - 1) % 2
                K.cp('scalar', cov(BT[0]), civ(BS[i]), [bBS[i]], [bBT[0]])
                K.dma('gpsimd', dd, sov(BT[0]), 'bgT0', reads=[bBT[0]])
                bg_in(n + 1)
            def wgu_load(j):
                ws = j % 4
                K.dma('sync', WGU[ws][:, 0, :], wg[j, :, :], 'wgu%d' % ws, writes=[bWGU[ws]])
                K.dma('sync', WGU[ws][:, 1, :], wu[j, :, :], 'wgu%d' % ws, writes=[bWGU[ws]])
            def ld_chunk(tl, c):
                (c0_, N_, ex_) = tl; h = c % 4
                K.dma('sync', HC[h][:, :N_], src[c * 128:(c + 1) * 128, c0_:c0_ + N_], 'hc%d' % h, writes=[bHC[h]])
                if ex_:
                    K.dma('sync', HC[h][:, N_:N_ + ex_[1]], src[c * 128:(c + 1) * 128, ex_[0]:ex_[0] + ex_[1]], 'hc%d' % h, writes=[bHC[h]])
            for ti, (col0, N, ex) in enumerate(tiles):
                EN = ex[1] if ex else 0; W = N + EN
                nxt = tiles[ti + 1] if ti + 1 < len(tiles) else None
                if ti == 0:
                    K.dma('sync', BIG[:, :, :N], src[:, col0:col0 + N].rearrange("(c p) n -> p c n", p=128), 'big', writes=[bBIG])
                    if ex:
                        K.dma('sync', BIG[:, :, N:W], src[:, ex[0]:ex[0] + EN].rearrange("(c p) n -> p c n", p=128), 'big', writes=[bBIG])
                    for c in range(DC):
                        i = c % 2
                        K.act(SQ[i][:, :W], BIG[:, c, :W], AF.Square, [bBIG], [bSQ[i]])
                        K.mm(PS[6][:, :N], ones_bf[:, :], SQ[i][:, :N], c == 0, c == DC - 1, [bSQ[i], b_ones], [bPS[6]])
                        if ex:
                            K.mm(PS[7][:, :EN], ones_bf[:, :], SQ[i][:, N:W], c == 0, c == DC - 1, [bSQ[i], b_ones], [bPS[7]])
                    K.act(R1[0][:, :N], PS[6][:, :N], AF.Sqrt, [bPS[6], b_cst], [bR1[0]], scale=1.0 / D, bias=epsb)
                    if ex:
                        K.act(R1[0][:, N:W], PS[7][:, :EN], AF.Sqrt, [bPS[7], b_cst], [bR1[0]], scale=1.0 / D, bias=epsb)
                    K.recip(R1[1][:, :W], R1[0][:, :W], [bR1[0]], [bR1[1]])
                    for c in range(DC):
                        K.stt(XN[:, c, :W], BIG[:, c, :W], gv[:, gpre + c:gpre + c + 1], R1[1][:, :W], ALU.mult, ALU.mult,
                              [bBIG, bR1[1], b_gv], [bXN])
                    for j in range(4): wgu_load(j)
                for j in range(FC):
                    i = j % 2
                    ws = j % 4
                    for which in range(2):
                        pb = 2 * i + which
                        for kc in range(DC):
                            K.mm(PS[pb][:, :N], WGU[ws][:, which, kc * 128:(kc + 1) * 128], XN[:, kc, :N], kc == 0, kc == DC - 1,
                                 [bWGU[ws], bXN], [bPS[pb]])
                        if ex:
                            for kc in range(DC):
                                K.mm(PS[6 + i][:, which * 32:which * 32 + EN], WGU[ws][:, which, kc * 128:(kc + 1) * 128], XN[:, kc, N:W],
                                     kc == 0, kc == DC - 1, [bWGU[ws], bXN], [bPS[6 + i]])
                    K.act(SG[i][:, :N], PS[2 * i][:, :N], AF.Silu, [bPS[2 * i]], [bSG[i]])
                    K.tt('vector', AT[:, j, :N], SG[i][:, :N], PS[2 * i + 1][:, :N], ALU.mult, [bSG[i], bPS[2 * i + 1]], [bAT])
                    bg_step()
                    if j + 4 < FC: wgu_load(j + 4)
                    if ex:
                        K.act(SG[i][:, N:W], PS[6 + i][:, 0:EN], AF.Silu, [bPS[6 + i]], [bSG[i]])
                        K.tt('vector', AT[:, j, N:W], SG[i][:, N:W], PS[6 + i][:, 32:32 + EN], ALU.mult, [bSG[i], bPS[6 + i]], [bAT])
                pend = []
                for m in range(DC):
                    i = m % 2
                    K.dma('sync', WD[i][:, :], wd[m, :, :], 'wd%d' % i, writes=[bWD[i]])
                    for kc in range(FC):
                        K.mm(PS[4 + i][:, :N], WD[i][:, kc * 128:(kc + 1) * 128], AT[:, kc, :N], kc == 0, kc == FC - 1, [bWD[i], bAT], [bPS[4 + i]])
                    if ex:
                        for kc in range(FC):
                            K.mm(PS[i][:, :EN], WD[i][:, kc * 128:(kc + 1) * 128], AT[:, kc, N:W], kc == 0, kc == FC - 1, [bWD[i], bAT], [bPS[i]])
                    for p_ in pend: K.mm(*p_[0], **p_[1])
                    K.cp('vector', BIG[:, m, :N], PS[4 + i][:, :N], [bPS[4 + i]], [bBIG])
                    if ex:
                        K.cp('vector', BIG[:, m, N:W], PS[i][:, :EN], [bPS[i]], [bBIG])
                    K.act(SQ[i][:, :W], BIG[:, m, :W], AF.Square, [bBIG], [bSQ[i]])
                    pend = [((PS[7][:, :N], ones_bf[:, :], SQ[i][:, :N], m == 0, m == DC - 1), dict(reads=[bSQ[i], b_ones], writes=[bPS[7]]))]
                    if ex:
                        pend.append(((PS[2][:, :EN], ones_bf[:, :], SQ[i][:, N:W], m == 0, m == DC - 1), dict(reads=[bSQ[i], b_ones], writes=[bPS[2]])))
                    if nxt:
                        (nc0, nN, nex) = nxt; nEN = nex[1] if nex else 0; nW = nN + nEN
                        ld_chunk(nxt, m)
                        K.act(SQB[i][:, :nW], HC[m % 4][:, :nW], AF.Square, [bHC[m % 4]], [bSQB[i]])
                        pend.append(((PS[6][:, :nN], ones_bf[:, :], SQB[i][:, :nN], m == 0, m == DC - 1), dict(reads=[bSQB[i], b_ones], writes=[bPS[6]])))
                        if nex:
                            pend.append(((PS[3][:, :nEN], ones_bf[:, :], SQB[i][:, nN:nW], m == 0, m == DC - 1), dict(reads=[bSQB[i], b_ones], writes=[bPS[3]])))
                for p_ in pend: K.mm(*p_[0], **p_[1])
                if nxt:
                    K.act(R1B[0][:, :nN], PS[6][:, :nN], AF.Sqrt, [bPS[6], b_cst], [bR1B[0]], scale=1.0 / D, bias=epsb)
                    if nex:
                        K.act(R1B[0][:, nN:nW], PS[3][:, :nEN], AF.Sqrt, [bPS[3], b_cst], [bR1B[0]], scale=1.0 / D, bias=epsb)
                    K.recip(R1B[1][:, :nW], R1B[0][:, :nW], [bR1B[0]], [bR1B[1]])
                    ld_chunk(nxt, 0); ld_chunk(nxt, 1)
                    for c in range(DC):
                        if c + 2 < DC: ld_chunk(nxt, c + 2)
                        K.stt(XN[:, c, :nW], HC[c % 4][:, :nW], gv[:, gpre + c:gpre + c + 1], R1B[1][:, :nW], ALU.mult, ALU.mult,
                              [bHC[c % 4], bR1B[1], b_gv], [bXN])
                    for j in range(4): wgu_load(j)
                K.act(R1[0][:, :N], PS[7][:, :N], AF.Sqrt, [bPS[7], b_cst], [bR1[0]], scale=1.0 / D, bias=epsb)
                if ex:
                    K.act(R1[0][:, N:W], PS[2][:, :EN], AF.Sqrt, [bPS[2], b_cst], [bR1[0]], scale=1.0 / D, bias=epsb)
                K.recip(R1[1][:, :W], R1[0][:, :W], [bR1[0]], [bR1[1]])
                def ld_hc(c):
                    h = c % 4
                    K.dma('sync', HC[h][:, :N], src[c * 128:(c + 1) * 128, col0:col0 + N], 'hc%d' % h, writes=[bHC[h]])
                    if ex:
                        K.dma('sync', HC[h][:, N:W], src[c * 128:(c + 1) * 128, ex[0]:ex[0] + EN], 'hc%d' % h, writes=[bHC[h]])
                ld_hc(0); ld_hc(1)
                for c in range(DC):
                    i = c % 2; h = c % 4; TMP = TMP2[i]; bTMP = bTMP2[i]
                    if c + 2 < DC: ld_hc(c + 2)
                    K.stt(TMP[:, :W], BIG[:, c, :W], gv[:, gpost + c:gpost + c + 1], R1[1][:, :W], ALU.mult, ALU.mult,
                          [bBIG, bR1[1], b_gv], [bTMP])
                    K.stt(OC[i][:, :W], TMP[:, :W], 0.5, HC[h][:, :W], ALU.mult, ALU.add, [bTMP, bHC[h]], [bOC[i]])
                    K.dma('sync', dst[c * 128:(c + 1) * 128, col0:col0 + N], OC[i][:, :N], 'oc%d' % i, reads=[bOC[i]])
                    if ex:
                        K.dma('sync', dst[c * 128:(c + 1) * 128, ex[0]:ex[0] + EN], OC[i][:, N:W], 'oc%d' % i, reads=[bOC[i]])
            while bg and bgn[0] <= len(bg): bg_step()
            K.barrier(); K.emit()

    es2 = None
    TILES_ALL = [(1024, 512, None), (1536, 512, None), (0, 512, None), (512, 512, (2048, NS))]
    ffn_phase(TILES_ALL, xT, h1T, G_F1PRE, G_F1POST, wsc['w1_gate'], wsc['w1_up'], wsc['w1_down'], bg=jobs_bg)
    if stop <= 2: return nc, es, K, locals()

    def v4(ap):
        return ap.rearrange("p (b t) -> p b t", b=4)

    with ExitStack() as es2:
        BIG = sb('BIG', [128, DC, 512], F32); bBIG = Buf('BIG')
        HN = sb('HN', [128, DC, 512], BF16); bHN = Buf('HN')
        WKV = sb('WKV', [128, DC, 576], BF16); bWKV = Buf('WKV')
        WFM = [sb('wfm%d' % i, [128, D], BF16) for i in range(2)]; bWFM = [Buf('wfm%d' % i) for i in range(2)]
        WPL = sb('WPL', [128, 2048], BF16); bWPL = Buf('WPL')
        UE = [sb('ue%d' % i, [128, 4, 143], F32) for i in range(2)]; bUE = [Buf('ue%d' % i) for i in range(2)]
        PA = sb('PA', [128, 4, 143], F32); bPA = Buf('PA'); PB = sb('PB', [128, 4, 143], F32); bPB = Buf('PB')
        DT = sb('DT', [128, 8, 512], BF16); bDT = Buf('DT')
        CQ = sb('CQ', [128, 4, 512], F32); bCQ = Buf('CQ')
        CRAW = sb('CRAW', [128, 512], F32); bCRAW = Buf('CRAW'); JUNK = sb('JUNK', [128, 512], F32); bJUNK = Buf('JUNK')
        CKV = [sb('ckv%d' % i, [128, 512], F32) for i in range(2)]; bCKV = [Buf('ckv%d' % i) for i in range(2)]
        VST = [sb('vst%d' % i, [128, 512], BF16) for i in range(2)]; bVST = [Buf('vst%d' % i) for i in range(2)]
        KPE = [sb('kpe%d' % i, [128, 64], F32) for i in range(2)]; bKPE = [Buf('kpe%d' % i) for i in range(2)]
        KR = sb('KR', [128, 256], F32); bKR = Buf('KR')
        SSK = sb('SSK', [128, 4], F32); bSS = Buf('SSK')
        KTST = sb('KTST', [128, 5, 512], BF16); bKTST = Buf('KTST')
        COSK = sb('COSK', [128, 17, 64], F32); SINK = sb('SINK', [128, 17, 64], F32); GKVB = sb('GKVB', [128, 512], F32)
        bTAB = Buf('TAB')
        INVC = sb('INVC', [128, 4, 512], F32); bINVC = Buf('INVC')
        UTAIL = sb('UTAIL', [128, 8, 8, 15], F32); bUTAIL = Buf('UTAIL')
        ULAST = sb('ULAST', [128, 8, 16], F32); bULAST = Buf('ULAST')
        PLS = sb('PLS', [16, 1024], F32); bPLS = Buf('PLS')
        POS = [sb('pos%d' % i, [128, 512], BF16) for i in range(2)]; bPOS = [Buf('pos%d' % i) for i in range(2)]
        SQ = [sb('sq%d' % i, [128, 512], BF16) for i in range(2)]; bSQ = [Buf('sq%d' % i) for i in range(2)]
        R1 = [sb('r1%d' % i, [128, 512], F32) for i in range(2)]; bR1 = [Buf('r1%d' % i) for i in range(2)]
        HS = [sb('hs%d' % i, [120, 1024], F32) for i in range(2)]; bHS = [Buf('hs%d' % i) for i in range(2)]
        UEXS = sb('UEXS', [128, 8, 16, 16], F32); bUEXS = Buf('UEXS')
        SWS = sb('SWS', [128, 16], F32); bSWS = Buf('SWS')
        K.dma('sync', WKV[:, :, :], s_winkv.rearrange("p (a b) -> p a b", a=DC), 'wkv', writes=[bWKV])
        K.dma('sync', WPL[:, :], s_small['wpool'][:, :], 'wpl', writes=[bWPL])
        K.dma('sync', COSK[:, :, :], cosk[:, :, :], 'tab', writes=[bTAB])
        K.dma('sync', SINK[:, :, :], sink[:, :, :], 'tab', writes=[bTAB])
        K.dma('sync', GKVB[:, :], gkv_b[:, :], 'tab', writes=[bTAB])
        for h in range(2):
            K.dma('sync', HS[h][:, :], state_pool[8 * h:8 * h + 8, :, :].rearrange("b r c -> (b r) c"), 'hs%d' % h, writes=[bHS[h]])
            for m in range(8):
                pb = 3 + (m % 2)
                K.tr(PS[pb][:, 0:120], HS[h][:120, m * 128:(m + 1) * 128], identf[:120, :120], [bHS[h], b_identf], [bPS[pb]])
                K.cp('scalar', UEXS[:, m, 8 * h:8 * h + 8, 0:15], PS[pb][:, 0:120].rearrange("p (b r) -> p b r", b=8),
                     [bPS[pb]], [bUEXS])
        K.dma('sync', spool_hist[:, :, :], state_pool[:, 1:15, :], 'sph')
        P2T = [(1024, 512, 'partner', 0), (1536, 512, 'partner', 1), (0, 512, 'own', 0), (512, 512, 'own', 1), (2048, NS, 'sample', 0)]
        for (col0, N, kind, tb) in P2T:
            nblk = max(1, N // 128); nt = min(N, 128)
            ocol0 = 1024 if kind == 'sample' else tb * 512
            K.dma('sync', BIG[:, :, :N], h1T[:, col0:col0 + N].rearrange("(c p) n -> p c n", p=128), 'big', writes=[bBIG])
            prenorm(BIG, bBIG, HN, bHN, N, G_MIXPRE, SQ, bSQ, R1, bR1, PS[6], bPS[6])
            def kv_mm(b):
                pk, pr = (0, 1) if b % 2 == 0 else (5, 7)
                for kc in range(DC):
                    K.mm(PS[pk][:nt, :512], HN[:, kc, b * 128:b * 128 + nt], WKV[:, kc, 0:512], kc == 0, kc == DC - 1, [bHN, bWKV], [bPS[pk]])
                for kc in range(DC):
                    K.mm(PS[pr][:nt, 0:64], HN[:, kc, b * 128:b * 128 + nt], WKV[:, kc, 512:576], kc == 0, kc == DC - 1, [bHN, bWKV], [bPS[pr]])
            kv_mm(0)
            for b in range(nblk):
                gb = col0 // 128 + b; i = b % 2
                pk, pr = (0, 1) if b % 2 == 0 else (5, 7)
                if b + 1 < nblk: kv_mm(b + 1)
                K.cp('scalar', CRAW[:nt, :], PS[pk][:nt, :512], [bPS[pk]], [bCRAW])
                K.tt('vector', JUNK[:nt, :], CRAW[:nt, :], CRAW[:nt, :], ALU.mult, [bCRAW], [bJUNK])
                K.op('vector', lambda e, nt=nt: e.tensor_reduce(out=SSK[:nt, 0:1], in_=JUNK[:nt, :], axis=AX.X, op=ALU.add), [bJUNK], [bSS])
                K.act(SSK[:nt, 1:2], SSK[:nt, 0:1], AF.Sqrt, [bSS, b_cst], [bSS], scale=1.0 / KV, bias=cst[:nt, 1:2])
                K.recip(SSK[:nt, 2:3], SSK[:nt, 1:2], [bSS], [bSS])
                K.stt(CKV[i][:nt, :], CRAW[:nt, :], SSK[:nt, 2:3], GKVB[:nt, :], ALU.mult, ALU.mult, [bCRAW, bSS, bTAB], [bCKV[i]])
                K.dma('sync', kv_out[gb * 128:gb * 128 + nt, :], CKV[i][:nt, :], 'ckv%d' % i, reads=[bCKV[i]])
                K.cp('gpsimd', VST[i][:nt, :], CKV[i][:nt, :], [bCKV[i]], [bVST[i]])
                K.dma('sync', V_s[gb * 128:gb * 128 + nt, :], VST[i][:nt, :], 'vst%d' % i, reads=[bVST[i]])
                for cc in range(4):
                    K.tr(PS[2][:, cc * 128:cc * 128 + nt], CKV[i][:nt, cc * 128:(cc + 1) * 128], identf[:nt, :nt],
                         [bCKV[i], b_identf], [bPS[2]])
                K.cp('scalar', KTST[:, 0:4, b * 128:b * 128 + nt], v4(PS[2][:, :])[:, :, :nt], [bPS[2]], [bKTST])
                K.cp('scalar', KR[:nt, 0:64], PS[pr][:nt, 0:64], [bPS[pr]], [bKR])
                K.cp('scalar', KR[:nt, 64:96], PS[pr][:nt, 32:64], [bPS[pr]], [bKR])
                K.cp('scalar', KR[:nt, 96:128], PS[pr][:nt, 0:32], [bPS[pr]], [bKR])
                K.tt('vector', KR[:nt, 128:192], KR[:nt, 0:64], COSK[:nt, gb, :], ALU.mult, [bKR, bTAB], [bKR])
                K.tt('vector', KR[:nt, 192:256], KR[:nt, 64:128], SINK[:nt, gb, :], ALU.mult, [bKR, bTAB], [bKR])
                K.tt('vector', KPE[i][:nt, :], KR[:nt, 128:192], KR[:nt, 192:256], ALU.add, [bKR], [bKPE[i]])
                K.dma('sync', kpe_out[gb * 128:gb * 128 + nt, :], KPE[i][:nt, :], 'kpe%d' % i, reads=[bKPE[i]])
                K.tr(PS[pr][0:64, 128:128 + nt], KPE[i][:nt, 0:64], identf[:nt, :nt], [bKPE[i], b_identf], [bPS[pr]])
                K.cp('scalar', KTST[0:64, 4, b * 128:b * 128 + nt], PS[pr][0:64, 128:128 + nt], [bPS[pr]], [bKTST])
            K.dma('sync', KT_s[0:512, col0:col0 + N].rearrange("(c p) n -> p c n", p=128), KTST[:, 0:4, :N], 'ktst', reads=[bKTST])
            K.dma('sync', KT_s[512:576, col0:col0 + N], KTST[0:64, 4, :N], 'ktst', reads=[bKTST])
            if kind != 'partner':
                tsel = 2 if kind == 'sample' else tb
                K.dma('sync', INVC[:, :, :], invc[:, tsel, :, :], 'invc', writes=[bINVC])
            pms = []
            for m in range(8):
                i = m % 2; pu = PS[3 + i]; bpu = bPS[3 + i]; g = m // 2; w = (2, 4, 8, 16)[g]
                K.dma('sync', WFM[i][:, :], s_winfm[m, :, :], 'wfm%d' % i, writes=[bWFM[i]])
                for kc in range(DC):
                    K.mm(pu[:, :N], WFM[i][:, kc * 128:(kc + 1) * 128], HN[:, kc, :N], kc == 0, kc == DC - 1, [bWFM[i], bHN], [bpu])
                if kind == 'partner':
                    K.cp('scalar', UTAIL[:, m, tb * 4:(tb + 1) * 4, :], v4(pu[:, :])[:, :, 113:128], [bpu], [bUTAIL])
                    continue
                if kind == 'own':
                    U = UE[i]; bU = bUE[i]
                    K.cp('scalar', U[:, :, 15:143], v4(pu[:, :]), [bpu], [bU])
                    K.cp('gpsimd', U[:, :, 0:15], UTAIL[:, m, tb * 4:(tb + 1) * 4, :], [bUTAIL], [bU])
                    K.tt('gpsimd', PA[:, :, 1:143], U[:, :, 1:143], U[:, :, 0:142], ALU.add, [bU], [bPA]); cur, bcur = PA, bPA
                    if w >= 4:
                        K.tt('gpsimd', PB[:, :, 3:143], PA[:, :, 3:143], PA[:, :, 1:141], ALU.add, [bPA], [bPB]); cur, bcur = PB, bPB
                    if w >= 8:
                        K.tt('gpsimd', PA[:, :, 7:143], PB[:, :, 7:143], PB[:, :, 3:139], ALU.add, [bPB], [bPA]); cur, bcur = PA, bPA
                    if w >= 16:
                        K.tt('gpsimd', PB[:, :, 15:143], PA[:, :, 15:143], PA[:, :, 7:135], ALU.add, [bPA], [bPB]); cur, bcur = PB, bPB
                    K.tt('gpsimd', cur[:, :, 15:143], cur[:, :, 15:143], v4(INVC[:, g, :]), ALU.mult, [bcur, bINVC], [bcur])
                    K.tt('gpsimd', v4(DT[:, m, :]), cur[:, :, 15:143], U[:, :, 15:143], ALU.subtract, [bcur, bU], [bDT])
                    if tb == 1:
                        K.cp('gpsimd', ULAST[:, m, :], U[:, 3, 127:143], [bU], [bULAST])
                else:
                    K.cp('scalar', UEXS[:, m, :, 15], pu[:, :NS], [bpu], [bUEXS])
                    K.op('vector', lambda e, m=m, w=w: e.tensor_reduce(out=SWS[:, :], in_=UEXS[:, m, :, 16 - w:16], axis=AX.X, op=ALU.add),
                         [bUEXS], [bSWS])
                    K.stt(DT[:, m, :NS], SWS[:, :], 1.0 / w, UEXS[:, m, :, 15], ALU.mult, ALU.subtract, [bSWS, bUEXS], [bDT])
                    K.cp('gpsimd', ULAST[:, m, :], UEXS[:, m, :, 15], [bUEXS], [bULAST])
                if m % 2 == 1:
                  def pm(g=g):
                    for dd in range(2):
                          for cc in range(2):
                              k = g * 2 + cc
                              K.mm(PS[5][:, :N], WPL[:, k * 256 + dd * 128:k * 256 + (dd + 1) * 128], DT[:, 2 * g + cc, :N], cc == 0, cc == 1,
                                   [bWPL, bDT], [bPS[5]])
                          mo = 2 * g + dd
                          K.ts('vector', POS[dd][:, :N], PS[5][:, :N], gv[:, G_PSC + mo:G_PSC + mo + 1], ALU.mult, [bPS[5], b_gv], [bPOS[dd]])
                          K.dma('sync', mixT[mo * 128:(mo + 1) * 128, ocol0:ocol0 + N], POS[dd][:, :N], 'pos%d' % dd, reads=[bPOS[dd]])
                  pms.append(pm)
                  if len(pms) > 1: pms.pop(0)()
            if kind == 'partner':
                continue
            if kind == 'sample' or tb == 1:
                for m in range(8):
                    pb = 7 if m < 4 else 5
                    K.tr(PS[pb][:16, (m % 4) * 128:(m % 4 + 1) * 128], ULAST[:, m, :], identf[:, :], [bULAST, b_identf], [bPS[pb]])
                K.cp('scalar', PLS[:16, 0:512], PS[7][:16, :512], [bPS[7]], [bPLS])
                K.cp('scalar', PLS[:16, 512:1024], PS[5][:16, :512], [bPS[5]], [bPLS])
                K.dma('sync', (spool_new if kind == 'sample' else pool_last)[:, :], PLS[:16, :], 'pls', reads=[bPLS])
            for mq in range(4):
                m = 8 + mq; i = m % 2; pu = PS[3 + i]; bpu = bPS[3 + i]
                K.dma('sync', WFM[i][:, :], s_winfm[m, :, :], 'wfm%d' % i, writes=[bWFM[i]])
                for kc in range(DC):
                    K.mm(pu[:, :N], WFM[i][:, kc * 128:(kc + 1) * 128], HN[:, kc, :N], kc == 0, kc == DC - 1, [bWFM[i], bHN], [bpu])
                K.cp('scalar', CQ[:, mq, :N], pu[:, :N], [bpu], [bCQ])
            while pms: pms.pop(0)()
            prenorm(CQ, bCQ, CQ, bCQ, N, G_Q, SQ, bSQ, R1, bR1, PS[6], bPS[6], nchunk=4, dim=QL)
            K.dma('sync', cqn_s[:, ocol0:ocol0 + N].rearrange("(c p) n -> p c n", p=128), CQ[:, :, :N], 'cq', reads=[bCQ])
        K.barrier(); K.emit()
    if stop <= 3: return nc, es, K, locals()

    with ExitStack() as es2:
        KT = sb('KT', [128, 5, 2048], BF16); V = sb('V', [128, 16, 512], BF16); bKT = Buf('KT'); bV = Buf('V')
        KTN = sb('KTN', [128, 5, NS], BF16); VN = sb('VN', [NS, 512], BF16)
        WQN = sb('WQN', [128, 4096], BF16); WQR = sb('WQR', [128, 2048], BF16); WQX = sb('WQX', [128, 2048], BF16)
        WUK = sb('WUK', [128, 4096], BF16); WUV = sb('WUV', [128, 4096], BF16); bW = Buf('Wq')
        CQL = sb('CQL', [128, 4, 128], F32); bCQL = Buf('CQL'); CQB = sb('CQB', [128, 4, 128], BF16); bCQB = Buf('CQB')
        QN = sb('QN', [128, 8, 128], BF16); bQN = Buf('QN')
        QCAT = sb('QCAT', [128, 5, 1024], BF16); bQCAT = Buf('QCAT')
        COSQ = sb('COSQ', [128, 1024 + NS], F32); SINQ = sb('SINQ', [128, 1024 + NS], F32); TRI = sb('TRI', [128, 128], F32); bTQ = Buf('TQ')
        T1 = sb('T1', [128, 128], F32); T2 = sb('T2', [128, 128], F32); bT1 = Buf('T1'); bT2 = Buf('T2')
        PT = [sb('pt%d' % i, [128, 512], BF16) for i in range(2)]; bPT = [Buf('pt%d' % i) for i in range(2)]
        ACC = sb('ACC', [128, 1024], F32); bACC = Buf('ACC'); RINV = sb('RINV', [128, 512], F32); bRINV = Buf('RINV')
        OT = sb('OT', [128, 4, 1024], BF16); bOT = Buf('OT'); AST = sb('AST', [128, 8, 512], BF16); bAST = Buf('AST')
        PTAB = sb('PTAB', [128, NS * NPG // 4], I32); bPTAB = Buf('PTAB')
        IDX = sb('IDX', [128, NS * NPG // 4], I32); bIDX = Buf('IDX')
        K.dma('sync', KT[:, 0:4, :], KT_s[0:512, 0:2048].rearrange("(c p) n -> p c n", p=128), 'kt', writes=[bKT])
        K.dma('sync', KT[0:64, 4, :], KT_s[512:576, 0:2048], 'kt', writes=[bKT])
        K.dma('sync', V[:, :, :], V_s[0:2048, :].rearrange("(k p) c -> p k c", p=128), 'v', writes=[bV])
        K.dma('sync', KTN[:, 0:4, :], KT_s[0:512, 2048:2048 + NS].rearrange("(c p) n -> p c n", p=128), 'kt', writes=[bKT])
        K.dma('sync', KTN[0:64, 4, :], KT_s[512:576, 2048:2048 + NS], 'kt', writes=[bKT])
        K.dma('sync', VN[:, :], V_s[2048:2048 + NS, :], 'v', writes=[bV])
        for t_, k_ in ((WQN, 'wqn'), (WQR, 'wqr'), (WQX, 'wqx'), (WUK, 'wuk'), (WUV, 'wuv')):
            K.dma('sync', t_[:, :], s_small[k_][:, :], 'wq', writes=[bW])
        K.dma('sync', COSQ[:, :], cosq[:, :], 'tq', writes=[bTQ]); K.dma('sync', SINQ[:, :], sinq[:, :], 'tq', writes=[bTQ])
        K.dma('sync', TRI[:, :], tri_in[:, :], 'tq', writes=[bTQ])
        K.dma('sync', PTAB[:, :], ptab[:, :], 'ptab', writes=[bPTAB])
        K.ts('vector', IDX[:, :], PTAB[:, :], 32.0, ALU.mult, [bPTAB, b_cst], [bIDX], s2=cst[:, 2:3], op1=ALU.add)

        def qpath(c0, nq, qcat_view):
            K.dma('sync', CQL[:, :, :nq], cqn_s[:, c0:c0 + nq].rearrange("(c p) n -> p c n", p=128), 'cql', writes=[bCQL])
            K.cp('vector', CQB[:, :, :nq], CQL[:, :, :nq], [bCQL], [bCQB])
            for h in range(NH):
                for kc in range(4):
                    K.mm(PS[6][:, :nq], WQN[:, kc * 1024 + h * 128:kc * 1024 + (h + 1) * 128], CQB[:, kc, :nq], kc == 0, kc == 3, [bW, bCQB], [bPS[6]])
                K.cp('scalar', QN[:, h, :nq], PS[6][:, :nq], [bPS[6]], [bQN])
            for h in range(NH):
                for kc in range(4):
                    K.mm(PS[6][0:64, :nq], WQR[:, kc * 512 + h * 64:kc * 512 + (h + 1) * 64], CQB[:, kc, :nq], kc == 0, kc == 3, [bW, bCQB], [bPS[6]])
                for kc in range(4):
                    K.mm(PS[7][0:64, :nq], WQX[:, kc * 512 + h * 64:kc * 512 + (h + 1) * 64], CQB[:, kc, :nq], kc == 0, kc == 3, [bW, bCQB], [bPS[7]])
                K.tt('vector', T1[0:64, :nq], PS[6][0:64, :nq], COSQ[0:64, c0:c0 + nq], ALU.mult, [bPS[6], bTQ], [bT1])
                K.tt('vector', T2[0:64, :nq], PS[7][0:64, :nq], SINQ[0:64, c0:c0 + nq], ALU.mult, [bPS[7], bTQ], [bT2])
                K.tt('vector', qcat_view(4, h)[0:64, :], T1[0:64, :nq], T2[0:64, :nq], ALU.add, [bT1, bT2], [bQCAT])
            for h in range(NH):
                for cc in range(4):
                    K.mm(PS[6][:, cc * 128:cc * 128 + nq], WUK[:, h * 512 + cc * 128:h * 512 + (cc + 1) * 128], QN[:, h, :nq], True, True,
                         [bW, bQN], [bPS[6]])
                for cc in range(4):
                    K.cp('scalar' if cc % 2 == 0 else 'vector', qcat_view(cc, h), PS[6][:, cc * 128:cc * 128 + nq], [bPS[6]], [bQCAT])

        for i in range(8):
            tb = i // 4
            qpath(i * 128, 128, lambda ch, h: QCAT[:, ch, h * 128:(h + 1) * 128])
            kbs = list(range(0, i + 1)) + list(range(8, 8 + i + 1))
            for half in range(2):
                hs = slice(half * 512, (half + 1) * 512)
                def st(n, kb):
                    ps = PS[n % 2]; bps = bPS[n % 2]; pt = PT[n % 2]; bpt = bPT[n % 2]
                    for ch in range(5):
                        rows = 128 if ch < 4 else 64
                        K.mm(ps[:, :512], KT[:rows, ch, kb * 128:(kb + 1) * 128], QCAT[:rows, ch, hs], ch == 0, ch == 4, [bKT, bQCAT], [bps])
                    K.act(pt[:, :], ps[:, :512], AF.Exp, [bps], [bpt], scale=SM_SCALE)
                    if kb == i:
                        for hh in range(4):
                            K.tt('vector', pt[:, hh * 128:(hh + 1) * 128], pt[:, hh * 128:(hh + 1) * 128], TRI[:, :], ALU.mult, [bpt, bTQ], [bpt])
                    if kb == 8:
                        K.ts('vector', pt[:, :], pt[:, :], cst[:, 0:1], ALU.mult, [bpt, b_cst], [bpt])
                    if n == 0:
                        K.cp('gpsimd', ACC[:, hs], pt[:, :], [bpt], [bACC])
                    else:
                        K.tt('gpsimd', ACC[:, hs], ACC[:, hs], pt[:, :], ALU.add, [bpt, bACC], [bACC])
                def pv(n, kb):
                    pt = PT[n % 2]; bpt = bPT[n % 2]
                    for cc in range(4):
                        K.mm(PS[2 + cc][:, :512], V[:, kb, cc * 128:(cc + 1) * 128], pt[:, :], n == 0, n == len(kbs) - 1, [bV, bpt], [bPS[2 + cc]])
                st(0, kbs[0])
                for n, kb in enumerate(kbs):
                    if n + 1 < len(kbs): st(n + 1, kbs[n + 1])
                    pv(n, kb)
                K.mm(PS[6][:, :512], ones_f[:, :], ACC[:, hs], True, True, [b_ones, bACC], [bPS[6]])
                K.recip(RINV[:, :], PS[6][:, :512], [bPS[6]], [bRINV])
                for cc in range(4):
                    K.tt('vector', OT[:, cc, hs], PS[2 + cc][:, :512], RINV[:, :], ALU.mult, [bPS[2 + cc], bRINV], [bOT])
            for h in range(NH):
                for cc in range(4):
                    K.mm(PS[7][:, :128], WUV[:, cc * 1024 + h * 128:cc * 1024 + (h + 1) * 128], OT[:, cc, h * 128:(h + 1) * 128], cc == 0, cc == 3,
                         [bW, bOT], [bPS[7]])
                K.cp('scalar', AST[:, h, (i % 4) * 128:(i % 4 + 1) * 128], PS[7][:, :128], [bPS[7]], [bAST])
            if i % 4 == 3:
                K.dma('sync', mixT[1024:2048, tb * 512:(tb + 1) * 512].rearrange("(h p) n -> p h n", p=128), AST[:, :, :], 'ast', reads=[bAST])

        qpath(1024, NS, lambda ch, h: QCAT[:, ch, h * NS:(h + 1) * NS])
        GP = 4; NG = NPG // GP; NGT = NS * NG
        KPV = [sb('kpv%d' % i, [128, GP, 512], F32) for i in range(3)]; bKPV = [Buf('kpv%d' % i) for i in range(3)]
        KPR = [sb('kpr%d' % i, [128, GP, 64], F32) for i in range(3)]; bKPR = [Buf('kpr%d' % i) for i in range(3)]
        KTP = [sb('ktp%d' % i, [128, 4, 128], BF16) for i in range(8)]; bKTP = [Buf('ktp%d' % i) for i in range(8)]
        KTR = [sb('ktr%d' % i, [64, GP * 128], BF16) for i in range(2)]; bKTR = [Buf('ktr%d' % i) for i in range(2)]
        VP = [sb('vp%d' % i, [128, 512], BF16) for i in range(12)]; bVP = [Buf('vp%d' % i) for i in range(12)]
        PTS = [sb('pts%d' % i, [128, GP * 8], BF16) for i in range(2)]; bPTS = [Buf('pts%d' % i) for i in range(2)]
        PTN = sb('PTN', [NS, 8], BF16); bPTN = Buf('PTN')
        ACCS2 = [sb('ACCS%d' % i, [128, GP * 8], F32) for i in range(2)]; bACCS2 = [Buf('ACCS%d' % i) for i in range(2)]; RINVS = sb('RINVS', [8, 1], F32); bRINVS = Buf('RINVS')
        OS = sb('OS', [8, 512], F32); bOS = Buf('OS')
        OTS = sb('OTS', [128, 4, NH * NS], BF16); bOTS = Buf('OTS')
        ckv4 = cache_kv.rearrange("(r j) c -> r (j c)", j=GP); ckr4 = cache_kr.rearrange("(r j) c -> r (j c)", j=GP)
        def qrhs(ch, rows, b):
            return QCAT[:rows, ch, 0:NH * NS].rearrange("p (h b) -> p h b", b=NS)[:, :, b]
        def stageA(G):
            sl = G % 3
            K.op('gpsimd', lambda e, sl=sl, G=G: e.indirect_dma_start(
                out=KPV[sl][:, :, :].rearrange("p j c -> p (j c)"), out_offset=None, in_=ckv4[:, :],
                in_offset=bass.IndirectOffsetOnAxis(ap=IDX[:, G:G + 1], axis=0)), [bIDX], [bKPV[sl]], dsem='kpv%d' % sl)
            K.op('gpsimd', lambda e, sl=sl, G=G: e.indirect_dma_start(
                out=KPR[sl][:, :, :].rearrange("p j c -> p (j c)"), out_offset=None, in_=ckr4[:, :],
                in_offset=bass.IndirectOffsetOnAxis(ap=IDX[:, G:G + 1], axis=0)), [bIDX], [bKPR[sl]], dsem='kpr%d' % sl)
        def stageB(G):
            sl = G % 3
            for j in range(GP):
                u = (G * GP + j) % 8; pb = 3 + (u % 4)
                for ch in range(4):
                    K.tr(PS[pb][:, ch * 128:(ch + 1) * 128], KPV[sl][:, j, ch * 128:(ch + 1) * 128], identf[:, :], [bKPV[sl], b_identf], [bPS[pb]])
                K.cp('scalar', KTP[u][:, :, :], v4(PS[pb][:, :]), [bPS[pb]], [bKTP[u]])
                K.tr(PS[7][0:64, j * 128:(j + 1) * 128], KPR[sl][:, j, :], identf[:, :], [bKPR[sl], b_identf], [bPS[7]])
                uv = (G * GP + j) % 12
                K.cp('vector', VP[uv][:, :], KPV[sl][:, j, :], [bKPV[sl]], [bVP[uv]])
            K.cp('vector', KTR[G % 2][0:64, :], PS[7][0:64, :512], [bPS[7]], [bKTR[G % 2]])
        def stageC1(G):
            b = G // NG; g = G % NG; ps = PS[G % 2]; bps = bPS[G % 2]; pts = PTS[G % 2]; bpts = bPTS[G % 2]
            ACCS = ACCS2[b % 2]; bACCS = bACCS2[b % 2]
            for j in range(GP):
                u = (G * GP + j) % 8
                for ch in range(5):
                    if ch < 4:
                        K.mm(ps[:, j * 8:(j + 1) * 8], KTP[u][:, ch, :], qrhs(ch, 128, b), ch == 0, False, [bKTP[u], bQCAT], [bps])
                    else:
                        K.mm(ps[:, j * 8:(j + 1) * 8], KTR[G % 2][0:64, j * 128:(j + 1) * 128], qrhs(4, 64, b), False, True, [bKTR[G % 2], bQCAT], [bps])
            K.act(pts[:, :], ps[:, 0:GP * 8], AF.Exp, [bps], [bpts], scale=SM_SCALE)
            if g == 0:
                K.cp('gpsimd', ACCS[:, :], pts[:, :], [bpts], [bACCS])
            else:
                K.tt('gpsimd', ACCS[:, :], ACCS[:, :], pts[:, :], ALU.add, [bpts, bACCS], [bACCS])
            if g == NG - 1:
                for ch in range(5):
                    rows = 128 if ch < 4 else 64
                    K.mm(ps[:NS, 40:48], KTN[:rows, ch, :], qrhs(ch, rows, b), ch == 0, ch == 4, [bKT, bQCAT], [bps])
                K.act(PTN[:, :], ps[:NS, 40:48], AF.Exp, [bps], [bPTN], scale=SM_SCALE)
                K.ts('vector', PTN[:, :], PTN[:, :], identf[:NS, b:b + 1], ALU.mult, [bPTN, b_identf], [bPTN])
                K.tt('gpsimd', ACCS[:NS, 0:8], ACCS[:NS, 0:8], PTN[:, :], ALU.add, [bPTN, bACCS], [bACCS])
        def stageC2(G):
            b = G // NG; g = G % NG; ps = PS[G % 2]; bps = bPS[G % 2]; pts = PTS[G % 2]; bpts = bPTS[G % 2]
            ACCS = ACCS2[b % 2]; bACCS = bACCS2[b % 2]
            for j in range(GP):
                uv = (G * GP + j) % 12
                K.mm(PS[2][0:8, :512], pts[:, j * 8:(j + 1) * 8], VP[uv][:, :], g == 0 and j == 0, False, [bpts, bVP[uv]], [bPS[2]])
            if g == NG - 1:
                K.mm(PS[2][0:8, :512], PTN[:, :], VN[:NS, :], False, True, [bV, bPTN], [bPS[2]])
                for j in range(GP):
                    K.mm(ps[0:8, 56:57], ACCS[:, j * 8:(j + 1) * 8], ones_f[:, 0:1], j == 0, j == GP - 1, [b_ones, bACCS], [bps])
                K.recip(RINVS[:, :], ps[0:8, 56:57], [bps], [bRINVS])
                K.ts('vector', OS[:, :], PS[2][0:8, :512], RINVS[:, 0:1], ALU.mult, [bPS[2], bRINVS], [bOS])
                for cc in range(4):
                    K.tr(ps[:, 64 + cc * 8:64 + (cc + 1) * 8], OS[0:8, cc * 128:(cc + 1) * 128], identf[0:8, 0:8], [bOS, b_identf], [bps])
                K.cp('vector', OTS[:, :, :].rearrange("p c (h b) -> p c h b", b=NS)[:, :, :, b],
                     ps[:, 64:96].rearrange("p (c h) -> p c h", c=4), [bps], [bOTS])
        for G in range(NGT + 3):
            if G < NGT: stageA(G)
            if 1 <= G <= NGT: stageB(G - 1)
            if 2 <= G <= NGT + 1: stageC1(G - 2)
            if G >= 3: stageC2(G - 3)
        for h in range(NH):
            for cc in range(4):
                K.mm(PS[7][:, :NS], WUV[:, cc * 1024 + h * 128:cc * 1024 + (h + 1) * 128], OTS[:, cc, h * NS:(h + 1) * NS], cc == 0, cc == 3,
                     [bW, bOTS], [bPS[7]])
            K.cp('scalar', AST[:, h, :NS], PS[7][:, :NS], [bPS[7]], [bAST])
        K.dma('sync', mixT[1024:2048, 1024:1024 + NS].rearrange("(h p) n -> p h n", p=128), AST[:, :, :NS], 'ast', reads=[bAST])
        K.barrier(); K.emit()
    if stop <= 4: return nc, es, K, locals()

    with ExitStack() as es2:
        BIG = sb('BIG', [128, DC, 512], F32); bBIG = Buf('BIG')
        MIX = sb('MIX', [128, DC, 512], BF16); bMIX = Buf('MIX')
        WS = [sb('wo%d' % i, [128, D], BF16) for i in range(2)]; bWS = [Buf('wo%d' % i) for i in range(2)]
        SQ = [sb('sq%d' % i, [128, 512], BF16) for i in range(2)]; bSQ = [Buf('sq%d' % i) for i in range(2)]
        R1 = [sb('r1%d' % i, [128, 512], F32) for i in range(2)]; bR1 = [Buf('r1%d' % i) for i in range(2)]
        HC = [sb('hc%d' % i, [128, 512], F32) for i in range(2)]; bHC = [Buf('hc%d' % i) for i in range(2)]
        OC = [sb('oc%d' % i, [128, 512], F32) for i in range(2)]; bOC = [Buf('oc%d' % i) for i in range(2)]
        TMP = sb('tmp', [128, 512], F32); bTMP = Buf('tmp')
        for (scol0, N, dcol0) in ((0, 512, 0), (512, 512, 512), (2048, NS, 1024)):
            K.dma('sync', MIX[:, :, :N], mixT[:, dcol0:dcol0 + N].rearrange("(c p) n -> p c n", p=128), 'mix', writes=[bMIX])
            linear_to_big(MIX, bMIX, DC, s_wo, DC, WS, bWS, 'wo', BIG, bBIG, N, SQ, bSQ, [PS[4], PS[5]], [bPS[4], bPS[5]], PS[7], bPS[7])
            post_residual(BIG, bBIG, N, G_MIXPOST, 1.0, R1, bR1, PS[7], bPS[7], h1T, h2T, scol0, HC, bHC, OC, bOC, TMP, bTMP, dcol0=dcol0)
        K.barrier(); K.emit()
    if stop <= 5: return nc, es, K, locals()

    ffn_phase([(0, 512, None), (512, 512, (1024, NS))], h2T, h3T, G_F2PRE, G_F2POST, wsc['w2_gate'], wsc['w2_up'], wsc['w2_down'])

    with ExitStack() as es2:
        HB = [sb('hb%d' % i, [128, DC, 128], F32) for i in range(2)]; bHB = [Buf('hb%d' % i) for i in range(2)]
        YB = [sb('yb%d' % i, [128, D], F32) for i in range(2)]; bYB = [Buf('yb%d' % i) for i in range(2)]
        for blk in range(9):
            nt = 128 if blk < 8 else NS; i = blk % 2
            K.dma('sync', HB[i][:, :, :nt], h3T[:, blk * 128:blk * 128 + nt].rearrange("(c p) n -> p c n", p=128), 'hb%d' % i, writes=[bHB[i]])
            for g in range(4):
                pb = (blk * 4 + g) % 8
                for c4 in range(4):
                    c = g * 4 + c4
                    K.tr(PS[pb][:nt, c4 * 128:(c4 + 1) * 128], HB[i][:, c, :nt], identf[:, :], [bHB[i], b_identf], [bPS[pb]])
                K.cp('scalar' if g % 2 == 0 else 'vector', YB[i][:nt, g * 512:(g + 1) * 512], PS[pb][:nt, :512], [bPS[pb]], [bYB[i]])
            K.dma('sync', y_out[blk * 128:blk * 128 + nt, :], YB[i][:nt, :], 'yb%d' % i, reads=[bYB[i]])
        K.barrier(); K.op('sync', None); K.emit()
    return nc, es, K, locals()


def _rope_tables(pos):
    half = R // 2
    inv = (10000.0 ** (-np.arange(half, dtype=np.float32) / half)).astype(np.float32)
    ang = pos.astype(np.float32)[:, None] * inv[None, :]
    return np.cos(ang).astype(np.float32), np.sin(ang).astype(np.float32)


_CACHE = {}


def kernel(x_prompt, x_sample, cache_kv_latent, cache_k_rope, state_pool, page_table,
           g_ffn1_pre, w1_gate, w1_up, w1_down, g_ffn1_post,
           g_mix_pre, w_in, w_pool, pool_scale, g_q, w_uq, g_kv, w_uk, w_uv, w_out, g_mix_post,
           g_ffn2_pre, w2_gate, w2_up, w2_down, g_ffn2_post):
    global NPHYS
    f = lambda a: np.ascontiguousarray(np.asarray(a, dtype=np.float32))
    NPHYS = int(cache_kv_latent.shape[1])
    if 'nc' not in _CACHE:
        nc, es, K, L = build_program()
        es.close()
        _CACHE['nc'] = nc
    nc = _CACHE['nc']
    x_prompt = f(x_prompt); x_sample = f(x_sample)
    w_uq_ = f(w_uq)[0].reshape(QL, NH, 192)
    rot = np.concatenate([np.arange(32, 64), np.arange(0, 32)])
    def cols(v):
        v = f(v).reshape(-1)
        return v.reshape(-1, 128).T
    gvec = np.concatenate([cols(g_ffn1_pre), cols(g_ffn1_post), cols(g_mix_pre), cols(g_mix_post), cols(g_ffn2_pre), cols(g_ffn2_post),
                           cols(pool_scale), cols(g_q)], axis=1)
    shared = {
        'cache_kv': f(cache_kv_latent)[0].reshape(NPHYS * PAGE, KV), 'cache_kr': f(cache_k_rope)[0].reshape(NPHYS * PAGE, R),
        'w1_gate': f(w1_gate)[0], 'w1_up': f(w1_up)[0], 'w1_down': f(w1_down)[0],
        'w2_gate': f(w2_gate)[0], 'w2_up': f(w2_up)[0], 'w2_down': f(w2_down)[0],
        'w_in': f(w_in)[0], 'w_out': f(w_out)[0], 'w_pool': f(w_pool)[0].reshape(1024, 256),
        'wq_nope': np.ascontiguousarray(w_uq_[:, :, :128].reshape(QL, 1024)),
        'wq_rope': np.ascontiguousarray(w_uq_[:, :, 128:].reshape(QL, 512)),
        'wq_rot': np.ascontiguousarray(w_uq_[:, :, 128:][:, :, rot].reshape(QL, 512)),
        'wukT': np.ascontiguousarray(f(w_uk)[0].transpose(2, 1, 0).reshape(128, NH * KV)),
        'w_uv': f(w_uv)[0].reshape(KV, NH * 128),
        'gvec': np.ascontiguousarray(gvec), 'gkv_b': np.ascontiguousarray(np.broadcast_to(f(g_kv).reshape(1, KV), (128, KV))),
        'tri': np.triu(np.ones((128, 128), np.float32)), 'ident': np.eye(128, dtype=np.float32),
    }
    pt = np.asarray(page_table).astype(np.int32)
    sp = f(state_pool)[0]
    wins = (2, 4, 8, 16)
    in_maps = []
    for c in range(8):
        s_, r_ = c // 2, c % 2
        own = [2 * j + r_ for j in range(8)]
        partner = [2 * j + 1 - r_ for j in range(8)] if r_ == 1 else [-1] + [2 * j - 1 for j in range(1, 8)]
        xl = np.zeros((NLOC, D), np.float32); pos = np.zeros(NLOC, np.int64)
        for li, gbk in enumerate(own + partner):
            if gbk >= 0:
                xl[li * 128:(li + 1) * 128] = x_prompt[s_, gbk * 128:(gbk + 1) * 128]
                pos[li * 128:(li + 1) * 128] = np.arange(gbk * 128, (gbk + 1) * 128)
        xl[2048:] = x_sample[c * NS:(c + 1) * NS, 0]
        pos[2048:] = pt.shape[1] * PAGE
        cs, sn = _rope_tables(pos)
        ck = np.zeros((17 * 128, R), np.float32); sk = np.zeros((17 * 128, R), np.float32)
        ck[:NLOC] = np.concatenate([cs, cs], 1); sk[:NLOC] = np.concatenate([-sn, sn], 1)
        qsel = np.concatenate([np.arange(0, 1024), np.arange(2048, NLOC)])
        cq = np.zeros((128, 1024 + NS), np.float32); sq_ = np.zeros((128, 1024 + NS), np.float32)
        cq[:64] = np.concatenate([cs[qsel], cs[qsel]], 1).T; sq_[:64] = np.concatenate([-sn[qsel], sn[qsel]], 1).T
        ic = np.zeros((3, 4, 512), np.float32)
        for t in range(2):
            p_ = pos[t * 512:(t + 1) * 512]
            for g, w in enumerate(wins):
                ic[t, g] = 1.0 / np.minimum(p_ + 1, w)
        for g, w in enumerate(wins):
            ic[2, g] = 1.0 / w
        cst = np.zeros((128, 4), np.float32); cst[:, 0] = 1.0 if r_ == 1 else 0.0; cst[:, 1] = EPS; cst[:, 2] = np.arange(128) % 32
        m = dict(shared)
        m.update({'x_loc': xl, 'state_pool': np.ascontiguousarray(sp[c * NS:(c + 1) * NS]),
                  'ptab': np.ascontiguousarray(pt[c * NS:(c + 1) * NS].reshape(NS * NPG // 4, 4).T[np.arange(128) // 32]),
                  'cosk': np.ascontiguousarray(ck.reshape(17, 128, R).transpose(1, 0, 2)),
                  'sink': np.ascontiguousarray(sk.reshape(17, 128, R).transpose(1, 0, 2)),
                  'cosq': cq, 'sinq': sq_, 'invc': np.ascontiguousarray(np.broadcast_to(ic[None], (128, 3, 4, 512))), 'consts': cst})
        in_maps.append(m)
    res = run_bass_kernel_spmd(nc, in_maps, core_ids=list(range(8))).results
    B = 4
    y_p = np.zeros((B, SEQ, D), np.float32); y_s = np.zeros((B * 32, 1, D), np.float32)
    p_kv = np.zeros((1, B, SEQ, KV), np.float32); p_pe = np.zeros((1, B, SEQ, R), np.float32)
    p_pool = np.zeros((1, B, 15, DP), np.float32)
    s_kv = np.zeros((1, 128, 1, KV), np.float32); s_pe = np.zeros((1, 128, 1, R), np.float32); s_pool = np.zeros((1, 128, 15, DP), np.float32)
    for c in range(8):
        s_, r_ = c // 2, c % 2; o = res[c]
        for j in range(8):
            gbk = 2 * j + r_
            y_p[s_, gbk * 128:(gbk + 1) * 128] = o['y_out'][j * 128:(j + 1) * 128]
            p_kv[0, s_, gbk * 128:(gbk + 1) * 128] = o['kv_out'][j * 128:(j + 1) * 128]
            p_pe[0, s_, gbk * 128:(gbk + 1) * 128] = o['kpe_out'][j * 128:(j + 1) * 128]
        if r_ == 1:
            p_pool[0, s_] = o['pool_last'][1:16]
        sl = slice(c * NS, (c + 1) * NS)
        y_s[sl, 0] = o['y_out'][1024:1024 + NS]
        s_kv[0, sl, 0] = o['kv_out'][2048:2048 + NS]; s_pe[0, sl, 0] = o['kpe_out'][2048:2048 + NS]
        s_pool[0, sl, :14] = o['spool_hist']; s_pool[0, sl, 14] = o['spool_new']
    return (y_p, y_s, p_kv, p_pe, p_pool, s_kv, s_pe, s_pool)
```

```python
import numpy as np
import concourse.bass as bass
import concourse.mybir as mybir
from concourse.bass_utils import run_bass_kernel_spmd
from contextlib import ExitStack
import os

F32, BF16, I32 = mybir.dt.float32, mybir.dt.bfloat16, mybir.dt.int32
AF = mybir.ActivationFunctionType
ALU = mybir.AluOpType
AX = mybir.AxisListType

D = 2048; DC = 16; FF = 5632; FC = 44; DP = 1024; QL = 512; KV = 512; R = 64
NH = 8; SEQ = 2048; NS = 16; NPG = 64; PAGE = 128
NPHYS = 10240
NLOC = 2048 + NS
EPS = 1e-6
SM_SCALE = float((128 + 64) ** -0.5)
ENGS = ['tensor', 'scalar', 'vector', 'gpsimd', 'sync']


class Buf:
    def __init__(s, name):
        s.name = name; s.w = None; s.r = {}; s.rd = []


class Op:
    __slots__ = ('eng', 'fn', 'deps', 'signal', 'ev', 'dsem', 'n')

    def __init__(s, eng, fn, dsem):
        s.eng = eng; s.fn = fn; s.deps = []; s.signal = False; s.ev = None; s.dsem = dsem


class Ker:
    def __init__(s, nc, es):
        s.nc = nc; s.es = es; s.ops = []; s.nall = 0
        s.esem = {e: es.enter_context(nc.semaphore('e_' + e)) for e in ENGS}
        s.ecnt = {e: 0 for e in ENGS}
        s.dsem = {}; s.dcnt = {}
        s.waited = {e: {} for e in ENGS}
        s.fence = []; s.fenced = set(ENGS)
        s.last = {}; s.phase0 = 0

    def _sem(s, key):
        if key not in s.dsem:
            s.dsem[key] = s.es.enter_context(s.nc.semaphore('d_' + key)); s.dcnt[key] = 0
        return s.dsem[key]

    def op(s, eng, fn, reads=(), writes=(), dsem=None):
        o = Op(eng, fn, dsem); o.n = s.nall; s.nall += 1
        deps = {}
        def add(d):
            if d is None or d is o: return
            if eng == 'tensor' and d.eng == 'tensor' and d.dsem is None: return
            deps[id(d)] = d
        for b in reads:
            add(b.w)
        for b in writes:
            add(b.w)
            for d in b.r.values(): add(d)
            for d in b.rd: add(d)
        if eng not in s.fenced:
            for d in s.fence: add(d)
            s.fenced.add(eng)
        o.deps = list(deps.values())
        for b in writes:
            b.w = o; b.r = {}; b.rd = []
        for b in reads:
            if b in writes: continue
            if dsem is not None: b.rd.append(o)
            else: b.r[eng] = o
        s.ops.append(o); s.last[eng] = o
        if dsem is not None: s.last['dma_' + dsem] = o
        return o

    def barrier(s):
        s.fence = list(s.last.values()); s.fenced = set()
        for o in s.fence: o.signal = True

    def emit(s):
        for o in s.ops:
            for d in o.deps: d.signal = True
        for o in s.ops:
            if o.dsem is not None:
                s._sem(o.dsem); s.dcnt[o.dsem] += 16; o.ev = (s.dsem[o.dsem], s.dcnt[o.dsem])
            elif o.signal:
                s.ecnt[o.eng] += 1; o.ev = (s.esem[o.eng], s.ecnt[o.eng])
        with s.nc.Block() as blk:
            for eng in ENGS:
                ops_e = [o for o in s.ops if o.eng == eng]
                if not ops_e: continue
                def body(e, ops_e=ops_e, eng=eng):
                    wd = s.waited[eng]
                    for o in ops_e:
                        for d in o.deps:
                            if d.ev is None:
                                assert d.n < s.phase0, (d.eng, d.n)
                                continue
                            sem, val = d.ev
                            k = id(sem)
                            if wd.get(k, 0) < val:
                                e.wait_ge(sem, val); wd[k] = val
                        if o.fn is None: continue
                        ins = o.fn(e)
                        if o.dsem is not None: ins.then_inc(o.ev[0], 16)
                        elif o.signal: ins.then_inc(o.ev[0], 1)
                getattr(blk, eng)(body)
        s.ops = []; s.phase0 = s.nall

    def dma(s, q, out, in_, sem, reads=(), writes=()):
        return s.op(q, lambda e: e.dma_start(out=out, in_=in_), reads, writes, dsem=sem)

    def mm(s, out, lhsT, rhs, start, stop, reads, writes):
        return s.op('tensor', lambda e: e.matmul(out, lhsT, rhs, start=start, stop=stop), reads, writes)

    def tr(s, out, in_, ident, reads, writes):
        return s.op('tensor', lambda e: e.transpose(out, in_, ident), reads, writes)

    def act(s, out, in_, func, reads, writes, scale=None, bias=None, accum=None, eng='scalar'):
        kw = {}
        if scale is not None: kw['scale'] = scale
        if bias is not None: kw['bias'] = bias
        if accum is not None: kw['accum_out'] = accum
        return s.op('scalar', lambda e: e.activation(out, in_, func, **kw), reads, writes)

    def cp(s, eng, out, in_, reads, writes):
        if eng == 'scalar':
            return s.op(eng, lambda e: e.copy(out, in_), reads, writes)
        return s.op(eng, lambda e: e.tensor_copy(out, in_), reads, writes)

    def tt(s, eng, out, a, b, op, reads, writes):
        return s.op(eng, lambda e: e.tensor_tensor(out, a, b, op), reads, writes)

    def ts(s, eng, out, a, s1, op0, reads, writes, s2=None, op1=None):
        if op1 is None:
            return s.op(eng, lambda e: e.tensor_scalar(out, a, s1, None, op0), reads, writes)
        return s.op(eng, lambda e: e.tensor_scalar(out, a, s1, s2, op0, op1), reads, writes)

    def stt(s, out, a, sc, b, op0, op1, reads, writes):
        return s.op('vector', lambda e: e.scalar_tensor_tensor(out, a, sc, b, op0, op1), reads, writes)

    def recip(s, out, in_, reads, writes):
        return s.op('vector', lambda e: e.reciprocal(out, in_), reads, writes)


def eval_tiles(t):
    return [tuple(int(v) for v in x.split(':')) for x in t.split(',')]


def build_program(stop=99):
    nc = bass.Bass("TRN2", target_bir_lowering=False)
    es = ExitStack()
    def din(name, shape, dt=F32):
        return nc.dram_tensor(name, list(shape), dt, kind="ExternalInput").ap()
    def dout(name, shape, dt=F32):
        return nc.dram_tensor(name, list(shape), dt, kind="ExternalOutput").ap()
    def dscr(name, shape, dt):
        return nc.dram_tensor(name, list(shape), dt, kind="Internal").ap()

    x_loc = din('x_loc', [NLOC, D])
    cache_kv = din('cache_kv', [NPHYS * PAGE, KV])
    cache_kr = din('cache_kr', [NPHYS * PAGE, R])
    state_pool = din('state_pool', [NS, 15, DP])
    ptab = din('ptab', [128, NS * NPG // 4], I32)
    w_f32 = {}
    for nm in ('w1_gate', 'w1_up', 'w2_gate', 'w2_up'):
        w_f32[nm] = din(nm, [D, FF])
    for nm in ('w1_down', 'w2_down'):
        w_f32[nm] = din(nm, [FF, D])
    w_in = din('w_in', [D, 2112]); w_out = din('w_out', [D, D])
    w_pool = din('w_pool', [4 * 256, 256])
    wq_nope = din('wq_nope', [QL, 1024]); wq_rope = din('wq_rope', [QL, 512]); wq_rot = din('wq_rot', [QL, 512])
    wukT = din('wukT', [128, NH * KV]); w_uv = din('w_uv', [KV, NH * 128])
    gvec = din('gvec', [128, 6 * DC + 8 + 4])
    gkv_b = din('gkv_b', [128, KV])
    cosk = din('cosk', [128, 17, R]); sink = din('sink', [128, 17, R])
    cosq = din('cosq', [128, 1024 + NS]); sinq = din('sinq', [128, 1024 + NS])
    invc = din('invc', [128, 3, 4, 512])
    consts = din('consts', [128, 4])
    tri_in = din('tri', [128, 128])
    ident_in = din('ident', [128, 128])

    y_out = dout('y_out', [1024 + NS, D])
    kv_out = dout('kv_out', [NLOC, KV]); kpe_out = dout('kpe_out', [NLOC, R])
    pool_last = dout('pool_last', [16, DP]); spool_new = dout('spool_new', [16, DP])
    spool_hist = dout('spool_hist', [NS, 14, DP])

    wsc = {}
    for nm in ('w1_gate', 'w1_up', 'w2_gate', 'w2_up'):
        wsc[nm] = dscr('s_' + nm, [FC, 128, D], BF16)
    for nm in ('w1_down', 'w2_down'):
        wsc[nm] = dscr('s_' + nm, [DC, 128, FF], BF16)
    s_winfm = dscr('s_winfm', [12, 128, D], BF16)
    s_winkv = dscr('s_winkv', [128, DC * 576], BF16)
    s_wo = dscr('s_wo', [DC, 128, D], BF16)
    s_small = {k: dscr('s_' + k, [128, n], BF16) for k, n in
               (('wqn', 4096), ('wqr', 2048), ('wqx', 2048), ('wuk', 4096), ('wuv', 4096), ('wpool', 2048))}
    xT = dscr('xT', [D, NLOC], F32); h1T = dscr('h1T', [D, NLOC], F32)
    h2T = dscr('h2T', [D, 1024 + NS], F32); h3T = dscr('h3T', [D, 1024 + NS], F32)
    cqn_s = dscr('cqn_s', [QL, 1024 + NS], F32)
    mixT = dscr('mixT', [D, 1024 + NS], BF16)
    KT_s = dscr('KT_s', [640, NLOC], BF16)
    V_s = dscr('V_s', [NLOC, KV], BF16)

    K = Ker(nc, es)
    _cnt = [0]
    def sb(name, shape, dt):
        _cnt[0] += 1
        return es2.enter_context(nc.sbuf_tensor('%s_%d' % (name, _cnt[0]), list(shape), dt))
    PS = [es.enter_context(nc.psum_tensor('ps%d' % i, [128, 512], F32)) for i in range(8)]
    bPS = [Buf('ps%d' % i) for i in range(8)]
    identf = es.enter_context(nc.sbuf_tensor('identf', [128, 128], F32)); b_identf = Buf('identf')
    ones_bf = es.enter_context(nc.sbuf_tensor('ones_bf', [128, 128], BF16)); b_ones = Buf('ones')
    ones_f = es.enter_context(nc.sbuf_tensor('ones_f', [128, 128], F32))
    gv = es.enter_context(nc.sbuf_tensor('gv', [128, 6 * DC + 12], F32)); b_gv = Buf('gv')
    cst = es.enter_context(nc.sbuf_tensor('cst', [128, 4], F32)); b_cst = Buf('cst')
    K.dma('sync', identf[:, :], ident_in[:, :], 'c_id', writes=[b_identf])
    K.dma('sync', gv[:, :], gvec[:, :], 'c_gv', writes=[b_gv])
    K.dma('sync', cst[:, :], consts[:, :], 'c_cst', writes=[b_cst])
    K.op('vector', lambda e: e.memset(ones_bf[:, :], 1.0), writes=[b_ones])
    K.op('vector', lambda e: e.memset(ones_f[:, :], 1.0), writes=[b_ones])
    epsb = cst[:, 1:2]
    G_F1PRE, G_F1POST, G_MIXPRE, G_MIXPOST, G_F2PRE, G_F2POST = [i * DC for i in range(6)]
    G_PSC = 6 * DC; G_Q = 6 * DC + 8

    with ExitStack() as es2:
        NSL = 6
        S = [sb('castS%d' % i, [128, 4096], F32) for i in range(NSL)]
        T = [sb('castT%d' % i, [128, 4096], BF16) for i in range(NSL)]
        bS = [Buf('cS%d' % i) for i in range(NSL)]; bT = [Buf('cT%d' % i) for i in range(NSL)]
        jobs = []
        def job_perm(src, col0, kc_n, row0, dst3, ncols=256, lst=None):
            jn = ncols // 128
            sv = src[row0:row0 + kc_n * 128, col0:col0 + ncols].rearrange("(k p) c -> p k c", p=128)
            (jobs if lst is None else lst).append((sv, kc_n * ncols,
                         lambda Si, n=kc_n, jn=jn: Si[:, :n * jn * 128].rearrange("p (k j f) -> p j k f", k=n, j=jn),
                         lambda Ti, n=kc_n, jn=jn: Ti[:, :n * jn * 128].rearrange("p (j k f) -> p j k f", j=jn, k=n),
                         lambda Si, n=kc_n, jn=jn: Si[:, :n * jn * 128].rearrange("p (k c) -> p k c", k=n),
                         lambda Ti, n=kc_n, jn=jn: Ti[:, :n * jn * 128].rearrange("p (j x) -> p j x", j=jn),
                         dst3))
        def job_plain(sv, a, b, dst):
            jobs.append((sv, a * b, lambda Si: Si[:, :a * b], lambda Ti: Ti[:, :a * b],
                         lambda Si: Si[:, :a * b].rearrange("p (a b) -> p a b", a=a),
                         lambda Ti: Ti[:, :a * b].rearrange("p (a b) -> p a b", a=a), dst))
        def jobs_gate(src, dst, npair, c0=0):
            for jj in range(npair):
                job_perm(src, c0 + jj * 256, DC, 0, dst[2 * jj:2 * jj + 2, :, :].rearrange("j p x -> p j x"))
        def jobs_down(src, dst):
            for mm_ in range(8):
                for q in range(4):
                    job_perm(src, mm_ * 256, 11, q * 1408,
                             dst[2 * mm_:2 * mm_ + 2, :, q * 1408:(q + 1) * 1408].rearrange("m p x -> p m x"))
        jobs_gate(w_f32['w1_gate'], wsc['w1_gate'], 22); jobs_gate(w_f32['w1_up'], wsc['w1_up'], 22)
        jobs_down(w_f32['w1_down'], wsc['w1_down'])
        jobs_gate(w_in, s_winfm, 6)
        for k0, kn in ((0, 6), (6, 6), (12, 4)):
            job_plain(w_in[k0 * 128:(k0 + kn) * 128, 1536:2112].rearrange("(k p) c -> p k c", p=128), kn, 576,
                      s_winkv[:, k0 * 576:(k0 + kn) * 576].rearrange("p (a b) -> p a b", a=kn))
        job_plain(wq_nope.rearrange("(k p) c -> p k c", p=128), 4, 1024, s_small['wqn'].rearrange("p (a b) -> p a b", a=4))
        job_plain(wq_rope.rearrange("(k p) c -> p k c", p=128), 4, 512, s_small['wqr'].rearrange("p (a b) -> p a b", a=4))
        job_plain(wq_rot.rearrange("(k p) c -> p k c", p=128), 4, 512, s_small['wqx'].rearrange("p (a b) -> p a b", a=4))
        job_plain(wukT.rearrange("p (a b) -> p a b", a=8), 8, 512, s_small['wuk'].rearrange("p (a b) -> p a b", a=8))
        job_plain(w_uv.rearrange("(k p) c -> p k c", p=128), 4, 1024, s_small['wuv'].rearrange("p (a b) -> p a b", a=4))
        job_plain(w_pool.rearrange("(k p) c -> p k c", p=128), 8, 256, s_small['wpool'].rearrange("p (a b) -> p a b", a=8))
        jobs_bg = []
        def bg_gate(src, dst, npair):
            for jj in range(npair):
                for hf in range(2):
                    job_perm(src, jj * 256, 8, hf * 1024, dst[2 * jj:2 * jj + 2, :, hf * 1024:(hf + 1) * 1024].rearrange("j p x -> p j x"), lst=jobs_bg)
        def bg_down(src, dst):
            for m_ in range(DC):
                for q in range(4):
                    job_perm(src, m_ * 128, 11, q * 1408, dst[m_:m_ + 1, :, q * 1408:(q + 1) * 1408].rearrange("j p x -> p j x"), ncols=128, lst=jobs_bg)
        bg_gate(w_out, s_wo, 8)
        bg_gate(w_f32['w2_gate'], wsc['w2_gate'], 22); bg_gate(w_f32['w2_up'], wsc['w2_up'], 22)
        bg_down(w_f32['w2_down'], wsc['w2_down'])
        PF = 4
        ceng = ['vector', 'scalar', 'vector', 'scalar', 'gpsimd']
        for n in range(len(jobs) + PF):
            if n < len(jobs):
                sv, ne, civ, cov, siv, sov, dst = jobs[n]; i = n % NSL
                K.dma('sync', siv(S[i]), sv, 'cS%d' % i, writes=[bS[i]])
            m = n - PF
            if m >= 0:
                sv, ne, civ, cov, siv, sov, dst = jobs[m]; i = m % NSL
                K.cp(ceng[m % 5], cov(T[i]), civ(S[i]), [bS[i]], [bT[i]])
                K.dma('sync', dst, sov(T[i]), 'cT%d' % i, reads=[bT[i]])
        K.barrier(); K.emit()
    if stop <= 0: return nc, es, K, locals()

    with ExitStack() as es2:
        XB = [sb('xb%d' % i, [128, D], F32) for i in range(2)]; bXB = [Buf('xb%d' % i) for i in range(2)]
        XS = [sb('xs%d' % i, [128, DC, 128], F32) for i in range(2)]; bXS = [Buf('xs%d' % i) for i in range(2)]
        for blk in range(17):
            nt = 128 if blk < 16 else NS
            i = blk % 2
            K.dma('sync', XB[i][:nt, :], x_loc[blk * 128:blk * 128 + nt, :], 'xb%d' % i, writes=[bXB[i]])
            for g in range(4):
                pb = (blk * 4 + g) % 8
                for c4 in range(4):
                    c = g * 4 + c4
                    K.tr(PS[pb][:, c4 * 128:c4 * 128 + nt], XB[i][:nt, c * 128:(c + 1) * 128], identf[:nt, :nt],
                         [bXB[i], b_identf], [bPS[pb]])
                K.cp('scalar' if g % 2 == 0 else 'vector', XS[i][:, g * 4:(g + 1) * 4, :nt],
                     PS[pb][:, :].rearrange("p (c t) -> p c t", c=4)[:, :, :nt], [bPS[pb]], [bXS[i]])
            K.dma('sync', xT[:, blk * 128:blk * 128 + nt].rearrange("(c p) n -> p c n", p=128), XS[i][:, :, :nt],
                  'xs%d' % i, reads=[bXS[i]])
        K.barrier(); K.emit()
    if stop <= 1: return nc, es, K, locals()

    def prenorm(BIG, bBIG, XN, bXN, N, gcol, SQ, bSQ, R1, bR1, ps_ss, b_ss, nchunk=DC, dim=D):
        for c in range(nchunk):
            i = c % 2
            K.act(SQ[i][:, :N], BIG[:, c, :N], AF.Square, [bBIG], [bSQ[i]])
            K.mm(ps_ss[:, :N], ones_bf[:, :], SQ[i][:, :N], c == 0, c == nchunk - 1, [bSQ[i], b_ones], [b_ss])
        K.act(R1[0][:, :N], ps_ss[:, :N], AF.Sqrt, [b_ss, b_cst], [bR1[0]], scale=1.0 / dim, bias=epsb)
        K.recip(R1[1][:, :N], R1[0][:, :N], [bR1[0]], [bR1[1]])
        for c in range(nchunk):
            K.stt(XN[:, c, :N], BIG[:, c, :N], gv[:, gcol + c:gcol + c + 1], R1[1][:, :N], ALU.mult, ALU.mult,
                  [bBIG, bR1[1], b_gv], [bXN])

    def linear_to_big(inT, b_in, KC, wscr, M, WS, bWS, wkey, BIG, bBIG, N, SQ, bSQ, ps_y, b_y, ps_ss, b_ss):
        pend = None
        for m in range(M):
            i = m % 2
            K.dma('sync', WS[i][:, :KC * 128], wscr[m, :, :], wkey + str(i), writes=[bWS[i]])
            for kc in range(KC):
                K.mm(ps_y[i][:, :N], WS[i][:, kc * 128:(kc + 1) * 128], inT[:, kc, :N], kc == 0, kc == KC - 1,
                     [bWS[i], b_in], [b_y[i]])
            if pend is not None and not os.environ.get('DBG_NOPEND'):
                K.mm(*pend[0], **pend[1])
            K.cp('vector', BIG[:, m, :N], ps_y[i][:, :N], [b_y[i]], [bBIG])
            K.act(SQ[i][:, :N], BIG[:, m, :N], AF.Square, [bBIG], [bSQ[i]])
            pend = ((ps_ss[:, :N], ones_bf[:, :], SQ[i][:, :N], m == 0, m == M - 1), dict(reads=[bSQ[i], b_ones], writes=[b_ss]))
        if not os.environ.get('DBG_NOPEND'): K.mm(*pend[0], **pend[1])

    def post_residual(BIG, bBIG, N, gcol, alpha, R1, bR1, ps_ss, b_ss, src, dst, col0, HC, bHC, OC, bOC, TMP, bTMP, dcol0=None):
        if dcol0 is None: dcol0 = col0
        K.act(R1[0][:, :N], ps_ss[:, :N], AF.Sqrt, [b_ss, b_cst], [bR1[0]], scale=1.0 / D, bias=epsb)
        K.recip(R1[1][:, :N], R1[0][:, :N], [bR1[0]], [bR1[1]])
        nh = len(HC)
        def ld(c):
            K.dma('sync', HC[c % nh][:, :N], src[c * 128:(c + 1) * 128, col0:col0 + N], 'hc%d' % (c % nh), writes=[bHC[c % nh]])
        ahead = 2 if nh >= 4 else 0
        for c in range(min(ahead, DC)): ld(c)
        for c in range(DC):
            i = c % 2; h = c % nh
            if ahead == 0: ld(c)
            elif c + ahead < DC: ld(c + ahead)
            K.stt(TMP[:, :N], BIG[:, c, :N], gv[:, gcol + c:gcol + c + 1], R1[1][:, :N], ALU.mult, ALU.mult,
                  [bBIG, bR1[1], b_gv], [bTMP])
            K.stt(OC[i][:, :N], TMP[:, :N], float(alpha), HC[h][:, :N], ALU.mult, ALU.add, [bTMP, bHC[h]], [bOC[i]])
            K.dma('sync', dst[c * 128:(c + 1) * 128, dcol0:dcol0 + N], OC[i][:, :N], 'oc%d' % i, reads=[bOC[i]])

    def ffn_phase(tiles, src, dst, gpre, gpost, wg, wu, wd, bg=None):
        with ExitStack() as es2_:
            nonlocal es2
            es2 = es2_
            WT = 512 + NS
            BIG = sb('BIG', [128, DC, WT], F32); bBIG = Buf('BIG')
            XN = sb('XN', [128, DC, WT], BF16); bXN = Buf('XN')
            AT = sb('AT', [128, FC, WT], BF16); bAT = Buf('AT')
            WGU = [sb('wgu%d' % i, [128, 2, D], BF16) for i in range(4)]; bWGU = [Buf('wgu%d' % i) for i in range(4)]
            WD = [sb('wd%d' % i, [128, FF], BF16) for i in range(2)]; bWD = [Buf('wd%d' % i) for i in range(2)]
            SQ = [sb('sq%d' % i, [128, WT], BF16) for i in range(2)]; bSQ = [Buf('sq%d' % i) for i in range(2)]
            R1 = [sb('r1%d' % i, [128, WT], F32) for i in range(2)]; bR1 = [Buf('r1%d' % i) for i in range(2)]
            SG = [sb('sg%d' % i, [128, WT], F32) for i in range(2)]; bSG = [Buf('sg%d' % i) for i in range(2)]
            HC = [sb('hc%d' % i, [128, WT], F32) for i in range(4)]; bHC = [Buf('hc%d' % i) for i in range(4)]
            OC = [sb('oc%d' % i, [128, WT], F32) for i in range(2)]; bOC = [Buf('oc%d' % i) for i in range(2)]
            TMP2 = [sb('tmp%d' % i, [128, WT], F32) for i in range(1)] * 2; bTMP2 = [Buf('tmp0')] * 2
            SQB = [sb('sqb%d' % i, [128, WT], BF16) for i in range(2)]; bSQB = [Buf('sqb%d' % i) for i in range(2)]
            R1B = [sb('r1b%d' % i, [128, WT], F32) for i in range(2)]; bR1B = [Buf('r1b%d' % i) for i in range(2)]
            if bg:
                BS = [sb('bgS%d' % i, [128, 2048], F32) for i in range(2)]; bBS = [Buf('bgS%d' % i) for i in range(2)]
                BT = [sb('bgT%d' % i, [128, 2048], BF16) for i in range(1)]; bBT = [Buf('bgT%d' % i) for i in range(1)]
            bgn = [0]
            def bg_in(n):
                if n < len(bg):
                    sv, ne, civ, cov, siv, sov, dd = bg[n]
                    K.dma('gpsimd', siv(BS[n % 2]), sv, 'bgS%d' % (n % 2), writes=[bBS[n % 2]])
            def bg_step():
                n = bgn[0]; bgn[0] += 1
                if not bg or n > len(bg): return
                if n == 0:
                    bg_in(0); bg_in(1); return
                sv, ne, civ, cov, siv, sov, dd = bg[n - 1]; i = (n - 1) % 2
                K.cp('scalar', cov(BT[0]), civ(BS[i]), [bBS[i]], [bBT[0]])
                K.dma('gpsimd', dd, sov(BT[0]), 'bgT0', reads=[bBT[0]])
                bg_in(n + 1)
            def wgu_load(j):
                ws = j % 4
                K.dma('sync', WGU[ws][:, 0, :], wg[j, :, :], 'wgu%d' % ws, writes=[bWGU[ws]])
                K.dma('sync', WGU[ws][:, 1, :], wu[j, :, :], 'wgu%d' % ws, writes=[bWGU[ws]])
            def ld_chunk(tl, c):
                (c0_, N_, ex_) = tl; h = c % 4
                K.dma('sync', HC[h][:, :N_], src[c * 128:(c + 1) * 128, c0_:c0_ + N_], 'hc%d' % h, writes=[bHC[h]])
                if ex_:
                    K.dma('sync', HC[h][:, N_:N_ + ex_[1]], src[c * 128:(c + 1) * 128, ex_[0]:ex_[0] + ex_[1]], 'hc%d' % h, writes=[bHC[h]])
            for ti, (col0, N, ex) in enumerate(tiles):
                EN = ex[1] if ex else 0; W = N + EN
                nxt = tiles[ti + 1] if ti + 1 < len(tiles) else None
                if ti == 0:
                    K.dma('sync', BIG[:, :, :N], src[:, col0:col0 + N].rearrange("(c p) n -> p c n", p=128), 'big', writes=[bBIG])
                    if ex:
                        K.dma('sync', BIG[:, :, N:W], src[:, ex[0]:ex[0] + EN].rearrange("(c p) n -> p c n", p=128), 'big', writes=[bBIG])
                    for c in range(DC):
                        i = c % 2
                        K.act(SQ[i][:, :W], BIG[:, c, :W], AF.Square, [bBIG], [bSQ[i]])
                        K.mm(PS[6][:, :N], ones_bf[:, :], SQ[i][:, :N], c == 0, c == DC - 1, [bSQ[i], b_ones], [bPS[6]])
                        if ex:
                            K.mm(PS[7][:, :EN], ones_bf[:, :], SQ[i][:, N:W], c == 0, c == DC - 1, [bSQ[i], b_ones], [bPS[7]])
                    K.act(R1[0][:, :N], PS[6][:, :N], AF.Sqrt, [bPS[6], b_cst], [bR1[0]], scale=1.0 / D, bias=epsb)
                    if ex:
                        K.act(R1[0][:, N:W], PS[7][:, :EN], AF.Sqrt, [bPS[7], b_cst], [bR1[0]], scale=1.0 / D, bias=epsb)
                    K.recip(R1[1][:, :W], R1[0][:, :W], [bR1[0]], [bR1[1]])
                    for c in range(DC):
                        K.stt(XN[:, c, :W], BIG[:, c, :W], gv[:, gpre + c:gpre + c + 1], R1[1][:, :W], ALU.mult, ALU.mult,
                              [bBIG, bR1[1], b_gv], [bXN])
                    for j in range(4): wgu_load(j)
                for j in range(FC):
                    i = j % 2
                    ws = j % 4
                    for which in range(2):
                        pb = 2 * i + which
                        for kc in range(DC):
                            K.mm(PS[pb][:, :N], WGU[ws][:, which, kc * 128:(kc + 1) * 128], XN[:, kc, :N], kc == 0, kc == DC - 1,
                                 [bWGU[ws], bXN], [bPS[pb]])
                        if ex:
                            for kc in range(DC):
                                K.mm(PS[6 + i][:, which * 32:which * 32 + EN], WGU[ws][:, which, kc * 128:(kc + 1) * 128], XN[:, kc, N:W],
                                     kc == 0, kc == DC - 1, [bWGU[ws], bXN], [bPS[6 + i]])
                    K.act(SG[i][:, :N], PS[2 * i][:, :N], AF.Silu, [bPS[2 * i]], [bSG[i]])
                    K.tt('vector', AT[:, j, :N], SG[i][:, :N], PS[2 * i + 1][:, :N], ALU.mult, [bSG[i], bPS[2 * i + 1]], [bAT])
                    bg_step()
                    if j + 4 < FC: wgu_load(j + 4)
                    if ex:
                        K.act(SG[i][:, N:W], PS[6 + i][:, 0:EN], AF.Silu, [bPS[6 + i]], [bSG[i]])
                        K.tt('vector', AT[:, j, N:W], SG[i][:, N:W], PS[6 + i][:, 32:32 + EN], ALU.mult, [bSG[i], bPS[6 + i]], [bAT])
                pend = []
                for m in range(DC):
                    i = m % 2
                    K.dma('sync', WD[i][:, :], wd[m, :, :], 'wd%d' % i, writes=[bWD[i]])
                    for kc in range(FC):
                        K.mm(PS[4 + i][:, :N], WD[i][:, kc * 128:(kc + 1) * 128], AT[:, kc, :N], kc == 0, kc == FC - 1, [bWD[i], bAT], [bPS[4 + i]])
                    if ex:
                        for kc in range(FC):
                            K.mm(PS[i][:, :EN], WD[i][:, kc * 128:(kc + 1) * 128], AT[:, kc, N:W], kc == 0, kc == FC - 1, [bWD[i], bAT], [bPS[i]])
                    for p_ in pend: K.mm(*p_[0], **p_[1])
                    K.cp('vector', BIG[:, m, :N], PS[4 + i][:, :N], [bPS[4 + i]], [bBIG])
                    if ex:
                        K.cp('vector', BIG[:, m, N:W], PS[i][:, :EN], [bPS[i]], [bBIG])
                    K.act(SQ[i][:, :W], BIG[:, m, :W], AF.Square, [bBIG], [bSQ[i]])
                    pend = [((PS[7][:, :N], ones_bf[:, :], SQ[i][:, :N], m == 0, m == DC - 1), dict(reads=[bSQ[i], b_ones], writes=[bPS[7]]))]
                    if ex:
                        pend.append(((PS[2][:, :EN], ones_bf[:, :], SQ[i][:, N:W], m == 0, m == DC - 1), dict(reads=[bSQ[i], b_ones], writes=[bPS[2]])))
                    if nxt:
                        (nc0, nN, nex) = nxt; nEN = nex[1] if nex else 0; nW = nN + nEN
                        ld_chunk(nxt, m)
                        K.act(SQB[i][:, :nW], HC[m % 4][:, :nW], AF.Square, [bHC[m % 4]], [bSQB[i]])
                        pend.append(((PS[6][:, :nN], ones_bf[:, :], SQB[i][:, :nN], m == 0, m == DC - 1), dict(reads=[bSQB[i], b_ones], writes=[bPS[6]])))
                        if nex:
                            pend.append(((PS[3][:, :nEN], ones_bf[:, :], SQB[i][:, nN:nW], m == 0, m == DC - 1), dict(reads=[bSQB[i], b_ones], writes=[bPS[3]])))
                for p_ in pend: K.mm(*p_[0], **p_[1])
                if nxt:
                    K.act(R1B[0][:, :nN], PS[6][:, :nN], AF.Sqrt, [bPS[6], b_cst], [bR1B[0]], scale=1.0 / D, bias=epsb)
                    if nex:
                        K.act(R1B[0][:, nN:nW], PS[3][:, :nEN], AF.Sqrt, [bPS[3], b_cst], [bR1B[0]], scale=1.0 / D, bias=epsb)
                    K.recip(R1B[1][:, :nW], R1B[0][:, :nW], [bR1B[0]], [bR1B[1]])
                    ld_chunk(nxt, 0); ld_chunk(nxt, 1)
                    for c in range(DC):
                        if c + 2 < DC: ld_chunk(nxt, c + 2)
                        K.stt(XN[:, c, :nW], HC[c % 4][:, :nW], gv[:, gpre + c:gpre + c + 1], R1B[1][:, :nW], ALU.mult, ALU.mult,
                              [bHC[c % 4], bR1B[1], b_gv], [bXN])
                    for j in range(4): wgu_load(j)
                K.act(R1[0][:, :N], PS[7][:, :N], AF.Sqrt, [bPS[7], b_cst], [bR1[0]], scale=1.0 / D, bias=epsb)
                if ex:
                    K.act(R1[0][:, N:W], PS[2][:, :EN], AF.Sqrt, [bPS[2], b_cst], [bR1[0]], scale=1.0 / D, bias=epsb)
                K.recip(R1[1][:, :W], R1[0][:, :W], [bR1[0]], [bR1[1]])
                def ld_hc(c):
                    h = c % 4
                    K.dma('sync', HC[h][:, :N], src[c * 128:(c + 1) * 128, col0:col0 + N], 'hc%d' % h, writes=[bHC[h]])
                    if ex:
                        K.dma('sync', HC[h][:, N:W], src[c * 128:(c + 1) * 128, ex[0]:ex[0] + EN], 'hc%d' % h, writes=[bHC[h]])
                ld_hc(0); ld_hc(1)
                for c in range(DC):
                    i = c % 2; h = c % 4; TMP = TMP2[i]; bTMP = bTMP2[i]
                    if c + 2 < DC: ld_hc(c + 2)
                    K.stt(TMP[:, :W], BIG[:, c, :W], gv[:, gpost + c:gpost + c + 1], R1[1][:, :W], ALU.mult, ALU.mult,
                          [bBIG, bR1[1], b_gv], [bTMP])
                    K.stt(OC[i][:, :W], TMP[:, :W], 0.5, HC[h][:, :W], ALU.mult, ALU.add, [bTMP, bHC[h]], [bOC[i]])
                    K.dma('sync', dst[c * 128:(c + 1) * 128, col0:col0 + N], OC[i][:, :N], 'oc%d' % i, reads=[bOC[i]])
                    if ex:
                        K.dma('sync', dst[c * 128:(c + 1) * 128, ex[0]:ex[0] + EN], OC[i][:, N:W], 'oc%d' % i, reads=[bOC[i]])
            while bg and bgn[0] <= len(bg): bg_step()
            K.barrier(); K.emit()

    es2 = None
    TILES_ALL = [(1024, 512, None), (1536, 512, None), (0, 512, None), (512, 512, (2048, NS))]
    ffn_phase(TILES_ALL, xT, h1T, G_F1PRE, G_F1POST, wsc['w1_gate'], wsc['w1_up'], wsc['w1_down'], bg=jobs_bg)
    if stop <= 2: return nc, es, K, locals()

    def v4(ap):
        return ap.rearrange("p (b t) -> p b t", b=4)

    with ExitStack() as es2:
        BIG = sb('BIG', [128, DC, 512], F32); bBIG = Buf('BIG')
        HN = sb('HN', [128, DC, 512], BF16); bHN = Buf('HN')
        WKV = sb('WKV', [128, DC, 576], BF16); bWKV = Buf('WKV')
        WFM = [sb('wfm%d' % i, [128, D], BF16) for i in range(2)]; bWFM = [Buf('wfm%d' % i) for i in range(2)]
        WPL = sb('WPL', [128, 2048], BF16); bWPL = Buf('WPL')
        UE = [sb('ue%d' % i, [128, 4, 143], F32) for i in range(2)]; bUE = [Buf('ue%d' % i) for i in range(2)]
        PA = sb('PA', [128, 4, 143], F32); bPA = Buf('PA'); PB = sb('PB', [128, 4, 143], F32); bPB = Buf('PB')
        DT = sb('DT', [128, 8, 512], BF16); bDT = Buf('DT')
        CQ = sb('CQ', [128, 4, 512], F32); bCQ = Buf('CQ')
        CRAW = sb('CRAW', [128, 512], F32); bCRAW = Buf('CRAW'); JUNK = sb('JUNK', [128, 512], F32); bJUNK = Buf('JUNK')
        CKV = [sb('ckv%d' % i, [128, 512], F32) for i in range(2)]; bCKV = [Buf('ckv%d' % i) for i in range(2)]
        VST = [sb('vst%d' % i, [128, 512], BF16) for i in range(2)]; bVST = [Buf('vst%d' % i) for i in range(2)]
        KPE = [sb('kpe%d' % i, [128, 64], F32) for i in range(2)]; bKPE = [Buf('kpe%d' % i) for i in range(2)]
        KR = sb('KR', [128, 256], F32); bKR = Buf('KR')
        SSK = sb('SSK', [128, 4], F32); bSS = Buf('SSK')
        KTST = sb('KTST', [128, 5, 512], BF16); bKTST = Buf('KTST')
        COSK = sb('COSK', [128, 17, 64], F32); SINK = sb('SINK', [128, 17, 64], F32); GKVB = sb('GKVB', [128, 512], F32)
        bTAB = Buf('TAB')
        INVC = sb('INVC', [128, 4, 512], F32); bINVC = Buf('INVC')
        UTAIL = sb('UTAIL', [128, 8, 8, 15], F32); bUTAIL = Buf('UTAIL')
        ULAST = sb('ULAST', [128, 8, 16], F32); bULAST = Buf('ULAST')
        PLS = sb('PLS', [16, 1024], F32); bPLS = Buf('PLS')
        POS = [sb('pos%d' % i, [128, 512], BF16) for i in range(2)]; bPOS = [Buf('pos%d' % i) for i in range(2)]
        SQ = [sb('sq%d' % i, [128, 512], BF16) for i in range(2)]; bSQ = [Buf('sq%d' % i) for i in range(2)]
        R1 = [sb('r1%d' % i, [128, 512], F32) for i in range(2)]; bR1 = [Buf('r1%d' % i) for i in range(2)]
        HS = [sb('hs%d' % i, [120, 1024], F32) for i in range(2)]; bHS = [Buf('hs%d' % i) for i in range(2)]
        UEXS = sb('UEXS', [128, 8, 16, 16], F32); bUEXS = Buf('UEXS')
        SWS = sb('SWS', [128, 16], F32); bSWS = Buf('SWS')
        K.dma('sync', WKV[:, :, :], s_winkv.rearrange("p (a b) -> p a b", a=DC), 'wkv', writes=[bWKV])
        K.dma('sync', WPL[:, :], s_small['wpool'][:, :], 'wpl', writes=[bWPL])
        K.dma('sync', COSK[:, :, :], cosk[:, :, :], 'tab', writes=[bTAB])
        K.dma('sync', SINK[:, :, :], sink[:, :, :], 'tab', writes=[bTAB])
        K.dma('sync', GKVB[:, :], gkv_b[:, :], 'tab', writes=[bTAB])
        for h in range(2):
            K.dma('sync', HS[h][:, :], state_pool[8 * h:8 * h + 8, :, :].rearrange("b r c -> (b r) c"), 'hs%d' % h, writes=[bHS[h]])
            for m in range(8):
                pb = 3 + (m % 2)
                K.tr(PS[pb][:, 0:120], HS[h][:120, m * 128:(m + 1) * 128], identf[:120, :120], [bHS[h], b_identf], [bPS[pb]])
                K.cp('scalar', UEXS[:, m, 8 * h:8 * h + 8, 0:15], PS[pb][:, 0:120].rearrange("p (b r) -> p b r", b=8),
                     [bPS[pb]], [bUEXS])
        K.dma('sync', spool_hist[:, :, :], state_pool[:, 1:15, :], 'sph')
        P2T = [(1024, 512, 'partner', 0), (1536, 512, 'partner', 1), (0, 512, 'own', 0), (512, 512, 'own', 1), (2048, NS, 'sample', 0)]
        for (col0, N, kind, tb) in P2T:
            nblk = max(1, N // 128); nt = min(N, 128)
            ocol0 = 1024 if kind == 'sample' else tb * 512
            K.dma('sync', BIG[:, :, :N], h1T[:, col0:col0 + N].rearrange("(c p) n -> p c n", p=128), 'big', writes=[bBIG])
            prenorm(BIG, bBIG, HN, bHN, N, G_MIXPRE, SQ, bSQ, R1, bR1, PS[6], bPS[6])
            def kv_mm(b):
                pk, pr = (0, 1) if b % 2 == 0 else (5, 7)
                for kc in range(DC):
                    K.mm(PS[pk][:nt, :512], HN[:, kc, b * 128:b * 128 + nt], WKV[:, kc, 0:512], kc == 0, kc == DC - 1, [bHN, bWKV], [bPS[pk]])
                for kc in range(DC):
                    K.mm(PS[pr][:nt, 0:64], HN[:, kc, b * 128:b * 128 + nt], WKV[:, kc, 512:576], kc == 0, kc == DC - 1, [bHN, bWKV], [bPS[pr]])
            kv_mm(0)
            for b in range(nblk):
                gb = col0 // 128 + b; i = b % 2
                pk, pr = (0, 1) if b % 2 == 0 else (5, 7)
                if b + 1 < nblk: kv_mm(b + 1)
                K.cp('scalar', CRAW[:nt, :], PS[pk][:nt, :512], [bPS[pk]], [bCRAW])
                K.tt('vector', JUNK[:nt, :], CRAW[:nt, :], CRAW[:nt, :], ALU.mult, [bCRAW], [bJUNK])
                K.op('vector', lambda e, nt=nt: e.tensor_reduce(out=SSK[:nt, 0:1], in_=JUNK[:nt, :], axis=AX.X, op=ALU.add), [bJUNK], [bSS])
                K.act(SSK[:nt, 1:2], SSK[:nt, 0:1], AF.Sqrt, [bSS, b_cst], [bSS], scale=1.0 / KV, bias=cst[:nt, 1:2])
                K.recip(SSK[:nt, 2:3], SSK[:nt, 1:2], [bSS], [bSS])
                K.stt(CKV[i][:nt, :], CRAW[:nt, :], SSK[:nt, 2:3], GKVB[:nt, :], ALU.mult, ALU.mult, [bCRAW, bSS, bTAB], [bCKV[i]])
                K.dma('sync', kv_out[gb * 128:gb * 128 + nt, :], CKV[i][:nt, :], 'ckv%d' % i, reads=[bCKV[i]])
                K.cp('gpsimd', VST[i][:nt, :], CKV[i][:nt, :], [bCKV[i]], [bVST[i]])
                K.dma('sync', V_s[gb * 128:gb * 128 + nt, :], VST[i][:nt, :], 'vst%d' % i, reads=[bVST[i]])
                for cc in range(4):
                    K.tr(PS[2][:, cc * 128:cc * 128 + nt], CKV[i][:nt, cc * 128:(cc + 1) * 128], identf[:nt, :nt],
                         [bCKV[i], b_identf], [bPS[2]])
                K.cp('scalar', KTST[:, 0:4, b * 128:b * 128 + nt], v4(PS[2][:, :])[:, :, :nt], [bPS[2]], [bKTST])
                K.cp('scalar', KR[:nt, 0:64], PS[pr][:nt, 0:64], [bPS[pr]], [bKR])
                K.cp('scalar', KR[:nt, 64:96], PS[pr][:nt, 32:64], [bPS[pr]], [bKR])
                K.cp('scalar', KR[:nt, 96:128], PS[pr][:nt, 0:32], [bPS[pr]], [bKR])
                K.tt('vector', KR[:nt, 128:192], KR[:nt, 0:64], COSK[:nt, gb, :], ALU.mult, [bKR, bTAB], [bKR])
                K.tt('vector', KR[:nt, 192:256], KR[:nt, 64:128], SINK[:nt, gb, :], ALU.mult, [bKR, bTAB], [bKR])
                K.tt('vector', KPE[i][:nt, :], KR[:nt, 128:192], KR[:nt, 192:256], ALU.add, [bKR], [bKPE[i]])
                K.dma('sync', kpe_out[gb * 128:gb * 128 + nt, :], KPE[i][:nt, :], 'kpe%d' % i, reads=[bKPE[i]])
                K.tr(PS[pr][0:64, 128:128 + nt], KPE[i][:nt, 0:64], identf[:nt, :nt], [bKPE[i], b_identf], [bPS[pr]])
                K.cp('scalar', KTST[0:64, 4, b * 128:b * 128 + nt], PS[pr][0:64, 128:128 + nt], [bPS[pr]], [bKTST])
            K.dma('sync', KT_s[0:512, col0:col0 + N].rearrange("(c p) n -> p c n", p=128), KTST[:, 0:4, :N], 'ktst', reads=[bKTST])
            K.dma('sync', KT_s[512:576, col0:col0 + N], KTST[0:64, 4, :N], 'ktst', reads=[bKTST])
            if kind != 'partner':
                tsel = 2 if kind == 'sample' else tb
                K.dma('sync', INVC[:, :, :], invc[:, tsel, :, :], 'invc', writes=[bINVC])
            pms = []
            for m in range(8):
                i = m % 2; pu = PS[3 + i]; bpu = bPS[3 + i]; g = m // 2; w = (2, 4, 8, 16)[g]
                K.dma('sync', WFM[i][:, :], s_winfm[m, :, :], 'wfm%d' % i, writes=[bWFM[i]])
                for kc in range(DC):
                    K.mm(pu[:, :N], WFM[i][:, kc * 128:(kc + 1) * 128], HN[:, kc, :N], kc == 0, kc == DC - 1, [bWFM[i], bHN], [bpu])
                if kind == 'partner':
                    K.cp('scalar', UTAIL[:, m, tb * 4:(tb + 1) * 4, :], v4(pu[:, :])[:, :, 113:128], [bpu], [bUTAIL])
                    continue
                if kind == 'own':
                    U = UE[i]; bU = bUE[i]
                    K.cp('scalar', U[:, :, 15:143], v4(pu[:, :]), [bpu], [bU])
                    K.cp('gpsimd', U[:, :, 0:15], UTAIL[:, m, tb * 4:(tb + 1) * 4, :], [bUTAIL], [bU])
                    K.tt('gpsimd', PA[:, :, 1:143], U[:, :, 1:143], U[:, :, 0:142], ALU.add, [bU], [bPA]); cur, bcur = PA, bPA
                    if w >= 4:
                        K.tt('gpsimd', PB[:, :, 3:143], PA[:, :, 3:143], PA[:, :, 1:141], ALU.add, [bPA], [bPB]); cur, bcur = PB, bPB
                    if w >= 8:
                        K.tt('gpsimd', PA[:, :, 7:143], PB[:, :, 7:143], PB[:, :, 3:139], ALU.add, [bPB], [bPA]); cur, bcur = PA, bPA
                    if w >= 16:
                        K.tt('gpsimd', PB[:, :, 15:143], PA[:, :, 15:143], PA[:, :, 7:135], ALU.add, [bPA], [bPB]); cur, bcur = PB, bPB
                    K.tt('gpsimd', cur[:, :, 15:143], cur[:, :, 15:143], v4(INVC[:, g, :]), ALU.mult, [bcur, bINVC], [bcur])
                    K.tt('gpsimd', v4(DT[:, m, :]), cur[:, :, 15:143], U[:, :, 15:143], ALU.subtract, [bcur, bU], [bDT])
                    if tb == 1:
                        K.cp('gpsimd', ULAST[:, m, :], U[:, 3, 127:143], [bU], [bULAST])
                else:
                    K.cp('scalar', UEXS[:, m, :, 15], pu[:, :NS], [bpu], [bUEXS])
                    K.op('vector', lambda e, m=m, w=w: e.tensor_reduce(out=SWS[:, :], in_=UEXS[:, m, :, 16 - w:16], axis=AX.X, op=ALU.add),
                         [bUEXS], [bSWS])
                    K.stt(DT[:, m, :NS], SWS[:, :], 1.0 / w, UEXS[:, m, :, 15], ALU.mult, ALU.subtract, [bSWS, bUEXS], [bDT])
                    K.cp('gpsimd', ULAST[:, m, :], UEXS[:, m, :, 15], [bUEXS], [bULAST])
                if m % 2 == 1:
                  def pm(g=g):
                    for dd in range(2):
                          for cc in range(2):
                              k = g * 2 + cc
                              K.mm(PS[5][:, :N], WPL[:, k * 256 + dd * 128:k * 256 + (dd + 1) * 128], DT[:, 2 * g + cc, :N], cc == 0, cc == 1,
                                   [bWPL, bDT], [bPS[5]])
                          mo = 2 * g + dd
                          K.ts('vector', POS[dd][:, :N], PS[5][:, :N], gv[:, G_PSC + mo:G_PSC + mo + 1], ALU.mult, [bPS[5], b_gv], [bPOS[dd]])
                          K.dma('sync', mixT[mo * 128:(mo + 1) * 128, ocol0:ocol0 + N], POS[dd][:, :N], 'pos%d' % dd, reads=[bPOS[dd]])
                  pms.append(pm)
                  if len(pms) > 1: pms.pop(0)()
            if kind == 'partner':
                continue
            if kind == 'sample' or tb == 1:
                for m in range(8):
                    pb = 7 if m < 4 else 5
                    K.tr(PS[pb][:16, (m % 4) * 128:(m % 4 + 1) * 128], ULAST[:, m, :], identf[:, :], [bULAST, b_identf], [bPS[pb]])
                K.cp('scalar', PLS[:16, 0:512], PS[7][:16, :512], [bPS[7]], [bPLS])
                K.cp('scalar', PLS[:16, 512:1024], PS[5][:16, :512], [bPS[5]], [bPLS])
                K.dma('sync', (spool_new if kind == 'sample' else pool_last)[:, :], PLS[:16, :], 'pls', reads=[bPLS])
            for mq in range(4):
                m = 8 + mq; i = m % 2; pu = PS[3 + i]; bpu = bPS[3 + i]
                K.dma('sync', WFM[i][:, :], s_winfm[m, :, :], 'wfm%d' % i, writes=[bWFM[i]])
                for kc in range(DC):
                    K.mm(pu[:, :N], WFM[i][:, kc * 128:(kc + 1) * 128], HN[:, kc, :N], kc == 0, kc == DC - 1, [bWFM[i], bHN], [bpu])
                K.cp('scalar', CQ[:, mq, :N], pu[:, :N], [bpu], [bCQ])
            while pms: pms.pop(0)()
            prenorm(CQ, bCQ, CQ, bCQ, N, G_Q, SQ, bSQ, R1, bR1, PS[6], bPS[6], nchunk=4, dim=QL)
            K.dma('sync', cqn_s[:, ocol0:ocol0 + N].rearrange("(c p) n -> p c n", p=128), CQ[:, :, :N], 'cq', reads=[bCQ])
        K.barrier(); K.emit()
    if stop <= 3: return nc, es, K, locals()

    with ExitStack() as es2:
        KT = sb('KT', [128, 5, 2048], BF16); V = sb('V', [128, 16, 512], BF16); bKT = Buf('KT'); bV = Buf('V')
        KTN = sb('KTN', [128, 5, NS], BF16); VN = sb('VN', [NS, 512], BF16)
        WQN = sb('WQN', [128, 4096], BF16); WQR = sb('WQR', [128, 2048], BF16); WQX = sb('WQX', [128, 2048], BF16)
        WUK = sb('WUK', [128, 4096], BF16); WUV = sb('WUV', [128, 4096], BF16); bW = Buf('Wq')
        CQL = sb('CQL', [128, 4, 128], F32); bCQL = Buf('CQL'); CQB = sb('CQB', [128, 4, 128], BF16); bCQB = Buf('CQB')
        QN = sb('QN', [128, 8, 128], BF16); bQN = Buf('QN')
        QCAT = sb('QCAT', [128, 5, 1024], BF16); bQCAT = Buf('QCAT')
        COSQ = sb('COSQ', [128, 1024 + NS], F32); SINQ = sb('SINQ', [128, 1024 + NS], F32); TRI = sb('TRI', [128, 128], F32); bTQ = Buf('TQ')
        T1 = sb('T1', [128, 128], F32); T2 = sb('T2', [128, 128], F32); bT1 = Buf('T1'); bT2 = Buf('T2')
        PT = [sb('pt%d' % i, [128, 512], BF16) for i in range(2)]; bPT = [Buf('pt%d' % i) for i in range(2)]
        ACC = sb('ACC', [128, 1024], F32); bACC = Buf('ACC'); RINV = sb('RINV', [128, 512], F32); bRINV = Buf('RINV')
        OT = sb('OT', [128, 4, 1024], BF16); bOT = Buf('OT'); AST = sb('AST', [128, 8, 512], BF16); bAST = Buf('AST')
        PTAB = sb('PTAB', [128, NS * NPG // 4], I32); bPTAB = Buf('PTAB')
        IDX = sb('IDX', [128, NS * NPG // 4], I32); bIDX = Buf('IDX')
        K.dma('sync', KT[:, 0:4, :], KT_s[0:512, 0:2048].rearrange("(c p) n -> p c n", p=128), 'kt', writes=[bKT])
        K.dma('sync', KT[0:64, 4, :], KT_s[512:576, 0:2048], 'kt', writes=[bKT])
        K.dma('sync', V[:, :, :], V_s[0:2048, :].rearrange("(k p) c -> p k c", p=128), 'v', writes=[bV])
        K.dma('sync', KTN[:, 0:4, :], KT_s[0:512, 2048:2048 + NS].rearrange("(c p) n -> p c n", p=128), 'kt', writes=[bKT])
        K.dma('sync', KTN[0:64, 4, :], KT_s[512:576, 2048:2048 + NS], 'kt', writes=[bKT])
        K.dma('sync', VN[:, :], V_s[2048:2048 + NS, :], 'v', writes=[bV])
        for t_, k_ in ((WQN, 'wqn'), (WQR, 'wqr'), (WQX, 'wqx'), (WUK, 'wuk'), (WUV, 'wuv')):
            K.dma('sync', t_[:, :], s_small[k_][:, :], 'wq', writes=[bW])
        K.dma('sync', COSQ[:, :], cosq[:, :], 'tq', writes=[bTQ]); K.dma('sync', SINQ[:, :], sinq[:, :], 'tq', writes=[bTQ])
        K.dma('sync', TRI[:, :], tri_in[:, :], 'tq', writes=[bTQ])
        K.dma('sync', PTAB[:, :], ptab[:, :], 'ptab', writes=[bPTAB])
        K.ts('vector', IDX[:, :], PTAB[:, :], 32.0, ALU.mult, [bPTAB, b_cst], [bIDX], s2=cst[:, 2:3], op1=ALU.add)

        def qpath(c0, nq, qcat_view):
            K.dma('sync', CQL[:, :, :nq], cqn_s[:, c0:c0 + nq].rearrange("(c p) n -> p c n", p=128), 'cql', writes=[bCQL])
            K.cp('vector', CQB[:, :, :nq], CQL[:, :, :nq], [bCQL], [bCQB])
            for h in range(NH):
                for kc in range(4):
                    K.mm(PS[6][:, :nq], WQN[:, kc * 1024 + h * 128:kc * 1024 + (h + 1) * 128], CQB[:, kc, :nq], kc == 0, kc == 3, [bW, bCQB], [bPS[6]])
                K.cp('scalar', QN[:, h, :nq], PS[6][:, :nq], [bPS[6]], [bQN])
            for h in range(NH):
                for kc in range(4):
                    K.mm(PS[6][0:64, :nq], WQR[:, kc * 512 + h * 64:kc * 512 + (h + 1) * 64], CQB[:, kc, :nq], kc == 0, kc == 3, [bW, bCQB], [bPS[6]])
                for kc in range(4):
                    K.mm(PS[7][0:64, :nq], WQX[:, kc * 512 + h * 64:kc * 512 + (h + 1) * 64], CQB[:, kc, :nq], kc == 0, kc == 3, [bW, bCQB], [bPS[7]])
                K.tt('vector', T1[0:64, :nq], PS[6][0:64, :nq], COSQ[0:64, c0:c0 + nq], ALU.mult, [bPS[6], bTQ], [bT1])
                K.tt('vector', T2[0:64, :nq], PS[7][0:64, :nq], SINQ[0:64, c0:c0 + nq], ALU.mult, [bPS[7], bTQ], [bT2])
                K.tt('vector', qcat_view(4, h)[0:64, :], T1[0:64, :nq], T2[0:64, :nq], ALU.add, [bT1, bT2], [bQCAT])
            for h in range(NH):
                for cc in range(4):
                    K.mm(PS[6][:, cc * 128:cc * 128 + nq], WUK[:, h * 512 + cc * 128:h * 512 + (cc + 1) * 128], QN[:, h, :nq], True, True,
                         [bW, bQN], [bPS[6]])
                for cc in range(4):
                    K.cp('scalar' if cc % 2 == 0 else 'vector', qcat_view(cc, h), PS[6][:, cc * 128:cc * 128 + nq], [bPS[6]], [bQCAT])

        for i in range(8):
            tb = i // 4
            qpath(i * 128, 128, lambda ch, h: QCAT[:, ch, h * 128:(h + 1) * 128])
            kbs = list(range(0, i + 1)) + list(range(8, 8 + i + 1))
            for half in range(2):
                hs = slice(half * 512, (half + 1) * 512)
                def st(n, kb):
                    ps = PS[n % 2]; bps = bPS[n % 2]; pt = PT[n % 2]; bpt = bPT[n % 2]
                    for ch in range(5):
                        rows = 128 if ch < 4 else 64
                        K.mm(ps[:, :512], KT[:rows, ch, kb * 128:(kb + 1) * 128], QCAT[:rows, ch, hs], ch == 0, ch == 4, [bKT, bQCAT], [bps])
                    K.act(pt[:, :], ps[:, :512], AF.Exp, [bps], [bpt], scale=SM_SCALE)
                    if kb == i:
                        for hh in range(4):
                            K.tt('vector', pt[:, hh * 128:(hh + 1) * 128], pt[:, hh * 128:(hh + 1) * 128], TRI[:, :], ALU.mult, [bpt, bTQ], [bpt])
                    if kb == 8:
                        K.ts('vector', pt[:, :], pt[:, :], cst[:, 0:1], ALU.mult, [bpt, b_cst], [bpt])
                    if n == 0:
                        K.cp('gpsimd', ACC[:, hs], pt[:, :], [bpt], [bACC])
                    else:
                        K.tt('gpsimd', ACC[:, hs], ACC[:, hs], pt[:, :], ALU.add, [bpt, bACC], [bACC])
                def pv(n, kb):
                    pt = PT[n % 2]; bpt = bPT[n % 2]
                    for cc in range(4):
                        K.mm(PS[2 + cc][:, :512], V[:, kb, cc * 128:(cc + 1) * 128], pt[:, :], n == 0, n == len(kbs) - 1, [bV, bpt], [bPS[2 + cc]])
                st(0, kbs[0])
                for n, kb in enumerate(kbs):
                    if n + 1 < len(kbs): st(n + 1, kbs[n + 1])
                    pv(n, kb)
                K.mm(PS[6][:, :512], ones_f[:, :], ACC[:, hs], True, True, [b_ones, bACC], [bPS[6]])
                K.recip(RINV[:, :], PS[6][:, :512], [bPS[6]], [bRINV])
                for cc in range(4):
                    K.tt('vector', OT[:, cc, hs], PS[2 + cc][:, :512], RINV[:, :], ALU.mult, [bPS[2 + cc], bRINV], [bOT])
            for h in range(NH):
                for cc in range(4):
                    K.mm(PS[7][:, :128], WUV[:, cc * 1024 + h * 128:cc * 1024 + (h + 1) * 128], OT[:, cc, h * 128:(h + 1) * 128], cc == 0, cc == 3,
                         [bW, bOT], [bPS[7]])
                K.cp('scalar', AST[:, h, (i % 4) * 128:(i % 4 + 1) * 128], PS[7][:, :128], [bPS[7]], [bAST])
            if i % 4 == 3:
                K.dma('sync', mixT[1024:2048, tb * 512:(tb + 1) * 512].rearrange("(h p) n -> p h n", p=128), AST[:, :, :], 'ast', reads=[bAST])

        qpath(1024, NS, lambda ch, h: QCAT[:, ch, h * NS:(h + 1) * NS])
        GP = 4; NG = NPG // GP; NGT = NS * NG
        KPV = [sb('kpv%d' % i, [128, GP, 512], F32) for i in range(3)]; bKPV = [Buf('kpv%d' % i) for i in range(3)]
        KPR = [sb('kpr%d' % i, [128, GP, 64], F32) for i in range(3)]; bKPR = [Buf('kpr%d' % i) for i in range(3)]
        KTP = [sb('ktp%d' % i, [128, 4, 128], BF16) for i in range(8)]; bKTP = [Buf('ktp%d' % i) for i in range(8)]
        KTR = [sb('ktr%d' % i, [64, GP * 128], BF16) for i in range(2)]; bKTR = [Buf('ktr%d' % i) for i in range(2)]
        VP = [sb('vp%d' % i, [128, 512], BF16) for i in range(12)]; bVP = [Buf('vp%d' % i) for i in range(12)]
        PTS = [sb('pts%d' % i, [128, GP * 8], BF16) for i in range(2)]; bPTS = [Buf('pts%d' % i) for i in range(2)]
        PTN = sb('PTN', [NS, 8], BF16); bPTN = Buf('PTN')
        ACCS2 = [sb('ACCS%d' % i, [128, GP * 8], F32) for i in range(2)]; bACCS2 = [Buf('ACCS%d' % i) for i in range(2)]; RINVS = sb('RINVS', [8, 1], F32); bRINVS = Buf('RINVS')
        OS = sb('OS', [8, 512], F32); bOS = Buf('OS')
        OTS = sb('OTS', [128, 4, NH * NS], BF16); bOTS = Buf('OTS')
        ckv4 = cache_kv.rearrange("(r j) c -> r (j c)", j=GP); ckr4 = cache_kr.rearrange("(r j) c -> r (j c)", j=GP)
        def qrhs(ch, rows, b):
            return QCAT[:rows, ch, 0:NH * NS].rearrange("p (h b) -> p h b", b=NS)[:, :, b]
        def stageA(G):
            sl = G % 3
            K.op('gpsimd', lambda e, sl=sl, G=G: e.indirect_dma_start(
                out=KPV[sl][:, :, :].rearrange("p j c -> p (j c)"), out_offset=None, in_=ckv4[:, :],
                in_offset=bass.IndirectOffsetOnAxis(ap=IDX[:, G:G + 1], axis=0)), [bIDX], [bKPV[sl]], dsem='kpv%d' % sl)
            K.op('gpsimd', lambda e, sl=sl, G=G: e.indirect_dma_start(
                out=KPR[sl][:, :, :].rearrange("p j c -> p (j c)"), out_offset=None, in_=ckr4[:, :],
                in_offset=bass.IndirectOffsetOnAxis(ap=IDX[:, G:G + 1], axis=0)), [bIDX], [bKPR[sl]], dsem='kpr%d' % sl)
        def stageB(G):
            sl = G % 3
            for j in range(GP):
                u = (G * GP + j) % 8; pb = 3 + (u % 4)
                for ch in range(4):
                    K.tr(PS[pb][:, ch * 128:(ch + 1) * 128], KPV[sl][:, j, ch * 128:(ch + 1) * 128], identf[:, :], [bKPV[sl], b_identf], [bPS[pb]])
                K.cp('scalar', KTP[u][:, :, :], v4(PS[pb][:, :]), [bPS[pb]], [bKTP[u]])
                K.tr(PS[7][0:64, j * 128:(j + 1) * 128], KPR[sl][:, j, :], identf[:, :], [bKPR[sl], b_identf], [bPS[7]])
                uv = (G * GP + j) % 12
                K.cp('vector', VP[uv][:, :], KPV[sl][:, j, :], [bKPV[sl]], [bVP[uv]])
            K.cp('vector', KTR[G % 2][0:64, :], PS[7][0:64, :512], [bPS[7]], [bKTR[G % 2]])
        def stageC1(G):
            b = G // NG; g = G % NG; ps = PS[G % 2]; bps = bPS[G % 2]; pts = PTS[G % 2]; bpts = bPTS[G % 2]
            ACCS = ACCS2[b % 2]; bACCS = bACCS2[b % 2]
            for j in range(GP):
                u = (G * GP + j) % 8
                for ch in range(5):
                    if ch < 4:
                        K.mm(ps[:, j * 8:(j + 1) * 8], KTP[u][:, ch, :], qrhs(ch, 128, b), ch == 0, False, [bKTP[u], bQCAT], [bps])
                    else:
                        K.mm(ps[:, j * 8:(j + 1) * 8], KTR[G % 2][0:64, j * 128:(j + 1) * 128], qrhs(4, 64, b), False, True, [bKTR[G % 2], bQCAT], [bps])
            K.act(pts[:, :], ps[:, 0:GP * 8], AF.Exp, [bps], [bpts], scale=SM_SCALE)
            if g == 0:
                K.cp('gpsimd', ACCS[:, :], pts[:, :], [bpts], [bACCS])
            else:
                K.tt('gpsimd', ACCS[:, :], ACCS[:, :], pts[:, :], ALU.add, [bpts, bACCS], [bACCS])
            if g == NG - 1:
                for ch in range(5):
                    rows = 128 if ch < 4 else 64
                    K.mm(ps[:NS, 40:48], KTN[:rows, ch, :], qrhs(ch, rows, b), ch == 0, ch == 4, [bKT, bQCAT], [bps])
                K.act(PTN[:, :], ps[:NS, 40:48], AF.Exp, [bps], [bPTN], scale=SM_SCALE)
                K.ts('vector', PTN[:, :], PTN[:, :], identf[:NS, b:b + 1], ALU.mult, [bPTN, b_identf], [bPTN])
                K.tt('gpsimd', ACCS[:NS, 0:8], ACCS[:NS, 0:8], PTN[:, :], ALU.add, [bPTN, bACCS], [bACCS])
        def stageC2(G):
            b = G // NG; g = G % NG; ps = PS[G % 2]; bps = bPS[G % 2]; pts = PTS[G % 2]; bpts = bPTS[G % 2]
            ACCS = ACCS2[b % 2]; bACCS = bACCS2[b % 2]
            for j in range(GP):
                uv = (G * GP + j) % 12
                K.mm(PS[2][0:8, :512], pts[:, j * 8:(j + 1) * 8], VP[uv][:, :], g == 0 and j == 0, False, [bpts, bVP[uv]], [bPS[2]])
            if g == NG - 1:
                K.mm(PS[2][0:8, :512], PTN[:, :], VN[:NS, :], False, True, [bV, bPTN], [bPS[2]])
                for j in range(GP):
                    K.mm(ps[0:8, 56:57], ACCS[:, j * 8:(j + 1) * 8], ones_f[:, 0:1], j == 0, j == GP - 1, [b_ones, bACCS], [bps])
                K.recip(RINVS[:, :], ps[0:8, 56:57], [bps], [bRINVS])
                K.ts('vector', OS[:, :], PS[2][0:8, :512], RINVS[:, 0:1], ALU.mult, [bPS[2], bRINVS], [bOS])
                for cc in range(4):
                    K.tr(ps[:, 64 + cc * 8:64 + (cc + 1) * 8], OS[0:8, cc * 128:(cc + 1) * 128], identf[0:8, 0:8], [bOS, b_identf], [bps])
                K.cp('vector', OTS[:, :, :].rearrange("p c (h b) -> p c h b", b=NS)[:, :, :, b],
                     ps[:, 64:96].rearrange("p (c h) -> p c h", c=4), [bps], [bOTS])
        for G in range(NGT + 3):
            if G < NGT: stageA(G)
            if 1 <= G <= NGT: stageB(G - 1)
            if 2 <= G <= NGT + 1: stageC1(G - 2)
            if G >= 3: stageC2(G - 3)
        for h in range(NH):
            for cc in range(4):
                K.mm(PS[7][:, :NS], WUV[:, cc * 1024 + h * 128:cc * 1024 + (h + 1) * 128], OTS[:, cc, h * NS:(h + 1) * NS], cc == 0, cc == 3,
                     [bW, bOTS], [bPS[7]])
            K.cp('scalar', AST[:, h, :NS], PS[7][:, :NS], [bPS[7]], [bAST])
        K.dma('sync', mixT[1024:2048, 1024:1024 + NS].rearrange("(h p) n -> p h n", p=128), AST[:, :, :NS], 'ast', reads=[bAST])
        K.barrier(); K.emit()
    if stop <= 4: return nc, es, K, locals()

    with ExitStack() as es2:
        BIG = sb('BIG', [128, DC, 512], F32); bBIG = Buf('BIG')
        MIXS = [sb('MIX%d' % i, [128, DC, 512], BF16) for i in range(2)]; bMIXS = [Buf('MIX%d' % i) for i in range(2)]
        WS = [sb('wo%d' % i, [128, D], BF16) for i in range(2)]; bWS = [Buf('wo%d' % i) for i in range(2)]
        SQ = [sb('sq%d' % i, [128, 512], BF16) for i in range(2)]; bSQ = [Buf('sq%d' % i) for i in range(2)]
        R1 = [sb('r1%d' % i, [128, 512], F32) for i in range(2)]; bR1 = [Buf('r1%d' % i) for i in range(2)]
        HC = [sb('hc%d' % i, [128, 512], F32) for i in range(4)]; bHC = [Buf('hc%d' % i) for i in range(4)]
        OC = [sb('oc%d' % i, [128, 512], F32) for i in range(2)]; bOC = [Buf('oc%d' % i) for i in range(2)]
        TMP = sb('tmp', [128, 512], F32); bTMP = Buf('tmp')
        P5T = ((0, 512, 0), (512, 512, 512), (2048, NS, 1024))
        def ld_mix(t):
            (scol0, N, dcol0) = P5T[t]
            K.dma('sync', MIXS[t % 2][:, :, :N], mixT[:, dcol0:dcol0 + N].rearrange("(c p) n -> p c n", p=128), 'mix%d' % (t % 2), writes=[bMIXS[t % 2]])
        ld_mix(0)
        for t, (scol0, N, dcol0) in enumerate(P5T):
            MIX = MIXS[t % 2]; bMIX = bMIXS[t % 2]
            linear_to_big(MIX, bMIX, DC, s_wo, DC, WS, bWS, 'wo', BIG, bBIG, N, SQ, bSQ, [PS[4], PS[5]], [bPS[4], bPS[5]], PS[7], bPS[7])
            if t + 1 < len(P5T): ld_mix(t + 1)
            post_residual(BIG, bBIG, N, G_MIXPOST, 1.0, R1, bR1, PS[7], bPS[7], h1T, h2T, scol0, HC, bHC, OC, bOC, TMP, bTMP, dcol0=dcol0)
        K.barrier(); K.emit()
    if stop <= 5: return nc, es, K, locals()

    ffn_phase([(0, 512, None), (512, 512, (1024, NS))], h2T, h3T, G_F2PRE, G_F2POST, wsc['w2_gate'], wsc['w2_up'], wsc['w2_down'])

    with ExitStack() as es2:
        HB = [sb('hb%d' % i, [128, DC, 128], F32) for i in range(3)]; bHB = [Buf('hb%d' % i) for i in range(3)]
        YB = [sb('yb%d' % i, [128, D], F32) for i in range(2)]; bYB = [Buf('yb%d' % i) for i in range(2)]
        def ld_hb(blk):
            nt = 128 if blk < 8 else NS; h = blk % 3
            K.dma('sync', HB[h][:, :, :nt], h3T[:, blk * 128:blk * 128 + nt].rearrange("(c p) n -> p c n", p=128), 'hb%d' % h, writes=[bHB[h]])
        ld_hb(0); ld_hb(1)
        for blk in range(9):
            nt = 128 if blk < 8 else NS; i = blk % 2; h = blk % 3
            if blk + 2 < 9: ld_hb(blk + 2)
            for g in range(4):
                pb = (blk * 4 + g) % 8
                for c4 in range(4):
                    c = g * 4 + c4
                    K.tr(PS[pb][:nt, c4 * 128:(c4 + 1) * 128], HB[h][:, c, :nt], identf[:, :], [bHB[h], b_identf], [bPS[pb]])
                K.cp('scalar' if g % 2 == 0 else 'vector', YB[i][:nt, g * 512:(g + 1) * 512], PS[pb][:nt, :512], [bPS[pb]], [bYB[i]])
            K.dma('sync', y_out[blk * 128:blk * 128 + nt, :], YB[i][:nt, :], 'yb%d' % i, reads=[bYB[i]])
        K.barrier(); K.op('sync', None); K.emit()
    return nc, es, K, locals()


def _rope_tables(pos):
    half = R // 2
    inv = (10000.0 ** (-np.arange(half, dtype=np.float32) / half)).astype(np.float32)
    ang = pos.astype(np.float32)[:, None] * inv[None, :]
    return np.cos(ang).astype(np.float32), np.sin(ang).astype(np.float32)


_CACHE = {}


def kernel(x_prompt, x_sample, cache_kv_latent, cache_k_rope, state_pool, page_table,
           g_ffn1_pre, w1_gate, w1_up, w1_down, g_ffn1_post,
           g_mix_pre, w_in, w_pool, pool_scale, g_q, w_uq, g_kv, w_uk, w_uv, w_out, g_mix_post,
           g_ffn2_pre, w2_gate, w2_up, w2_down, g_ffn2_post):
    global NPHYS
    f = lambda a: np.ascontiguousarray(np.asarray(a, dtype=np.float32))
    NPHYS = int(cache_kv_latent.shape[1])
    if 'nc' not in _CACHE:
        nc, es, K, L = build_program()
        es.close()
        _CACHE['nc'] = nc
    nc = _CACHE['nc']
    x_prompt = f(x_prompt); x_sample = f(x_sample)
    w_uq_ = f(w_uq)[0].reshape(QL, NH, 192)
    rot = np.concatenate([np.arange(32, 64), np.arange(0, 32)])
    def cols(v):
        v = f(v).reshape(-1)
        return v.reshape(-1, 128).T
    gvec = np.concatenate([cols(g_ffn1_pre), cols(g_ffn1_post), cols(g_mix_pre), cols(g_mix_post), cols(g_ffn2_pre), cols(g_ffn2_post),
                           cols(pool_scale), cols(g_q)], axis=1)
    shared = {
        'cache_kv': f(cache_kv_latent)[0].reshape(NPHYS * PAGE, KV), 'cache_kr': f(cache_k_rope)[0].reshape(NPHYS * PAGE, R),
        'w1_gate': f(w1_gate)[0], 'w1_up': f(w1_up)[0], 'w1_down': f(w1_down)[0],
        'w2_gate': f(w2_gate)[0], 'w2_up': f(w2_up)[0], 'w2_down': f(w2_down)[0],
        'w_in': f(w_in)[0], 'w_out': f(w_out)[0], 'w_pool': f(w_pool)[0].reshape(1024, 256),
        'wq_nope': np.ascontiguousarray(w_uq_[:, :, :128].reshape(QL, 1024)),
        'wq_rope': np.ascontiguousarray(w_uq_[:, :, 128:].reshape(QL, 512)),
        'wq_rot': np.ascontiguousarray(w_uq_[:, :, 128:][:, :, rot].reshape(QL, 512)),
        'wukT': np.ascontiguousarray(f(w_uk)[0].transpose(2, 1, 0).reshape(128, NH * KV)),
        'w_uv': f(w_uv)[0].reshape(KV, NH * 128),
        'gvec': np.ascontiguousarray(gvec), 'gkv_b': np.ascontiguousarray(np.broadcast_to(f(g_kv).reshape(1, KV), (128, KV))),
        'tri': np.triu(np.ones((128, 128), np.float32)), 'ident': np.eye(128, dtype=np.float32),
    }
    pt = np.asarray(page_table).astype(np.int32)
    sp = f(state_pool)[0]
    wins = (2, 4, 8, 16)
    in_maps = []
    for c in range(8):
        s_, r_ = c // 2, c % 2
        own = [2 * j + r_ for j in range(8)]
        partner = [2 * j + 1 - r_ for j in range(8)] if r_ == 1 else [-1] + [2 * j - 1 for j in range(1, 8)]
        xl = np.zeros((NLOC, D), np.float32); pos = np.zeros(NLOC, np.int64)
        for li, gbk in enumerate(own + partner):
            if gbk >= 0:
                xl[li * 128:(li + 1) * 128] = x_prompt[s_, gbk * 128:(gbk + 1) * 128]
                pos[li * 128:(li + 1) * 128] = np.arange(gbk * 128, (gbk + 1) * 128)
        xl[2048:] = x_sample[c * NS:(c + 1) * NS, 0]
        pos[2048:] = pt.shape[1] * PAGE
        cs, sn = _rope_tables(pos)
        ck = np.zeros((17 * 128, R), np.float32); sk = np.zeros((17 * 128, R), np.float32)
        ck[:NLOC] = np.concatenate([cs, cs], 1); sk[:NLOC] = np.concatenate([-sn, sn], 1)
        qsel = np.concatenate([np.arange(0, 1024), np.arange(2048, NLOC)])
        cq = np.zeros((128, 1024 + NS), np.float32); sq_ = np.zeros((128, 1024 + NS), np.float32)
        cq[:64] = np.concatenate([cs[qsel], cs[qsel]], 1).T; sq_[:64] = np.concatenate([-sn[qsel], sn[qsel]], 1).T
        ic = np.zeros((3, 4, 512), np.float32)
        for t in range(2):
            p_ = pos[t * 512:(t + 1) * 512]
            for g, w in enumerate(wins):
                ic[t, g] = 1.0 / np.minimum(p_ + 1, w)
        for g, w in enumerate(wins):
            ic[2, g] = 1.0 / w
        cst = np.zeros((128, 4), np.float32); cst[:, 0] = 1.0 if r_ == 1 else 0.0; cst[:, 1] = EPS; cst[:, 2] = np.arange(128) % 32
        m = dict(shared)
        m.update({'x_loc': xl, 'state_pool': np.ascontiguousarray(sp[c * NS:(c + 1) * NS]),
                  'ptab': np.ascontiguousarray(pt[c * NS:(c + 1) * NS].reshape(NS * NPG // 4, 4).T[np.arange(128) // 32]),
                  'cosk': np.ascontiguousarray(ck.reshape(17, 128, R).transpose(1, 0, 2)),
                  'sink': np.ascontiguousarray(sk.reshape(17, 128, R).transpose(1, 0, 2)),
                  'cosq': cq, 'sinq': sq_, 'invc': np.ascontiguousarray(np.broadcast_to(ic[None], (128, 3, 4, 512))), 'consts': cst})
        in_maps.append(m)
    res = run_bass_kernel_spmd(nc, in_maps, core_ids=list(range(8))).results
    B = 4
    y_p = np.zeros((B, SEQ, D), np.float32); y_s = np.zeros((B * 32, 1, D), np.float32)
    p_kv = np.zeros((1, B, SEQ, KV), np.float32); p_pe = np.zeros((1, B, SEQ, R), np.float32)
    p_pool = np.zeros((1, B, 15, DP), np.float32)
    s_kv = np.zeros((1, 128, 1, KV), np.float32); s_pe = np.zeros((1, 128, 1, R), np.float32); s_pool = np.zeros((1, 128, 15, DP), np.float32)
    for c in range(8):
        s_, r_ = c // 2, c % 2; o = res[c]
        for j in range(8):
            gbk = 2 * j + r_
            y_p[s_, gbk * 128:(gbk + 1) * 128] = o['y_out'][j * 128:(j + 1) * 128]
            p_kv[0, s_, gbk * 128:(gbk + 1) * 128] = o['kv_out'][j * 128:(j + 1) * 128]
            p_pe[0, s_, gbk * 128:(gbk + 1) * 128] = o['kpe_out'][j * 128:(j + 1) * 128]
        if r_ == 1:
            p_pool[0, s_] = o['pool_last'][1:16]
        sl = slice(c * NS, (c + 1) * NS)
        y_s[sl, 0] = o['y_out'][1024:1024 + NS]
        s_kv[0, sl, 0] = o['kv_out'][2048:2048 + NS]; s_pe[0, sl, 0] = o['kpe_out'][2048:2048 + NS]
        s_pool[0, sl, :14] = o['spool_hist']; s_pool[0, sl, 14] = o['spool_new']
    return (y_p, y_s, p_kv, p_pe, p_pool, s_kv, s_pe, s_pool)
```
